# Optimizing a Trainium2 kernel written in Bass

```python
import math
import jax
import jax.numpy as jnp
from jax import lax
import numpy as np

D_MODEL = 1024
BATCH = 8
SEQ = 4096
DEPTH = 2

GRID_W = 64
CTX_LEN = 256
QBLOCK = 128
ROPE_THETA = 10000.0
NORM_EPS = 1e-6
NEG_INF = -1e30

DIFF_HEADS = 4
DIFF_QK_DIM = 32
DIFF_V_DIM = 64
SWA_Q_HEADS = 4
SWA_KV_HEADS = 2
SWA_HEAD_DIM = 64
WINDOW = 128
MLA_HEADS = 4
MLA_Q_RANK = 192
MLA_KV_RANK = 128
MLA_NOPE_DIM = 64
MLA_ROPE_DIM = 32
MLA_V_DIM = 64
GQA_Q_HEADS = 4
GQA_KV_HEADS = 2
GQA_HEAD_DIM = 64
FFN_DIM = 2816
CONV_WIDTH = 3
N_MOD = 6

DIFF_COLS = 4 * DIFF_HEADS * DIFF_QK_DIM + DIFF_HEADS * DIFF_V_DIM
SWA_COLS = (SWA_Q_HEADS + 2 * SWA_KV_HEADS) * SWA_HEAD_DIM
MLA_COLS = MLA_Q_RANK + MLA_KV_RANK + MLA_ROPE_DIM
GQA_COLS = (GQA_Q_HEADS + 2 * GQA_KV_HEADS) * GQA_HEAD_DIM
IN_COLS = DIFF_COLS + SWA_COLS + MLA_COLS + GQA_COLS
MIX_WIDTH = (DIFF_HEADS * DIFF_V_DIM + SWA_Q_HEADS * SWA_HEAD_DIM
             + MLA_HEADS * MLA_V_DIM + GQA_Q_HEADS * GQA_HEAD_DIM)

kernel_name = "hybrid_head_group_diffusion_trunk"


def rms_norm(x, g):
    x32 = x.astype(jnp.float32)
    y = x32 * lax.rsqrt(jnp.mean(jnp.square(x32), axis=-1, keepdims=True) + NORM_EPS)
    return y.astype(x.dtype) * g


def modulate(h, shift, scale):
    return h * (1.0 + scale[..., None, :]) + shift[..., None, :]


def axial_rope_tables(row, col, rot_dim):
    n = rot_dim // 4
    inv_freq = ROPE_THETA ** (-jnp.arange(n, dtype=jnp.float32) / n)
    ang = jnp.stack([row.astype(jnp.float32)[:, None] * inv_freq,
                     col.astype(jnp.float32)[:, None] * inv_freq], axis=1)
    return jnp.cos(ang), jnp.sin(ang)


def apply_axial_rope(x, cos, sin):
    B, S, H, d = x.shape
    n = d // 4
    xr = x.reshape(B, S, H, 2, 2, n)
    x1, x2 = xr[..., 0, :], xr[..., 1, :]
    c = cos[None, :, None].astype(x.dtype)
    s = sin[None, :, None].astype(x.dtype)
    out = jnp.stack([x1 * c - x2 * s, x2 * c + x1 * s], axis=-2)
    return out.reshape(B, S, H, d)


def attend(q, k, v, scale, sink=None):
    s = jnp.einsum('bqgrd,bkgd->bgrqk', q, k).astype(jnp.float32) * scale
    if sink is None:
        p = jax.nn.softmax(s, axis=-1)
    else:
        G, R = q.shape[2], q.shape[3]
        sk = jnp.broadcast_to(sink.astype(jnp.float32).reshape(1, G, R, 1, 1), s.shape[:-1] + (1,))
        p = jax.nn.softmax(jnp.concatenate([s, sk], axis=-1), axis=-1)[..., :-1]
    return jnp.einsum('bgrqk,bkgd->bqgrd', p.astype(v.dtype), v)


def blocked_attend(q, k, v, scale):
    B, S, G, R, d = q.shape
    nb = S // QBLOCK
    qb = jnp.moveaxis(q.reshape(B, nb, QBLOCK, G, R, d), 1, 0)
    out = lax.map(lambda qi: attend(qi, k, v, scale), qb)
    return jnp.moveaxis(out, 0, 1).reshape(B, S, G, R, v.shape[-1])


def banded_window_attend(q, k, v, k_ctx, v_ctx, sink, scale):
    B, S, G, R, d = q.shape
    W = WINDOW
    nb = S // W
    pad = ((0, 0), (W, W), (0, 0), (0, 0))
    kp = jnp.pad(k, pad)
    vp = jnp.pad(v, pad)

    def band(t):
        return jnp.concatenate(
            [t[:, i * W:i * W + S].reshape(B, nb, W, G, t.shape[-1]) for i in range(3)], axis=2)

    kb, vb = band(kp), band(vp)
    qb = q.reshape(B, nb, W, G, R, d)
    s_loc = jnp.einsum('bnqgrd,bnkgd->bngrqk', qb, kb).astype(jnp.float32) * scale
    qi = jnp.arange(W)[:, None]
    kj = jnp.arange(3 * W)[None, :]
    rel = kj - W - qi
    j = jnp.arange(nb)[:, None, None] * W + kj[None] - W
    valid = (jnp.abs(rel)[None] <= W) & (j >= 0) & (j < S)
    s_loc = jnp.where(valid[None, :, None, None], s_loc, NEG_INF)
    s_ctx = jnp.einsum('bnqgrd,bkgd->bngrqk', qb, k_ctx).astype(jnp.float32) * scale
    s_sink = jnp.broadcast_to(sink.astype(jnp.float32).reshape(1, 1, G, R, 1, 1), s_loc.shape[:-1] + (1,))
    p = jax.nn.softmax(jnp.concatenate([s_loc, s_ctx, s_sink], axis=-1), axis=-1).astype(v.dtype)
    n_loc = 3 * W
    n_ctx = k_ctx.shape[1]
    out = (jnp.einsum('bngrqk,bnkgd->bnqgrd', p[..., :n_loc], vb)
           + jnp.einsum('bngrqk,bkgd->bnqgrd', p[..., n_loc:n_loc + n_ctx], v_ctx))
    return out.reshape(B, S, G, R, v.shape[-1])


def diff_attention_mixer(p, pc, lam_q1, lam_k1, lam_q2, lam_k2, g_sub, lambda_init, rope, need_ctx):
    H, dk, dv = DIFF_HEADS, DIFF_QK_DIM, DIFF_V_DIM
    scale = dk ** -0.5

    def qkv(t):
        B, L, _ = t.shape
        q = t[..., :2 * H * dk].reshape(B, L, 2 * H, dk)
        k = t[..., 2 * H * dk:4 * H * dk].reshape(B, L, 2 * H, dk)
        v = t[..., 4 * H * dk:].reshape(B, L, H, dv)
        return q, k, v

    q, k, v = qkv(p)
    qc, kc, vc = qkv(pc)
    q = apply_axial_rope(q, *rope)
    k = apply_axial_rope(k, *rope)
    B, S = q.shape[:2]
    Lc = qc.shape[1]
    lam = (jnp.exp(jnp.sum(lam_q1.astype(jnp.float32) * lam_k1.astype(jnp.float32)))
           - jnp.exp(jnp.sum(lam_q2.astype(jnp.float32) * lam_k2.astype(jnp.float32)))
           + lambda_init)

    def combine(qh, kh, vh, attn):
        a1 = attn(qh[:, :, :, 0:1], kh[:, :, :, 0], vh, scale)[:, :, :, 0]
        a2 = attn(qh[:, :, :, 1:2], kh[:, :, :, 1], vh, scale)[:, :, :, 0]
        o = a1 - lam.astype(a1.dtype) * a2
        o = rms_norm(o, g_sub) * (1.0 - lambda_init)
        return o.reshape(o.shape[0], o.shape[1], H * dv)

    k_all = jnp.concatenate([kc, k], axis=1).reshape(B, Lc + S, H, 2, dk)
    v_all = jnp.concatenate([vc, v], axis=1)
    out = combine(q.reshape(B, S, H, 2, dk), k_all, v_all, blocked_attend)
    out_c = None
    if need_ctx:
        out_c = combine(qc.reshape(B, Lc, H, 2, dk), kc.reshape(B, Lc, H, 2, dk), vc, attend)
    return out, out_c


def window_attention_mixer(p, pc, sink, rope, need_ctx):
    H, G, d = SWA_Q_HEADS, SWA_KV_HEADS, SWA_HEAD_DIM
    R = H // G
    scale = d ** -0.5

    def qkv(t):
        B, L, _ = t.shape
        q = t[..., :H * d].reshape(B, L, H, d)
        k = t[..., H * d:(H + G) * d].reshape(B, L, G, d)
        v = t[..., (H + G) * d:].reshape(B, L, G, d)
        return q, k, v

    q, k, v = qkv(p)
    qc, kc, vc = qkv(pc)
    q = apply_axial_rope(q, *rope)
    k = apply_axial_rope(k, *rope)
    B, S = q.shape[:2]
    out = banded_window_attend(q.reshape(B, S, G, R, d), k, v, kc, vc, sink, scale).reshape(B, S, H * d)
    out_c = None
    if need_ctx:
        Lc = qc.shape[1]
        out_c = attend(qc.reshape(B, Lc, G, R, d), kc, vc, scale, sink).reshape(B, Lc, H * d)
    return out, out_c


def latent_attention_mixer(p, pc, g_q, g_kv, w_uq, w_ukv, rope, need_ctx):
    H, dn, dr, dv = MLA_HEADS, MLA_NOPE_DIM, MLA_ROPE_DIM, MLA_V_DIM
    scale = (dn + dr) ** -0.5

    def qkv(t, rotate):
        B, L, _ = t.shape
        c_q = t[..., :MLA_Q_RANK]
        c_kv = t[..., MLA_Q_RANK:MLA_Q_RANK + MLA_KV_RANK]
        k_r = t[..., MLA_Q_RANK + MLA_KV_RANK:].reshape(B, L, 1, dr)
        q = (rms_norm(c_q, g_q) @ w_uq).reshape(B, L, H, dn + dr)
        kv = (rms_norm(c_kv, g_kv) @ w_ukv).reshape(B, L, H, dn + dv)
        q_n, q_r = q[..., :dn], q[..., dn:]
        if rotate:
            q_r = apply_axial_rope(q_r, *rope)
            k_r = apply_axial_rope(k_r, *rope)
        q = jnp.concatenate([q_n, q_r], axis=-1)
        k = jnp.concatenate([kv[..., :dn], jnp.broadcast_to(k_r, (B, L, H, dr))], axis=-1)
        v = kv[..., dn:]
        return q, k, v

    q, k, v = qkv(p, True)
    qc, kc, vc = qkv(pc, False)
    B, S = q.shape[:2]
    k_all = jnp.concatenate([kc, k], axis=1)
    v_all = jnp.concatenate([vc, v], axis=1)
    out = blocked_attend(q[:, :, :, None], k_all, v_all, scale).reshape(B, S, H * dv)
    out_c = None
    if need_ctx:
        Lc = qc.shape[1]
        out_c = attend(qc[:, :, :, None], kc, vc, scale).reshape(B, Lc, H * dv)
    return out, out_c


def axial_gqa_mixer(p, pc, g_q, g_k, rope, need_ctx):
    H, G, d = GQA_Q_HEADS, GQA_KV_HEADS, GQA_HEAD_DIM
    R = H // G
    scale = d ** -0.5

    def qkv(t):
        B, L, _ = t.shape
        q = rms_norm(t[..., :H * d].reshape(B, L, H, d), g_q)
        k = rms_norm(t[..., H * d:(H + G) * d].reshape(B, L, G, d), g_k)
        v = t[..., (H + G) * d:].reshape(B, L, G, d)
        return q, k, v

    q, k, v = qkv(p)
    qc, kc, vc = qkv(pc)
    q = apply_axial_rope(q, *rope)
    k = apply_axial_rope(k, *rope)
    B, S = q.shape[:2]
    k_all = jnp.concatenate([kc, k], axis=1)
    v_all = jnp.concatenate([vc, v], axis=1)
    out = blocked_attend(q.reshape(B, S, G, R, d), k_all, v_all, scale).reshape(B, S, H * d)
    out_c = None
    if need_ctx:
        Lc = qc.shape[1]
        out_c = attend(qc.reshape(B, Lc, G, R, d), kc, vc, scale).reshape(B, Lc, H * d)
    return out, out_c


def depthwise_conv(u, w, b):
    C = u.shape[-1]
    y = lax.conv_general_dilated(u, w[:, None, :].astype(u.dtype), window_strides=(1,),
                                 padding=((CONV_WIDTH // 2, CONV_WIDTH // 2),),
                                 dimension_numbers=('NWC', 'WIO', 'NWC'),
                                 feature_group_count=C)
    return y + b


def conv_ffn(h, w_up, conv_w, conv_b, w_down):
    u = depthwise_conv(h @ w_up, conv_w, conv_b)
    a, v = u[..., :FFN_DIM], u[..., FFN_DIM:]
    return (jax.nn.silu(a) * v) @ w_down


def setup_inputs(seed: int = 0) -> dict:
    key = jax.random.key(seed)
    keys = iter(jax.random.split(key, 32))
    L, D = DEPTH, D_MODEL

    def normal(shape, scale):
        return jax.random.normal(next(keys), shape, jnp.float32) * scale

    def gain(shape):
        return 1.0 + normal(shape, 0.05)

    return {
        "x": normal((BATCH, SEQ, D), 1.0),
        "c": normal((BATCH, D), 1.0),
        "ctx": normal((BATCH, CTX_LEN, D), 1.0),
        "c_ctx": normal((D,), 1.0),
        "w_mod": normal((L, D, N_MOD * D), 0.5 * D ** -0.5),
        "b_mod": normal((L, N_MOD * D), 0.01),
        "g_mix_pre": gain((L, D)),
        "g_mix_post": gain((L, D)),
        "g_ffn_pre": gain((L, D)),
        "g_ffn_post": gain((L, D)),
        "w_in": normal((L, D, IN_COLS), D ** -0.5),
        "diff_lam_q1": normal((L, DIFF_QK_DIM), 0.1),
        "diff_lam_k1": normal((L, DIFF_QK_DIM), 0.1),
        "diff_lam_q2": normal((L, DIFF_QK_DIM), 0.1),
        "diff_lam_k2": normal((L, DIFF_QK_DIM), 0.1),
        "diff_g_sub": gain((L, DIFF_V_DIM)),
        "swa_sink": normal((L, SWA_Q_HEADS), 1.0),
        "mla_g_q": gain((L, MLA_Q_RANK)),
        "mla_g_kv": gain((L, MLA_KV_RANK)),
        "mla_w_uq": normal((L, MLA_Q_RANK, MLA_HEADS * (MLA_NOPE_DIM + MLA_ROPE_DIM)), MLA_Q_RANK ** -0.5),
        "mla_w_ukv": normal((L, MLA_KV_RANK, MLA_HEADS * (MLA_NOPE_DIM + MLA_V_DIM)), MLA_KV_RANK ** -0.5),
        "gqa_g_q": gain((L, GQA_HEAD_DIM)),
        "gqa_g_k": gain((L, GQA_HEAD_DIM)),
        "w_out": normal((L, MIX_WIDTH, D), MIX_WIDTH ** -0.5),
        "ffn_w_up": normal((L, D, 2 * FFN_DIM), D ** -0.5),
        "ffn_conv_w": normal((L, CONV_WIDTH, 2 * FFN_DIM), CONV_WIDTH ** -0.5),
        "ffn_conv_b": normal((L, 2 * FFN_DIM), 0.01),
        "ffn_w_down": normal((L, FFN_DIM, D), FFN_DIM ** -0.5),
    }


def reference(x, c, ctx, c_ctx, w_mod, b_mod, g_mix_pre, g_mix_post, g_ffn_pre, g_ffn_post,
              w_in, diff_lam_q1, diff_lam_k1, diff_lam_q2, diff_lam_k2, diff_g_sub, swa_sink,
              mla_g_q, mla_g_kv, mla_w_uq, mla_w_ukv, gqa_g_q, gqa_g_k, w_out,
              ffn_w_up, ffn_conv_w, ffn_conv_b, ffn_w_down):
    B, S, D = x.shape
    rows = S // GRID_W
    row = jnp.repeat(jnp.arange(rows, dtype=jnp.int32), GRID_W)
    col = jnp.tile(jnp.arange(GRID_W, dtype=jnp.int32), rows)
    rope_diff = axial_rope_tables(row, col, DIFF_QK_DIM)
    rope_swa = axial_rope_tables(row, col, SWA_HEAD_DIM)
    rope_mla = axial_rope_tables(row, col, MLA_ROPE_DIM)
    rope_gqa = axial_rope_tables(row, col, GQA_HEAD_DIM)
    silu_c = jax.nn.silu(c)
    silu_cc = jax.nn.silu(c_ctx)
    o1 = DIFF_COLS
    o2 = o1 + SWA_COLS
    o3 = o2 + MLA_COLS
    xc = ctx
    for l in range(DEPTH):
        need_ctx = l < DEPTH - 1
        lambda_init = 0.8 - 0.6 * math.exp(-0.3 * l)
        mod = (silu_c @ w_mod[l] + b_mod[l]).reshape(B, N_MOD, D)
        modc = (silu_cc @ w_mod[l] + b_mod[l]).reshape(N_MOD, D)

        h = modulate(rms_norm(x, g_mix_pre[l]), mod[:, 0], mod[:, 1])
        hc = modulate(rms_norm(xc, g_mix_pre[l]), modc[0], modc[1])
        p = h @ w_in[l]
        pc = hc @ w_in[l]
        ya, yca = diff_attention_mixer(p[..., :o1], pc[..., :o1], diff_lam_q1[l], diff_lam_k1[l],
                                       diff_lam_q2[l], diff_lam_k2[l], diff_g_sub[l], lambda_init,
                                       rope_diff, need_ctx)
        yb, ycb = window_attention_mixer(p[..., o1:o2], pc[..., o1:o2], swa_sink[l], rope_swa, need_ctx)
        ym, ycm = latent_attention_mixer(p[..., o2:o3], pc[..., o2:o3], mla_g_q[l], mla_g_kv[l],
                                         mla_w_uq[l], mla_w_ukv[l], rope_mla, need_ctx)
        yd, ycd = axial_gqa_mixer(p[..., o3:], pc[..., o3:], gqa_g_q[l], gqa_g_k[l], rope_gqa, need_ctx)
        y = jnp.concatenate([ya, yb, ym, yd], axis=-1) @ w_out[l]
        x = x + mod[:, 2, None] * rms_norm(y, g_mix_post[l])
        if need_ctx:
            yc = jnp.concatenate([yca, ycb, ycm, ycd], axis=-1) @ w_out[l]
            xc = xc + modc[2] * rms_norm(yc, g_mix_post[l])

        f = conv_ffn(modulate(rms_norm(x, g_ffn_pre[l]), mod[:, 3], mod[:, 4]),
                     ffn_w_up[l], ffn_conv_w[l], ffn_conv_b[l], ffn_w_down[l])
        x = x + mod[:, 5, None] * rms_norm(f, g_ffn_post[l])
        if need_ctx:
            fc = conv_ffn(modulate(rms_norm(xc, g_ffn_pre[l]), modc[3], modc[4]),
                          ffn_w_up[l], ffn_conv_w[l], ffn_conv_b[l], ffn_w_down[l])
            xc = xc + modc[5] * rms_norm(fc, g_ffn_post[l])
    return x
```

```python
import math
from contextlib import ExitStack

import numpy as np
import ml_dtypes

import concourse.bass as bass
import concourse.mybir as mybir
from concourse.bass_utils import run_bass_kernel_spmd

F32 = mybir.dt.float32
BF16 = mybir.dt.bfloat16
ALU = mybir.AluOpType
AF = mybir.ActivationFunctionType

PE, ACT, DVE, POOL, SP = "pe", "act", "dve", "pool", "sp"

D = 1024
S = 4096
LC = 256
TT = S + LC
NKT = TT // 128
DEPTH = 2
EPS = 1e-6
FFN = 2816
NCT = FFN // 128
NEXT = 3072 + 512


class T:
    def __init__(self, ap, name="", psum=False):
        self.ap = ap
        self.name = name
        self.psum = psum
        self.w = None
        self.r = []

    def __getitem__(self, idx):
        return self.ap[idx]


class Prog:
    def __init__(self, nc, n_dma_sems=8):
        self.nc = nc
        self.engs = (PE, ACT, DVE, POOL, SP)
        self.ops = {e: [] for e in self.engs}
        self.cnt = {("e", e): 0 for e in (PE, ACT, DVE, POOL)}
        self.seen = {e: {} for e in self.engs}
        self.n_dma_sems = n_dma_sems
        self.dma_rr = {e: 0 for e in self.engs}
        self.dma_last = {}
        self.nops = 0
        self.maxops = None

    def all_keys(self):
        keys = [("e", e) for e in (PE, ACT, DVE, POOL)]
        for q in (SP, POOL, ACT):
            for i in range(self.n_dma_sems):
                keys.append(("d", q, i))
        return keys

    def _waits_for(self, eng, reads, writes, extra=()):
        need = {}

        def add(tok):
            if tok is None:
                return
            k, v = tok
            if need.get(k, 0) < v:
                need[k] = v

        for t in reads:
            add(t.w)
            if t.psum:
                for x in t.r:
                    if x is not None and x[0] != ("e", eng):
                        add(x)
        for t in writes:
            add(t.w)
            for x in t.r:
                add(x)
        for x in extra:
            add(x)
        out = []
        for k, v in need.items():
            if eng == PE and k == ("e", PE):
                continue
            if self.seen[eng].get(k, 0) < v:
                self.seen[eng][k] = v
                out.append((k, v))
        return out

    def _reg(self, tok, reads, writes):
        for t in reads:
            t.r.append(tok)
            if len(t.r) > 64:
                best = {}
                for (k, v) in t.r:
                    if best.get(k, 0) < v:
                        best[k] = v
                t.r = list(best.items())
        for t in writes:
            t.w = tok
            t.r = []

    def op(self, eng, fn, reads=(), writes=(), extra=()):
        return self.group(eng, [fn], reads, writes, extra)

    def group(self, eng, fns, reads=(), writes=(), extra=()):
        if self.maxops is not None and self.nops >= self.maxops:
            return None
        waits = self._waits_for(eng, reads, writes, extra)
        k = ("e", eng)
        self.cnt[k] += 1
        tok = (k, self.cnt[k])
        n = len(fns)
        for i, fn in enumerate(fns):
            self.ops[eng].append((fn, waits if i == 0 else [], (k, 1) if i == n - 1 else None))
        self.nops += n
        self._reg(tok, reads, writes)
        return tok

    def dma(self, eng, out_ap, in_ap, reads=(), writes=(), extra=(), **kw):
        if self.maxops is not None and self.nops >= self.maxops:
            return None
        i = self.dma_rr[eng]
        self.dma_rr[eng] = (i + 1) % self.n_dma_sems
        k = ("d", eng, i)
        waits = self._waits_for(eng, reads, writes, extra)
        prev = self.dma_last.get(k, 0)
        if prev and self.seen[eng].get(k, 0) < prev:
            self.seen[eng][k] = prev
            waits.append((k, prev))
        val = prev + 16
        self.dma_last[k] = val
        tok = (k, val)

        def fn(e, out_ap=out_ap, in_ap=in_ap, kw=kw):
            return e.dma_start(out=out_ap, in_=in_ap, **kw)

        self.ops[eng].append((fn, waits, (k, 16)))
        self.nops += 1
        self._reg(tok, reads, writes)
        return tok

    def barrier(self):
        toks = [(k, v) for k, v in self.cnt.items() if v > 0]
        toks += [(k, v) for k, v in self.dma_last.items() if v > 0]
        for eng in self.engs:
            waits = []
            for (k, v) in toks:
                if self.seen[eng].get(k, 0) < v:
                    self.seen[eng][k] = v
                    waits.append((k, v))
            if waits:
                self.ops[eng].append((None, waits, None))

    def emit(self, block, sems):
        def run(engname):
            def body(e):
                for (fn, waits, inc) in self.ops[engname]:
                    for (k, v) in waits:
                        e.wait_ge(sems[k], v)
                    if fn is None:
                        continue
                    ins = fn(e)
                    if inc is not None:
                        ins.then_inc(sems[inc[0]], inc[1])
            return body

        block.tensor(run(PE))
        block.scalar(run(ACT))
        block.vector(run(DVE))
        block.gpsimd(run(POOL))
        block.sync(run(SP))
        self.ops = {e: [] for e in self.engs}


def mm(out, lhsT, rhs, start=True, stop=True, **kw):
    return lambda e: e.matmul(out, lhsT, rhs, start=start, stop=stop, **kw)


def act(out, in_, func, bias=None, scale=1.0):
    if bias is None:
        return lambda e: e.activation(out, in_, func, scale=scale)
    return lambda e: e.activation(out, in_, func, bias=bias, scale=scale)


def tt(out, a, b, op):
    return lambda e: e.tensor_tensor(out=out, in0=a, in1=b, op=op)


def ts2(out, a, s1, s2, op0, op1):
    return lambda e: e.tensor_scalar(out, a, s1, s2, op0, op1)


def ts1(out, a, s1, op0):
    return lambda e: e.tensor_scalar(out, a, s1, None, op0)


def stt(out, a, s, b, op0, op1):
    return lambda e: e.scalar_tensor_tensor(out=out, in0=a, scalar=s, in1=b, op0=op0, op1=op1)


def cp(out, a):
    return lambda e: e.tensor_copy(out, a)


def acp(out, a):
    return lambda e: e.copy(out, a)


def memset(out, v):
    return lambda e: e.memset(out, v)


VEC_L = [("gpre", 8), ("gpost", 8), ("gfpre", 8), ("gfpost", 8), ("bmod", 48),
         ("lq1", 32), ("lk1", 32), ("lq2", 32), ("lk2", 32), ("gsub", 1), ("sink", 4),
         ("mgq", 2), ("mgkv", 1), ("ggq", 1), ("ggqp", 1), ("ggk", 1), ("ggkp", 1),
         ("cw", 132), ("cb", 44)]
VEC_G = [("cin", 16)]


def vec_offsets():
    off = {}
    o = 0
    for (n, c) in VEC_G:
        off[n] = o
        o += c
    for l in range(DEPTH):
        for (n, c) in VEC_L:
            off[(n, l)] = o
            o += c
    return off, o


VOFF, NV = vec_offsets()


def fm(v):
    return np.ascontiguousarray(v.reshape(-1, 128).T)


def rope_perm_cols(cols, d):
    n = d // 4
    cols = list(cols)
    out = []
    for i, c in enumerate(cols):
        base = (i // d) * d
        out.append(cols[base + ((i - base) ^ n)])
    return out


def w_in_ext_cols():
    tiles = []
    r = lambda a, b: list(range(a, b))
    dq0, dq1 = r(0, 128), r(128, 256)
    dk0, dk1 = r(256, 384), r(384, 512)
    o1 = 768
    sqh = [r(o1 + h * 64, o1 + (h + 1) * 64) for h in range(4)]
    sq0, sq1 = sqh[0] + sqh[2], sqh[1] + sqh[3]
    sk = r(o1 + 256, o1 + 384)
    o2 = 1280
    mcq0, mcq1 = r(o2, o2 + 128), r(o2 + 128, o2 + 192)
    mckv = r(o2 + 192, o2 + 320)
    kr = r(o2 + 320, o2 + 352)
    mkr = r(o2 + 192, o2 + 256) + kr
    mkrp = r(o2 + 192, o2 + 256) + rope_perm_cols(kr, 32)
    o3 = 1632
    gqh = [r(o3 + h * 64, o3 + (h + 1) * 64) for h in range(4)]
    gq0, gq1 = gqh[0] + gqh[2], gqh[1] + gqh[3]
    gk = r(o3 + 256, o3 + 384)
    P32 = lambda c: rope_perm_cols(c, 32)
    P64 = lambda c: rope_perm_cols(c, 64)
    tiles = [dq0, dq1, P32(dq0), P32(dq1), dk0, dk1, P32(dk0), P32(dk1),
             sq0, sq1, P64(sq0), P64(sq1), sk, P64(sk),
             mcq0, mcq1, mckv, mkr, mkrp,
             gq0, gq1, P64(gq0), P64(gq1), gk, P64(gk)]
    wv = r(512, 768) + r(o1 + 384, o1 + 512) + r(o3 + 384, o3 + 512)
    offs = []
    cols = []
    for t in tiles:
        offs.append((len(cols), len(t)))
        cols += t
    assert len(cols) == 3072
    voff = len(cols)
    cols += wv
    assert len(cols) == NEXT
    return cols, offs, voff


EXT_COLS, EXT_OFFS, EXT_VOFF = w_in_ext_cols()
(T_DQ0, T_DQ1, T_DQ0P, T_DQ1P, T_DK0, T_DK1, T_DK0P, T_DK1P, T_SQ0, T_SQ1, T_SQ0P, T_SQ1P, T_SK, T_SKP,
 T_MCQ0, T_MCQ1, T_MCKV, T_MKR, T_MKRP, T_GQ0, T_GQ1, T_GQ0P, T_GQ1P, T_GK, T_GKP) = range(25)

(Q_DQ0, Q_DQ1, Q_DK0, Q_DK1, Q_SQ0, Q_SQ1, Q_SK, Q_GQ0, Q_GQ1, Q_GK) = range(10)
Q_MQ = 10
Q_MK = 14
NQK = 18


def rope_tables():
    out = np.zeros((4, 128, TT), np.float32)
    s = np.arange(S)
    row = (s // 64).astype(np.float32)
    col = (s % 64).astype(np.float32)
    for fi, d in ((0, 32), (2, 64)):
        n = d // 4
        inv = (np.float32(10000.0) ** (-np.arange(n, dtype=np.float32) / np.float32(n))).astype(np.float32)
        for p in range(128):
            idx = p % d
            axis = idx // (2 * n)
            half = (idx // n) % 2
            i = idx % n
            pos = row if axis == 0 else col
            ang = (pos * inv[i]).astype(np.float32)
            out[fi, p, :LC] = 1.0
            out[fi, p, LC:] = np.cos(ang)
            out[fi + 1, p, :LC] = 0.0
            out[fi + 1, p, LC:] = np.sin(ang) * (-1.0 if half == 0 else 1.0)
    return out


def swa_masks():
    m = np.zeros((6, 128, 512), np.float32)
    kl = np.arange(128)[:, None]
    ql = np.arange(512)[None, :]
    for ri, r in enumerate(range(-1, 5)):
        m[ri] = (np.abs(ql - kl - 128 * r) <= 128).astype(np.float32)
    return m.astype(ml_dtypes.bfloat16)


def build(debug=False, stop_after=None, maxops=None):
    nc = bass.Bass("TRN2", target_bir_lowering=False)
    dk = "ExternalOutput" if debug else "Internal"

    def din(name, shape, dt=F32):
        return nc.dram_tensor(name, shape, dt, kind="ExternalInput").ap()

    dbg_list = []

    def dscr(name, shape, dt):
        ap = nc.dram_tensor(name, shape, dt, kind="Internal").ap()
        if debug and name in debug:
            o = nc.dram_tensor("D_" + name, shape, F32, kind="ExternalOutput").ap()
            dbg_list.append((name, ap, o, shape))
        return ap

    xin = din("xin", [D, TT])
    vecs_d = din("vecs", [128, NV])
    rope_d = din("rope", [4, 128, TT])
    mask_d = din("masks", [6, 128, 512], BF16)
    wmod_d = din("w_mod", [DEPTH, D, 6 * D])
    winx_d = din("w_inx", [DEPTH, D, NEXT])
    wuqx_d = din("w_uqx", [DEPTH, 192, 768])
    wukvx_d = din("w_ukvx", [DEPTH, 128, 512])
    wout_d = din("w_out", [DEPTH, D, D])
    wup_d = din("w_up", [DEPTH, D, 2 * FFN])
    wdn_d = din("w_dn", [DEPTH, FFN, D])
    yT = nc.dram_tensor("yT", [D, S], F32, kind="ExternalOutput").ap()

    XA = T(dscr("XA", [D, TT], F32), "XA")
    XB = T(dscr("XB", [D, TT], F32), "XB")
    XIN = T(xin, "xin")
    YT = T(yT, "yT")
    QK = [T(dscr("QK%d" % i, [128, TT], BF16), "QK%d" % i) for i in range(NQK)]
    VD = T(dscr("VD", [TT, 260], BF16), "VD")
    VM = T(dscr("VM", [TT, 260], BF16), "VM")
    VS = T(dscr("VS", [TT, 130], BF16), "VS")
    VG = T(dscr("VG", [TT, 130], BF16), "VG")
    MIX = T(dscr("MIX", [D, TT], BF16), "MIX")
    WUPB = [T(dscr("WUPB%d" % l, [NCT, 128, 8 * 256], BF16), "WUPB") for l in range(DEPTH)]
    WDNB = [T(dscr("WDNB%d" % l, [8, 128, NCT * 128], BF16), "WDNB") for l in range(DEPTH)]

    P = Prog(nc)
    P.maxops = maxops
    out_toks = []
    uid = [0]

    def sbuf_t(name, shape, dt):
        uid[0] += 1
        return nc.sbuf_tensor("%s_u%d" % (name, uid[0]), shape, dt)

    def psum_t(name, shape, dt):
        uid[0] += 1
        return nc.psum_tensor("%s_u%d" % (name, uid[0]), shape, dt)

    with ExitStack() as es0:
        sems = {k: es0.enter_context(nc.semaphore("s%d" % i)) for i, k in enumerate(P.all_keys())}

        def flush():
            with nc.Block() as block:
                P.emit(block, sems)

        def sb0(name, shape, dt):
            return T(es0.enter_context(sbuf_t(name, shape, dt))[:], name)

        vecs = sb0("vecs", [128, NV], F32)
        mv = sb0("mv", [128, DEPTH * 2 * 4 * 8], F32)
        modsb = sb0("modsb", [128, DEPTH * 96], F32)
        ones_b = sb0("ones_b", [128, 128], BF16)
        bd64 = sb0("bd64", [128, 128], BF16)
        sel = sb0("sel", [128, 64], F32)
        eps_t = sb0("eps_t", [128, 1], F32)
        small = sb0("small", [128, 16 * DEPTH], F32)

        def MV(l, who, kind):
            o = ((l * 2 + who) * 4 + kind) * 8
            return mv[:, o:o + 8]

        def MOD(l, j, who):
            base = l * 96
            return modsb[:, base:base + 96].rearrange("p (i w) -> p i w", w=2)[:, j * 8:(j + 1) * 8, who]

        def V(name, l=None, n=None):
            o = VOFF[name] if l is None else VOFF[(name, l)]
            w = n if n is not None else dict(VEC_L + VEC_G)[name]
            return vecs[:, o:o + w]

        with ExitStack() as es:
            def sb(name, shape, dt):
                return T(es.enter_context(sbuf_t(name, shape, dt))[:], name)

            def ps(name, shape, dt=F32):
                return T(es.enter_context(psum_t(name, shape, dt))[:], name, psum=True)

            P.dma(SP, vecs[:], vecs_d, writes=[vecs])
            P.op(DVE, memset(ones_b[:], 1.0), writes=[ones_b])
            P.op(DVE, memset(bd64[:], 0.0), writes=[bd64])
            P.op(DVE, memset(bd64[0:64, 0:64], 1.0), writes=[bd64])
            P.op(DVE, memset(bd64[64:128, 64:128], 1.0), writes=[bd64])
            P.op(DVE, memset(sel[:], 0.0), writes=[sel])
            P.op(DVE, memset(sel[64:65, :], 1.0), writes=[sel])
            P.op(DVE, memset(eps_t[:], EPS), writes=[eps_t])

            sil = sb("sil", [128, 16], F32)
            P.op(ACT, act(sil[:], V("cin"), AF.Silu), reads=[vecs], writes=[sil])
            wmt = [sb("wmt%d" % i, [128, 8, 512], F32) for i in range(2)]
            pm = ps("pm", [128, 96])
            for l in range(DEPTH):
                for cb in range(12):
                    w = wmt[(l * 12 + cb) % 2]
                    P.dma(SP if cb % 2 == 0 else ACT, w[:],
                          wmod_d[l, :, cb * 512:(cb + 1) * 512].rearrange("(k p) c -> p k c", p=128), writes=[w])
                    fns = []
                    for sub in range(4):
                        idx = cb * 4 + sub
                        for kt in range(8):
                            fns.append(mm(pm[:, idx * 2:idx * 2 + 2], w[:, kt, sub * 128:(sub + 1) * 128],
                                          sil[:, kt * 2:kt * 2 + 2], start=(kt == 0), stop=(kt == 7)))
                    P.group(PE, fns, reads=[w, sil], writes=[pm])
                mo = modsb[:, l * 96:(l + 1) * 96].rearrange("p (i w) -> p i w", w=2)
                pmv = pm[:, :].rearrange("p (i w) -> p i w", w=2)
                for who in range(2):
                    P.op(DVE, tt(mo[:, :, who], pmv[:, :, who], V("bmod", l), ALU.add),
                         reads=[pm, vecs], writes=[modsb])
                for who in range(2):
                    P.op(DVE, stt(MV(l, who, 0), MOD(l, 1, who), 1.0, V("gpre", l), ALU.add, ALU.mult),
                         reads=[modsb, vecs], writes=[mv])
                    P.op(DVE, tt(MV(l, who, 1), MOD(l, 2, who), V("gpost", l), ALU.mult),
                         reads=[modsb, vecs], writes=[mv])
                    P.op(DVE, stt(MV(l, who, 2), MOD(l, 4, who), 1.0, V("gfpre", l), ALU.add, ALU.mult),
                         reads=[modsb, vecs], writes=[mv])
                    P.op(DVE, tt(MV(l, who, 3), MOD(l, 5, who), V("gfpost", l), ALU.mult),
                         reads=[modsb, vecs], writes=[mv])
                lam_init = 0.8 - 0.6 * math.exp(-0.3 * l)
                so = l * 16
                tmp32 = sb("tmp32_%d" % l, [128, 32], F32)
                acc = sb("acc_%d" % l, [128, 4], F32)
                P.op(DVE, tt(tmp32[:], V("lq1", l), V("lk1", l), ALU.mult), reads=[vecs], writes=[tmp32])
                P.op(DVE, lambda e, a=acc, t=tmp32: e.reduce_sum(a[:, 0:1], t[:], mybir.AxisListType.X),
                     reads=[tmp32], writes=[acc])
                P.op(DVE, tt(tmp32[:], V("lq2", l), V("lk2", l), ALU.mult), reads=[vecs, acc], writes=[tmp32])
                P.op(DVE, lambda e, a=acc, t=tmp32: e.reduce_sum(a[:, 1:2], t[:], mybir.AxisListType.X),
                     reads=[tmp32], writes=[acc])
                P.op(ACT, act(acc[:, 2:4], acc[:, 0:2], AF.Exp), reads=[acc], writes=[acc])
                P.op(DVE, stt(small[:, so:so + 1], acc[:, 3:4], -lam_init, acc[:, 2:3], ALU.add, ALU.subtract),
                     reads=[acc], writes=[small])
                P.op(DVE, ts1(small[:, so + 1:so + 2], V("gsub", l), 1.0 - lam_init, ALU.mult),
                     reads=[vecs], writes=[small])
                P.op(ACT, act(small[:, so + 2:so + 6], V("sink", l), AF.Exp), reads=[vecs], writes=[small])

            stg = [sb("stg%d" % i, [128, 8, 256], BF16) for i in range(6)]
            n = 0
            for l in range(DEPTH):
                for ct in range(NCT):
                    s_ = stg[n % 6]
                    n += 1
                    for part in range(2):
                        c0 = part * FFN + ct * 128
                        P.dma(POOL, s_[:, :, part * 128:(part + 1) * 128],
                              wup_d[l, :, c0:c0 + 128].rearrange("(k p) c -> p k c", p=128), writes=[s_])
                    P.dma(SP, WUPB[l][ct].rearrange("p (k c) -> p k c", c=256), s_[:], reads=[s_], writes=[WUPB[l]])
            stg2 = [sb("stgd%d" % i, [128, 1024], BF16) for i in range(6)]
            for l in range(DEPTH):
                for ct in range(NCT):
                    s_ = stg2[n % 6]
                    n += 1
                    P.dma(POOL, s_[:], wdn_d[l, ct * 128:(ct + 1) * 128, :], writes=[s_])
                    P.dma(SP, WDNB[l][:, :, ct * 128:(ct + 1) * 128].rearrange("d p c -> p d c"),
                          s_[:].rearrange("p (d c) -> p d c", c=128), reads=[s_], writes=[WDNB[l]])
            P.barrier()
            flush()

        X_seq = [(XIN, XA, XB), (XB, XA, YT)]
        print("ops after phase0:", P.nops)

        for l in range(DEPTH):
            if stop_after == ("0", 0):
                break
            X0, X1, X2 = X_seq[l]
            need_ctx = l < DEPTH - 1
            with ExitStack() as es:
                def sb(name, shape, dt):
                    return T(es.enter_context(sbuf_t(name, shape, dt))[:], name)

                def ps(name, shape, dt=F32):
                    return T(es.enter_context(psum_t(name, shape, dt))[:], name, psum=True)

                wext_h = es.enter_context(sbuf_t("wext", [128, 8, NEXT], BF16))
                wext = [T(wext_h[:, kt, :], "wext%d" % kt) for kt in range(8)]
                for kt in range(8):
                    for hh in range(2):
                        c0, c1 = hh * (NEXT // 2), (hh + 1) * (NEXT // 2)
                        P.dma(POOL, wext[kt][:, c0:c1], winx_d[l, kt * 128:(kt + 1) * 128, c0:c1], writes=[wext[kt]],
                              max_dma_last_dim=4096)
                wuq0 = sb("wuq0", [128, 768], BF16)
                wuq1 = sb("wuq1", [64, 768], BF16)
                wukv = sb("wukv", [128, 512], BF16)
                P.dma(POOL, wuq0[:], wuqx_d[l, 0:128, :], writes=[wuq0])
                P.dma(POOL, wuq1[:], wuqx_d[l, 128:192, :], writes=[wuq1])
                P.dma(POOL, wukv[:], wukvx_d[l], writes=[wukv])

                xts = [sb("xt%d" % i, [128, 8, 512], F32) for i in range(2)]
                tabs = [sb("tab%d" % i, [128, 4, 512], F32) for i in range(2)]
                sq_h = es.enter_context(sbuf_t("sq", [128, 8, 512], BF16))
                sq = [T(sq_h[:, kt, :], "sq%d" % kt) for kt in range(8)]
                hT_h = [es.enter_context(sbuf_t("hT%d" % i, [128, 8, 512], BF16)) for i in range(2)]
                hTs = [[T(h[:, kt, :], "hT") for kt in range(8)] for h in hT_h]
                rstd = sb("rstd", [128, 512], F32)
                tmpf = [sb("tmpf%d" % i, [128, 512], F32) for i in range(4)]
                ost = [sb("ost%d" % i, [128, 512], BF16) for i in range(6)]
                sqs = [sb("sqs%d" % i, [128, 512], BF16) for i in range(2)]
                nrm = [sb("nrm%d" % i, [128, 512], F32) for i in range(2)]
                cqn0 = sb("cqn0", [128, 512], BF16)
                cqn1 = sb("cqn1", [64, 512], BF16)
                ckvn = sb("ckvn", [128, 512], BF16)
                krr = sb("krr", [128, 512], BF16)
                vst = [[sb("vst%d_%d" % (i, j), [128, 4, 4 * 65 if j < 2 else 2 * 65], BF16) for j in range(4)]
                       for i in range(2)]
                for i in range(2):
                    for j in range(4):
                        P.op(POOL, memset(vst[i][j][:], 1.0), writes=[vst[i][j]])
                pp = [ps("pp%d" % i, [128, 512]) for i in range(8)]
                ppi = [0]

                def nxt():
                    ppi[0] = (ppi[0] + 1) % 8
                    return pp[ppi[0]]

                tfi = [0]

                def ntmp():
                    tfi[0] = (tfi[0] + 1) % 4
                    return tmpf[tfi[0]]

                osi = [0]

                def nost():
                    osi[0] = (osi[0] + 1) % 6
                    return ost[osi[0]]

                blocks = [(0, LC, 1)] + [(LC + i * 512, 512, 0) for i in range(8)]
                for bi, (t0, tw, who) in enumerate(blocks):
                    xt = xts[bi % 2]
                    tab = tabs[bi % 2]
                    hT = hTs[bi % 2]
                    P.dma(SP, xt[:, :, 0:tw], X0[:, t0:t0 + tw].rearrange("(k p) t -> p k t", p=128),
                          reads=[X0], writes=[xt])
                    P.dma(SP, tab[:, :, 0:tw], rope_d[:, :, t0:t0 + tw].rearrange("f p t -> p f t"), writes=[tab])
                    for kt in range(8):
                        P.op(ACT, lambda e, o=sq[kt][:, 0:tw], i=xt[:, kt, 0:tw]: e.square(o, i),
                             reads=[xt], writes=[sq[kt]])
                    pn = nxt()
                    P.group(PE, [mm(pn[:, 0:tw], ones_b[:], sq[kt][:, 0:tw], start=(kt == 0), stop=(kt == 7))
                                 for kt in range(8)], reads=sq + [ones_b], writes=[pn])
                    P.op(ACT, act(rstd[:, 0:tw], pn[:, 0:tw], AF.Ln, bias=eps_t[:, 0:1], scale=1.0 / D),
                         reads=[pn, eps_t], writes=[rstd])
                    P.op(ACT, act(rstd[:, 0:tw], rstd[:, 0:tw], AF.Exp, scale=-0.5), reads=[rstd], writes=[rstd])
                    A1 = MV(l, who, 0)
                    B1 = MOD(l, 0, who)
                    for kt in range(8):
                        tm = ntmp()
                        P.op(DVE, stt(tm[:, 0:tw], xt[:, kt, 0:tw], A1[:, kt:kt + 1], rstd[:, 0:tw], ALU.mult, ALU.mult),
                             reads=[xt, mv, rstd], writes=[tm])
                        P.op(ACT, act(hT[kt][:, 0:tw], tm[:, 0:tw], AF.Identity, bias=B1[:, kt:kt + 1]),
                             reads=[tm, modsb], writes=[hT[kt]])

                    def proj(ti, rows=None):
                        c0, cw = EXT_OFFS[ti]
                        p_ = nxt()
                        P.group(PE, [mm(p_[0:cw, 0:tw], wext[kt][:, c0:c0 + cw], hT[kt][:, 0:tw],
                                        start=(kt == 0), stop=(kt == 7)) for kt in range(8)],
                                reads=wext + hT, writes=[p_])
                        return p_

                    def store_qk(qi, src, rows=128):
                        P.dma(SP, QK[qi][0:rows, t0:t0 + tw], src[0:rows, 0:tw], reads=[src], writes=[QK[qi]])

                    def rope(pq, ppm, fi, dst, r0=0, r1=128, eng2=DVE):
                        a = ntmp()
                        P.op(DVE, tt(a[r0:r1, 0:tw], pq[r0:r1, 0:tw], tab[r0:r1, fi, 0:tw], ALU.mult),
                             reads=[pq, tab], writes=[a])
                        b = ntmp()
                        P.op(DVE, tt(b[r0:r1, 0:tw], ppm[r0:r1, 0:tw], tab[r0:r1, fi + 1, 0:tw], ALU.mult),
                             reads=[ppm, tab], writes=[b])
                        P.op(eng2, tt(dst[r0:r1, 0:tw], a[r0:r1, 0:tw], b[r0:r1, 0:tw], ALU.add),
                             reads=[a, b], writes=[dst])

                    for (ta, tp, fi, qi) in ((T_DQ0, T_DQ0P, 0, Q_DQ0), (T_DQ1, T_DQ1P, 0, Q_DQ1),
                                             (T_DK0, T_DK0P, 0, Q_DK0), (T_DK1, T_DK1P, 0, Q_DK1),
                                             (T_SQ0, T_SQ0P, 2, Q_SQ0), (T_SQ1, T_SQ1P, 2, Q_SQ1),
                                             (T_SK, T_SKP, 2, Q_SK)):
                        pa = proj(ta)
                        pb = proj(tp)
                        o = nost()
                        rope(pa, pb, fi, o)
                        store_qk(qi, o)
                    for (ta, tp, qi, g, gp) in ((T_GQ0, T_GQ0P, Q_GQ0, "ggq", "ggqp"), (T_GQ1, T_GQ1P, Q_GQ1, "ggq", "ggqp"),
                                                (T_GK, T_GKP, Q_GK, "ggk", "ggkp")):
                        pa = proj(ta)
                        pb = proj(tp)
                        s_ = sqs[0]
                        P.op(ACT, lambda e, o=s_[:, 0:tw], i=pa[:, 0:tw]: e.square(o, i), reads=[pa], writes=[s_])
                        pn = nxt()
                        P.op(PE, mm(pn[:, 0:tw], bd64[:], s_[:, 0:tw]), reads=[bd64, s_], writes=[pn])
                        r_ = nrm[0]
                        P.op(ACT, act(r_[:, 0:tw], pn[:, 0:tw], AF.Ln, bias=eps_t[:, 0:1], scale=1.0 / 64),
                             reads=[pn, eps_t], writes=[r_])
                        P.op(ACT, act(r_[:, 0:tw], r_[:, 0:tw], AF.Exp, scale=-0.5), reads=[r_], writes=[r_])
                        a = ntmp()
                        P.op(DVE, stt(a[:, 0:tw], pa[:, 0:tw], V(g, l), tab[:, 2, 0:tw], ALU.mult, ALU.mult),
                             reads=[pa, vecs, tab], writes=[a])
                        b = ntmp()
                        P.op(DVE, stt(b[:, 0:tw], pb[:, 0:tw], V(gp, l), tab[:, 3, 0:tw], ALU.mult, ALU.mult),
                             reads=[pb, vecs, tab], writes=[b])
                        P.op(DVE, tt(a[:, 0:tw], a[:, 0:tw], b[:, 0:tw], ALU.add), reads=[a, b], writes=[a])
                        o = nost()
                        P.op(DVE, tt(o[:, 0:tw], a[:, 0:tw], r_[:, 0:tw], ALU.mult), reads=[a, r_], writes=[o])
                        store_qk(qi, o)
                    pc0 = proj(T_MCQ0)
                    pc1 = proj(T_MCQ1)
                    P.op(ACT, lambda e, o=sqs[0][:, 0:tw], i=pc0[:, 0:tw]: e.square(o, i), reads=[pc0], writes=[sqs[0]])
                    P.op(ACT, lambda e, o=sqs[1][0:64, 0:tw], i=pc1[0:64, 0:tw]: e.square(o, i), reads=[pc1], writes=[sqs[1]])
                    pn = nxt()
                    P.group(PE, [mm(pn[:, 0:tw], ones_b[:, :], sqs[0][:, 0:tw], start=True, stop=False),
                                 mm(pn[:, 0:tw], ones_b[0:64, :], sqs[1][0:64, 0:tw], start=False, stop=True)],
                            reads=[ones_b, sqs[0], sqs[1]], writes=[pn])
                    r_ = nrm[0]
                    P.op(ACT, act(r_[:, 0:tw], pn[:, 0:tw], AF.Ln, bias=eps_t[:, 0:1], scale=1.0 / 192),
                         reads=[pn, eps_t], writes=[r_])
                    P.op(ACT, act(r_[:, 0:tw], r_[:, 0:tw], AF.Exp, scale=-0.5), reads=[r_], writes=[r_])
                    mg = V("mgq", l)
                    P.op(DVE, stt(cqn0[:, 0:tw], pc0[:, 0:tw], mg[:, 0:1], r_[:, 0:tw], ALU.mult, ALU.mult),
                         reads=[pc0, vecs, r_], writes=[cqn0])
                    P.op(DVE, stt(cqn1[0:64, 0:tw], pc1[0:64, 0:tw], mg[0:64, 1:2], r_[0:64, 0:tw], ALU.mult, ALU.mult),
                         reads=[pc1, vecs, r_], writes=[cqn1])
                    pkv = proj(T_MCKV)
                    P.op(ACT, lambda e, o=sqs[0][:, 0:tw], i=pkv[:, 0:tw]: e.square(o, i), reads=[pkv], writes=[sqs[0]])
                    pn = nxt()
                    P.op(PE, mm(pn[:, 0:tw], ones_b[:], sqs[0][:, 0:tw]), reads=[ones_b, sqs[0]], writes=[pn])
                    r2 = nrm[1]
                    P.op(ACT, act(r2[:, 0:tw], pn[:, 0:tw], AF.Ln, bias=eps_t[:, 0:1], scale=1.0 / 128),
                         reads=[pn, eps_t], writes=[r2])
                    P.op(ACT, act(r2[:, 0:tw], r2[:, 0:tw], AF.Exp, scale=-0.5), reads=[r2], writes=[r2])
                    P.op(DVE, stt(ckvn[:, 0:tw], pkv[:, 0:tw], V("mgkv", l), r2[:, 0:tw], ALU.mult, ALU.mult),
                         reads=[pkv, vecs, r2], writes=[ckvn])
                    pk = proj(T_MKR)
                    pkp = proj(T_MKRP)
                    rope(pk, pkp, 0, krr, 64, 96)
                    for h in range(4):
                        pq = nxt()
                        P.group(PE, [mm(pq[0:96, 0:tw], wuq0[:, h * 192:h * 192 + 96], cqn0[:, 0:tw], start=True, stop=False),
                                     mm(pq[0:96, 0:tw], wuq1[0:64, h * 192:h * 192 + 96], cqn1[0:64, 0:tw], start=False, stop=True)],
                                reads=[wuq0, wuq1, cqn0, cqn1], writes=[pq])
                        pqp = nxt()
                        P.group(PE, [mm(pqp[0:96, 0:tw], wuq0[:, h * 192 + 96:h * 192 + 192], cqn0[:, 0:tw], start=True, stop=False),
                                     mm(pqp[0:96, 0:tw], wuq1[0:64, h * 192 + 96:h * 192 + 192], cqn1[0:64, 0:tw], start=False, stop=True)],
                                reads=[wuq0, wuq1, cqn0, cqn1], writes=[pqp])
                        o = nost()
                        P.op(ACT, acp(o[0:64, 0:tw], pq[0:64, 0:tw]), reads=[pq], writes=[o])
                        rope(pq, pqp, 0, o, 64, 96)
                        store_qk(Q_MQ + h, o, 96)
                        pkn = nxt()
                        P.op(PE, mm(pkn[0:64, 0:tw], wukv[:, h * 64:(h + 1) * 64], ckvn[:, 0:tw]),
                             reads=[wukv, ckvn], writes=[pkn])
                        o2 = nost()
                        P.op(ACT, acp(o2[0:64, 0:tw], pkn[0:64, 0:tw]), reads=[pkn], writes=[o2])
                        P.op(POOL, cp(o2[64:96, 0:tw], krr[64:96, 0:tw]), reads=[krr], writes=[o2])
                        store_qk(Q_MK + h, o2, 96)
                    vs_ = vst[bi % 2]
                    nsub = tw // 128
                    for sub in range(nsub):
                        pv = nxt()
                        P.group(PE, [mm(pv[:, 0:512], hT[kt][:, sub * 128:(sub + 1) * 128],
                                        wext[kt][:, EXT_VOFF:EXT_VOFF + 512], start=(kt == 0), stop=(kt == 7))
                                     for kt in range(8)], reads=wext + hT, writes=[pv])
                        P.op(DVE, cp(vs_[0][:, sub, :].rearrange("p (h c) -> p h c", c=65)[:, :, 0:64],
                                     pv[:, 0:256].rearrange("p (h c) -> p h c", c=64)), reads=[pv], writes=[vs_[0]])
                        P.op(DVE, cp(vs_[2][:, sub, :].rearrange("p (h c) -> p h c", c=65)[:, :, 0:64],
                                     pv[:, 256:384].rearrange("p (h c) -> p h c", c=64)), reads=[pv], writes=[vs_[2]])
                        P.op(DVE, cp(vs_[3][:, sub, :].rearrange("p (h c) -> p h c", c=65)[:, :, 0:64],
                                     pv[:, 384:512].rearrange("p (h c) -> p h c", c=64)), reads=[pv], writes=[vs_[3]])
                        pm_ = nxt()
                        P.op(PE, mm(pm_[:, 0:256], ckvn[:, sub * 128:(sub + 1) * 128], wukv[:, 256:512]),
                             reads=[ckvn, wukv], writes=[pm_])
                        P.op(DVE, cp(vs_[1][:, sub, :].rearrange("p (h c) -> p h c", c=65)[:, :, 0:64],
                                     pm_[:, 0:256].rearrange("p (h c) -> p h c", c=64)), reads=[pm_], writes=[vs_[1]])
                    for (j, VT_) in ((0, VD), (1, VM), (2, VS), (3, VG)):
                        P.dma(SP, VT_[t0:t0 + tw, :].rearrange("(s p) c -> p s c", p=128), vs_[j][:, 0:nsub, :],
                              reads=[vs_[j]], writes=[VT_])
                P.barrier()
                flush()
                print("ops after P", l, P.nops)
            if stop_after == ("P", l):
                break

            with ExitStack() as es:
                def sb(name, shape, dt):
                    return T(es.enter_context(sbuf_t(name, shape, dt))[:], name)

                def ps(name, shape, dt=F32):
                    return T(es.enter_context(psum_t(name, shape, dt))[:], name, psum=True)

                Sps = [ps("Sps%d" % i, [128, 2, 512]) for i in range(2)]
                Ops = [ps("Ops%d" % i, [128, 512]) for i in range(2)]
                BCp = ps("BCp", [128, 512])
                SSp = BCp
                JK = ps("JK", [128, 512])
                jrhs = sb("jrhs", [128, 512], BF16)
                P.op(POOL, memset(jrhs[:], 0.0), writes=[jrhs])
                kres = [sb("kres%d" % i, [128, TT], BF16) for i in range(4)]
                vres = sb("vres", [128, NKT, 260], BF16)
                qts = [[sb("qt%d_%d" % (i, j), [128, 512], BF16) for j in range(8)] for i in range(2)]
                for i in (2, 3):
                    P.op(POOL, memset(kres[i][:], 0.0), writes=[kres[i]])
                pts = [sb("pt%d" % i, [128, 2, 512], BF16) for i in range(3)]
                osb = [sb("osb%d" % i, [128, 512], F32) for i in range(2)]
                rec = [sb("rec%d" % i, [128, 512], F32) for i in range(2)]
                a1 = sb("a1", [128, 512], F32)
                od = sb("od", [128, 512], F32)
                osq = sb("osq", [128, 512], BF16)
                rs2 = sb("rs2", [128, 512], F32)
                outb = [sb("outb%d" % i, [128, 512], BF16) for i in range(3)]
                msk = sb("msk", [128, 6, 512], BF16)
                P.dma(SP, msk[:], mask_d.rearrange("r p q -> p r q"), writes=[msk])
                for i in range(2):
                    P.op(POOL, memset(osb[i][:], 0.0), writes=[osb[i]])
                so = l * 16
                cnt = {"pt": 0, "S": 0, "O": 0, "ob": 0, "q": 0}

                qblocks = [(LC + i * 512, 512, False) for i in range(8)]
                if need_ctx:
                    qblocks = [(0, LC, True)] + qblocks

                mixers = []
                mixers.append(dict(name="diff", ktiles=[Q_DK0, Q_DK1], krows=[128, 128], V=VD, vw=260,
                                   qtiles=[Q_DQ0, Q_DQ1], qrows=[128, 128],
                                   heads=[dict(qt=j // 4, kt=j // 4, r0=(j % 4) * 32, r1=(j % 4) * 32 + 32, vh=j // 2,
                                               scale=32 ** -0.5, feat=(j // 2) * 64, diff=j % 2) for j in range(8)]))
                mixers.append(dict(name="swa", ktiles=[Q_SK], krows=[128], V=VS, vw=130,
                                   qtiles=[Q_SQ0, Q_SQ1], qrows=[128, 128],
                                   heads=[dict(qt=h % 2, kt=0, r0=(h // 2) * 64, r1=(h // 2) * 64 + 64, vh=h // 2,
                                               scale=64 ** -0.5, feat=256 + h * 64, sink=h) for h in range(4)]))
                mixers.append(dict(name="mla", ktiles=[Q_MK + h for h in range(4)], krows=[96] * 4, V=VM, vw=260,
                                   qtiles=[Q_MQ + h for h in range(4)], qrows=[96] * 4,
                                   heads=[dict(qt=h, kt=h, r0=0, r1=96, vh=h, scale=96 ** -0.5, feat=512 + h * 64)
                                          for h in range(4)]))
                mixers.append(dict(name="gqa", ktiles=[Q_GK], krows=[128], V=VG, vw=130,
                                   qtiles=[Q_GQ0, Q_GQ1], qrows=[128, 128],
                                   heads=[dict(qt=h % 2, kt=0, r0=(h // 2) * 64, r1=(h // 2) * 64 + 64, vh=h // 2,
                                               scale=64 ** -0.5, feat=768 + h * 64) for h in range(4)]))

                for mx in mixers:
                    for hi_, hd_ in enumerate(mx["heads"]):
                        hd_["hi"] = hi_
                    for i in range(2):
                        for j in range(len(mx["heads"])):
                            P.op(POOL, memset(qts[i][j][:], 0.0), writes=[qts[i][j]])
                    for i, qi in enumerate(mx["ktiles"]):
                        r = mx["krows"][i]
                        P.dma(SP, kres[i][0:r, :], QK[qi][0:r, :], reads=[QK[qi]], writes=[kres[i]])
                    vw = mx["vw"]
                    P.dma(SP, vres[:, :, 0:vw], mx["V"][:, :].rearrange("(s p) c -> p s c", p=128),
                          reads=[mx["V"]], writes=[vres])
                    jobs = []
                    for (t0, qw, isctx) in qblocks:
                        qs = qts[cnt["q"] % 2]
                        cnt["q"] += 1
                        qload = []
                        for hd_ in mx["heads"]:
                            qload.append((qs[hd_["hi"]], hd_["r0"], hd_["r1"], mx["qtiles"][hd_["qt"]]))
                        for hd in mx["heads"]:
                            if isctx:
                                kts = [(0, None), (1, None)]
                            elif mx["name"] == "swa":
                                qb = (t0 - LC) // 128
                                kts = [(0, None), (1, None)]
                                for r in range(-1, 5):
                                    kt_ = qb + r
                                    if 0 <= kt_ < 32:
                                        kts.append((2 + kt_, r + 1))
                            else:
                                kts = [(i, None) for i in range(NKT)]
                            pairs = [kts[i:i + 2] for i in range(0, len(kts), 2)]
                            for pi, pr in enumerate(pairs):
                                jobs.append(dict(t0=t0, qw=qw, hd=hd, pr=pr, first=(pi == 0), last=(pi == len(pairs) - 1),
                                                 qs=qs, qload=qload if (hd is mx["heads"][0] and pi == 0) else None,
                                                 isctx=isctx))

                    def do_S(jb):
                        if jb["qload"] is not None:
                            for (qt_, ra, rb, qi) in jb["qload"]:
                                P.dma(SP, qt_[ra:rb, 0:jb["qw"]], QK[qi][ra:rb, jb["t0"]:jb["t0"] + jb["qw"]],
                                      reads=[QK[qi]], writes=[qt_])
                        hd = jb["hd"]
                        Sp = Sps[cnt["S"] % 2]
                        cnt["S"] += 1
                        jb["Sp"] = Sp
                        qt_ = jb["qs"][hd["hi"]]
                        kr = kres[hd["kt"]]
                        fns = []
                        for i, (kt_, _) in enumerate(jb["pr"]):
                            fns.append(mm(Sp[:, i, 0:jb["qw"]], kr[:, kt_ * 128:(kt_ + 1) * 128],
                                          qt_[:, 0:jb["qw"]]))
                        P.group(PE, fns, reads=[kr, qt_], writes=[Sp])

                    def do_exp(jb):
                        hd = jb["hd"]
                        qw = jb["qw"]
                        npair = len(jb["pr"])
                        pt = pts[cnt["pt"] % 3]
                        cnt["pt"] += 1
                        jb["pt"] = pt
                        Sp = jb["Sp"]
                        if qw == 512:
                            P.op(ACT, act(pt[:, 0:npair, :].rearrange("p a b -> p (a b)"),
                                          Sp[:, 0:npair, :].rearrange("p a b -> p (a b)"), AF.Exp, scale=hd["scale"]),
                                 reads=[Sp], writes=[pt])
                        else:
                            for i in range(npair):
                                P.op(ACT, act(pt[:, i, 0:qw], Sp[:, i, 0:qw], AF.Exp, scale=hd["scale"]),
                                     reads=[Sp], writes=[pt])
                        for i, (kt_, mi) in enumerate(jb["pr"]):
                            if mi is not None:
                                P.op(DVE, tt(pt[:, i, 0:qw], pt[:, i, 0:qw], msk[:, mi, 0:qw], ALU.mult),
                                     reads=[pt, msk], writes=[pt])

                    def do_pv(jb):
                        hd = jb["hd"]
                        qw = jb["qw"]
                        npair = len(jb["pr"])
                        pt = jb["pt"]
                        if jb["first"]:
                            cnt["O"] += 1
                        Op = Ops[cnt["O"] % 2]
                        vh = hd["vh"]
                        fns = []
                        for i, (kt_, _) in enumerate(jb["pr"]):
                            fns.append(mm(Op[0:65, 0:qw], vres[:, kt_, vh * 65:(vh + 1) * 65], pt[:, i, 0:qw],
                                          start=(jb["first"] and i == 0), stop=(jb["last"] and i == npair - 1)))
                        P.group(PE, fns, reads=[vres, pt], writes=[Op])
                        if jb["last"]:
                            finalize(jb, Op)

                    deferred = []

                    def tick():
                        ready = []
                        for it in deferred:
                            it[0] -= 1
                        while deferred and deferred[0][0] <= 0:
                            ready.append(deferred.pop(0)[1])
                        for fn_ in ready:
                            fn_()

                    def finalize(jb, Op):
                        hd = jb["hd"]
                        qw = jb["qw"]
                        t0 = jb["t0"]
                        par = cnt["O"] % 2
                        ob_ = osb[par]
                        rc = rec[par]
                        P.op(DVE, cp(ob_[0:65, 0:qw], Op[0:65, 0:qw]), reads=[Op], writes=[ob_])
                        f0 = hd["feat"]

                        def stage_b():
                            P.op(PE, mm(BCp[0:64, 0:qw], sel[:, 0:64], ob_[:, 0:qw]), reads=[sel, ob_], writes=[BCp])
                            if "sink" in hd:
                                c_ = so + 2 + hd["sink"]
                                P.op(DVE, ts1(rc[0:64, 0:qw], BCp[0:64, 0:qw], small[0:64, c_:c_ + 1], ALU.add),
                                     reads=[BCp, small], writes=[rc])
                                P.op(DVE, lambda e, o=rc[0:64, 0:qw]: e.reciprocal(o, o), reads=[rc], writes=[rc])
                            else:
                                P.op(DVE, lambda e, o=rc[0:64, 0:qw], i=BCp[0:64, 0:qw]: e.reciprocal(o, i),
                                     reads=[BCp], writes=[rc])
                            if "diff" not in hd:
                                ob2 = outb[cnt["ob"] % 3]
                                cnt["ob"] += 1
                                P.op(DVE, tt(ob2[0:64, 0:qw], ob_[0:64, 0:qw], rc[0:64, 0:qw], ALU.mult),
                                     reads=[ob_, rc], writes=[ob2])
                                P.dma(SP, MIX[f0:f0 + 64, t0:t0 + qw], ob2[0:64, 0:qw], reads=[ob2], writes=[MIX])
                            elif hd["diff"] == 0:
                                P.op(DVE, tt(a1[0:64, 0:qw], ob_[0:64, 0:qw], rc[0:64, 0:qw], ALU.mult),
                                     reads=[ob_, rc], writes=[a1])
                            else:
                                P.op(DVE, stt(od[0:64, 0:qw], ob_[0:64, 0:qw], small[0:64, so:so + 1], rc[0:64, 0:qw],
                                              ALU.mult, ALU.mult), reads=[ob_, small, rc], writes=[od])
                                P.op(DVE, tt(od[0:64, 0:qw], od[0:64, 0:qw], a1[0:64, 0:qw], ALU.add),
                                     reads=[od, a1], writes=[od])
                                P.op(POOL, tt(osq[0:64, 0:qw], od[0:64, 0:qw], od[0:64, 0:qw], ALU.mult),
                                     reads=[od], writes=[osq])
                                deferred.append([2, stage_c])

                        def stage_c():
                            P.op(PE, mm(SSp[0:64, 0:qw], ones_b[0:64, 0:64], osq[0:64, 0:qw]),
                                 reads=[ones_b, osq], writes=[SSp])
                            P.op(ACT, act(rs2[0:64, 0:qw], SSp[0:64, 0:qw], AF.Ln, bias=eps_t[0:64, 0:1], scale=1.0 / 64),
                                 reads=[SSp, eps_t], writes=[rs2])
                            P.op(ACT, act(rs2[0:64, 0:qw], rs2[0:64, 0:qw], AF.Exp, scale=-0.5), reads=[rs2], writes=[rs2])
                            ob2 = outb[cnt["ob"] % 3]
                            cnt["ob"] += 1
                            P.op(DVE, stt(ob2[0:64, 0:qw], od[0:64, 0:qw], small[0:64, so + 1:so + 2], rs2[0:64, 0:qw],
                                          ALU.mult, ALU.mult), reads=[od, small, rs2], writes=[ob2])
                            P.dma(SP, MIX[f0:f0 + 64, t0:t0 + qw], ob2[0:64, 0:qw], reads=[ob2], writes=[MIX])

                        deferred.append([2, stage_b])

                    if jobs:
                        do_S(jobs[0])
                        if len(jobs) > 1:
                            do_S(jobs[1])
                        for ji, jb in enumerate(jobs):
                            do_exp(jb)
                            if ji + 2 < len(jobs):
                                do_S(jobs[ji + 2])
                            tick()
                            do_pv(jb)
                        for _ in range(4):
                            tick()
                    flush()
                P.barrier()
                flush()
            if stop_after == ("A", l):
                break

            with ExitStack() as es:
                def sb(name, shape, dt):
                    return T(es.enter_context(sbuf_t(name, shape, dt))[:], name)

                def ps(name, shape, dt=F32):
                    return T(es.enter_context(psum_t(name, shape, dt))[:], name, psum=True)

                wo_h = es.enter_context(sbuf_t("wo", [128, 8, D], BF16))
                wo = [T(wo_h[:, kt, :], "wo%d" % kt) for kt in range(8)]
                for kt in range(8):
                    P.dma(POOL, wo[kt][:], wout_d[l, kt * 128:(kt + 1) * 128, :], writes=[wo[kt]])
                xts = [sb("xo%d" % i, [128, 8, 512], F32) for i in range(2)]
                mxs = [sb("mx%d" % i, [128, 8, 512], BF16) for i in range(2)]
                ysb_h = es.enter_context(sbuf_t("ysb", [128, 8, 512], F32))
                ysb = [T(ysb_h[:, dt, :], "ysb") for dt in range(8)]
                ysq_h = es.enter_context(sbuf_t("ysq", [128, 8, 512], BF16))
                ysq = [T(ysq_h[:, dt, :], "ysq") for dt in range(8)]
                rstd = sb("rstdo", [128, 512], F32)
                tmpf = [sb("tmpo%d" % i, [128, 512], F32) for i in range(3)]
                pp = [ps("po%d" % i, [128, 512]) for i in range(6)]
                blocks = [(LC + i * 512, 512, 0) for i in range(8)]
                if need_ctx:
                    blocks = [(0, LC, 1)] + blocks
                k = 0
                for bi, (t0, tw, who) in enumerate(blocks):
                    xt = xts[bi % 2]
                    mx_ = mxs[bi % 2]
                    P.dma(SP, xt[:, :, 0:tw], X0[:, t0:t0 + tw].rearrange("(k p) t -> p k t", p=128),
                          reads=[X0], writes=[xt])
                    P.dma(SP, mx_[:, :, 0:tw], MIX[:, t0:t0 + tw].rearrange("(k p) t -> p k t", p=128),
                          reads=[MIX], writes=[mx_])
                    for dt in range(8):
                        p_ = pp[k % 5]
                        k += 1
                        P.group(PE, [mm(p_[:, 0:tw], wo[mt][:, dt * 128:(dt + 1) * 128], mx_[:, mt, 0:tw],
                                        start=(mt == 0), stop=(mt == 7)) for mt in range(8)],
                                reads=wo + [mx_], writes=[p_])
                        P.op(ACT, lambda e, o=ysq[dt][:, 0:tw], i=p_[:, 0:tw]: e.square(o, i), reads=[p_], writes=[ysq[dt]])
                        P.op(DVE, cp(ysb[dt][:, 0:tw], p_[:, 0:tw]), reads=[p_], writes=[ysb[dt]])
                    pn = pp[5]
                    P.group(PE, [mm(pn[:, 0:tw], ones_b[:], ysq[dt][:, 0:tw], start=(dt == 0), stop=(dt == 7))
                                 for dt in range(8)], reads=ysq + [ones_b], writes=[pn])
                    P.op(ACT, act(rstd[:, 0:tw], pn[:, 0:tw], AF.Ln, bias=eps_t[:, 0:1], scale=1.0 / D),
                         reads=[pn, eps_t], writes=[rstd])
                    P.op(ACT, act(rstd[:, 0:tw], rstd[:, 0:tw], AF.Exp, scale=-0.5), reads=[rstd], writes=[rstd])
                    G2 = MV(l, who, 1)
                    for dt in range(8):
                        tm = tmpf[dt % 3]
                        P.op(DVE, stt(tm[:, 0:tw], ysb[dt][:, 0:tw], G2[:, dt:dt + 1], rstd[:, 0:tw], ALU.mult, ALU.mult),
                             reads=[ysb[dt], mv, rstd], writes=[tm])
                        P.op(DVE, tt(xt[:, dt, 0:tw], xt[:, dt, 0:tw], tm[:, 0:tw], ALU.add), reads=[tm, xt], writes=[xt])
                    P.dma(SP, X1[:, t0:t0 + tw].rearrange("(k p) t -> p k t", p=128), xt[:, :, 0:tw],
                          reads=[xt], writes=[X1])
                P.barrier()
                flush()
            if stop_after == ("O", l):
                break

            with ExitStack() as es:
                def sb(name, shape, dt):
                    return T(es.enter_context(sbuf_t(name, shape, dt))[:], name)

                def ps(name, shape, dt=F32):
                    return T(es.enter_context(psum_t(name, shape, dt))[:], name, psum=True)

                CW = 1024
                xt = sb("xf", [128, 8, CW + 2], F32)
                fsb_v = [T(xt[:, dt, 0:512], "fsb") for dt in range(8)]
                xre_v = T(xt[:, :, 512:1024], "xre")
                sq_h = es.enter_context(sbuf_t("sqf", [128, 8, CW + 2], BF16))
                sqf = [T(sq_h[:, kt, :], "sqf") for kt in range(8)]
                hT_h = es.enter_context(sbuf_t("hTf", [128, 8, CW + 2], BF16))
                hTf = [T(hT_h[:, kt, :], "hTf") for kt in range(8)]
                g_h = es.enter_context(sbuf_t("gf", [128, NCT, CW], BF16))
                gf = [T(g_h[:, ct, :], "gf") for ct in range(NCT)]
                rstd = sb("rstdf", [128, CW + 2], F32)
                tmpf = [sb("tmpff%d" % i, [128, 512], F32) for i in range(3)]
                U = [[sb("U%d_%d" % (i, j), [128, CW + 2], F32) for j in range(2)] for i in range(2)]
                Y = [[sb("Y%d_%d" % (i, j), [128, CW], F32) for j in range(2)] for i in range(2)]
                wup = [sb("wup%d" % i, [128, 8, 256], BF16) for i in range(3)]
                wdn = [sb("wdn%d" % i, [128, NCT, 128], BF16) for i in range(3)]
                pp = [ps("pf%d" % i, [128, 512]) for i in range(8)]
                ppi = [0]

                def nxt():
                    ppi[0] = (ppi[0] + 1) % 8
                    return pp[ppi[0]]

                chunks = [(LC + i * CW, CW, 0) for i in range(S // CW)]
                if need_ctx:
                    chunks = [(0, LC, 1)] + chunks
                prev_done = []
                nw = 0
                nd = 0
                for (t0, cw, who) in chunks:
                    lo = 0 if who == 1 else LC
                    hi = LC if who == 1 else TT
                    hasl = t0 - 1 >= lo
                    hasr = t0 + cw < hi
                    a0 = t0 - 1 if hasl else t0
                    a1_ = t0 + cw + 1 if hasr else t0 + cw
                    c0 = 0 if hasl else 1
                    c1 = c0 + (a1_ - a0)
                    P.dma(SP, xt[:, :, c0:c1], X1[:, a0:a1_].rearrange("(k p) t -> p k t", p=128),
                          reads=[X1], writes=[xt], extra=prev_done)
                    pieces = [(s_, min(s_ + 512, c1)) for s_ in range(c0, c1, 512)]
                    xre = T(xt[:, :, 512:512 + min(512, cw)], "xre")
                    for kt in range(8):
                        if kt % 2 == 0:
                            P.op(ACT, lambda e, o=sqf[kt][:, c0:c1], i=xt[:, kt, c0:c1]: e.square(o, i),
                                 reads=[xt], writes=[sqf[kt]])
                        else:
                            P.op(POOL, tt(sqf[kt][:, c0:c1], xt[:, kt, c0:c1], xt[:, kt, c0:c1], ALU.mult),
                                 reads=[xt], writes=[sqf[kt]])
                    for (s_, e_) in pieces:
                        pn = nxt()
                        P.group(PE, [mm(pn[:, 0:e_ - s_], ones_b[:], sqf[kt][:, s_:e_], start=(kt == 0), stop=(kt == 7))
                                     for kt in range(8)], reads=sqf + [ones_b], writes=[pn])
                        P.op(ACT, act(rstd[:, s_:e_], pn[:, 0:e_ - s_], AF.Ln, bias=eps_t[:, 0:1], scale=1.0 / D),
                             reads=[pn, eps_t], writes=[rstd])
                    P.op(ACT, act(rstd[:, c0:c1], rstd[:, c0:c1], AF.Exp, scale=-0.5), reads=[rstd], writes=[rstd])
                    A4 = MV(l, who, 2)
                    B4 = MOD(l, 3, who)
                    for kt in range(8):
                        for (s_, e_) in pieces:
                            tm = tmpf[(kt + s_ // 512) % 3]
                            P.op(DVE, stt(tm[:, 0:e_ - s_], xt[:, kt, s_:e_], A4[:, kt:kt + 1], rstd[:, s_:e_],
                                          ALU.mult, ALU.mult), reads=[xt, mv, rstd], writes=[tm])
                            P.op(ACT, act(hTf[kt][:, s_:e_], tm[:, 0:e_ - s_], AF.Identity, bias=B4[:, kt:kt + 1]),
                                 reads=[tm, modsb], writes=[hTf[kt]])
                        if not hasl:
                            P.op(POOL, memset(hTf[kt][:, 0:1], 0.0), writes=[hTf[kt]])
                        if not hasr:
                            P.op(POOL, memset(hTf[kt][:, cw + 1:cw + 2], 0.0), writes=[hTf[kt]])
                    mpieces = [(s_, min(s_ + 512, cw + 2)) for s_ in range(0, cw + 2, 512)]
                    pend_gate = None
                    for ct in range(NCT):
                        w_ = wup[nw % 3]
                        nw += 1
                        P.dma(SP if ct % 2 == 0 else ACT, w_[:], WUPB[l][ct].rearrange("p (k c) -> p k c", c=256),
                              reads=[WUPB[l]], writes=[w_])
                        ys = []
                        for part in range(2):
                            u = U[ct % 2][part]
                            for (s_, e_) in mpieces:
                                p_ = nxt()
                                P.group(PE, [mm(p_[:, 0:e_ - s_], w_[:, kt, part * 128:(part + 1) * 128], hTf[kt][:, s_:e_],
                                                start=(kt == 0), stop=(kt == 7)) for kt in range(8)],
                                        reads=[w_] + hTf, writes=[p_])
                                if part == 0:
                                    P.op(ACT, acp(u[:, s_:e_], p_[:, 0:e_ - s_]), reads=[p_], writes=[u])
                                else:
                                    P.op(DVE, cp(u[:, s_:e_], p_[:, 0:e_ - s_]), reads=[p_], writes=[u])
                            y = Y[ct % 2][part]
                            ti = part * NCT + ct
                            cwv = V("cw", l)
                            cbv = V("cb", l)
                            eng = DVE
                            P.op(ACT, act(y[:, 0:cw], u[:, 0:cw], AF.Identity, bias=cbv[:, ti:ti + 1], scale=cwv[:, ti:ti + 1]),
                                 reads=[u, vecs], writes=[y])
                            P.op(eng, stt(y[:, 0:cw], u[:, 1:cw + 1], cwv[:, 44 + ti:44 + ti + 1], y[:, 0:cw], ALU.mult, ALU.add),
                                 reads=[u, vecs, y], writes=[y])
                            P.op(eng, stt(y[:, 0:cw], u[:, 2:cw + 2], cwv[:, 88 + ti:88 + ti + 1], y[:, 0:cw], ALU.mult, ALU.add),
                                 reads=[u, vecs, y], writes=[y])
                            ys.append(y)
                        if pend_gate is not None:
                            pend_gate()

                        def gate(ct=ct, ys=ys, cw=cw):
                            ya, yv = ys
                            sa = U[ct % 2][0]
                            P.op(ACT, act(sa[:, 0:cw], ya[:, 0:cw], AF.Silu), reads=[ya], writes=[sa])
                            P.op(DVE, tt(gf[ct][:, 0:cw], sa[:, 0:cw], yv[:, 0:cw], ALU.mult), reads=[sa, yv], writes=[gf[ct]])
                        pend_gate = gate
                    pend_gate()
                    G5 = MV(l, who, 3)
                    done = []
                    for s_ in range(0, cw, 512):
                        hw_ = min(512, cw - s_)
                        tx = P.dma(SP, xre[:], X1[:, t0 + s_:t0 + s_ + hw_].rearrange("(k p) t -> p k t", p=128),
                                   reads=[X1] + hTf, writes=[xre], extra=[xt.w] + list(xt.r))
                        for dt in range(8):
                            wd = wdn[nd % 3]
                            nd += 1
                            P.dma(ACT if dt % 2 == 0 else SP, wd[:], WDNB[l][dt].rearrange("p (c k) -> p c k", k=128),
                                  reads=[WDNB[l]], writes=[wd])
                            p_ = nxt()
                            P.group(PE, [mm(p_[:, 0:hw_], wd[:, ct, :], gf[ct][:, s_:s_ + hw_], start=(ct == 0), stop=(ct == NCT - 1))
                                         for ct in range(NCT)], reads=[wd] + gf, writes=[p_])
                            P.op(ACT, lambda e, o=sqf[dt][:, 0:hw_], i=p_[:, 0:hw_]: e.square(o, i), reads=[p_], writes=[sqf[dt]])
                            P.op(DVE, cp(fsb_v[dt][:, 0:hw_], p_[:, 0:hw_]), reads=[p_], writes=[fsb_v[dt]],
                                 extra=[xt.w] + list(xt.r))
                        pn = nxt()
                        P.group(PE, [mm(pn[:, 0:hw_], ones_b[:], sqf[dt][:, 0:hw_], start=(dt == 0), stop=(dt == 7))
                                     for dt in range(8)], reads=sqf + [ones_b], writes=[pn])
                        P.op(ACT, act(rstd[:, 0:hw_], pn[:, 0:hw_], AF.Ln, bias=eps_t[:, 0:1], scale=1.0 / D),
                             reads=[pn, eps_t], writes=[rstd])
                        P.op(ACT, act(rstd[:, 0:hw_], rstd[:, 0:hw_], AF.Exp, scale=-0.5), reads=[rstd], writes=[rstd])
                        for dt in range(8):
                            tm = tmpf[dt % 3]
                            P.op(DVE, stt(tm[:, 0:hw_], fsb_v[dt][:, 0:hw_], G5[:, dt:dt + 1], rstd[:, 0:hw_], ALU.mult, ALU.mult),
                                 reads=[fsb_v[dt], mv, rstd], writes=[tm])
                            P.op(DVE, tt(xre[:, dt, :], xre[:, dt, :], tm[:, 0:hw_], ALU.add), reads=[tm, xre], writes=[xre])
                        if X2 is YT:
                            dst = X2[:, t0 - LC + s_:t0 - LC + s_ + hw_]
                        else:
                            dst = X2[:, t0 + s_:t0 + s_ + hw_]
                        tk = P.dma(SP, dst.rearrange("(k p) t -> p k t", p=128), xre[:], reads=[xre], writes=[X2])
                        done.append(tk)
                        if X2 is YT:
                            out_toks.append(tk)
                        done += [f.w for f in fsb_v if f.w is not None]
                        for f in fsb_v:
                            done += list(f.r)
                    prev_done = [d for d in done if d is not None]
                P.barrier()
                flush()
            if stop_after == ("F", l):
                break

        P.maxops = None
        P.barrier()
        for (name, ap, o, shape) in dbg_list:
            rows = shape[0]
            for r0 in range(0, rows, 128):
                r1 = min(rows, r0 + 128)
                P.dma(POOL, o[r0:r1], ap[r0:r1], max_dma_last_dim=2048)
        P.barrier()
        flush()
    return nc


def make_inputs(inp):
    f32 = np.float32
    g = lambda n: np.asarray(inp[n], f32)
    x, c, ctx, c_ctx = g("x"), g("c"), g("ctx"), g("c_ctx")
    B = x.shape[0]
    w_in = g("w_in")
    w_inx = np.ascontiguousarray(w_in[:, :, EXT_COLS])
    w_uq = g("mla_w_uq")
    cols = []
    for h in range(4):
        a = list(range(h * 96, (h + 1) * 96))
        b = a[:64] + rope_perm_cols(a[64:], 32)
        cols += a + b
    w_uqx = np.ascontiguousarray(w_uq[:, :, cols])
    w_ukv = g("mla_w_ukv")
    kc, vc = [], []
    for h in range(4):
        kc += list(range(h * 128, h * 128 + 64))
        vc += list(range(h * 128 + 64, h * 128 + 128))
    w_ukvx = np.ascontiguousarray(w_ukv[:, :, kc + vc])

    rep = lambda v: np.broadcast_to(v[None, :], (128, v.shape[0]))
    p64 = np.arange(128) % 64
    common = np.zeros((128, NV), f32)
    for l in range(DEPTH):
        def put(name, arr):
            o = VOFF[(name, l)]
            arr = np.asarray(arr, f32)
            if arr.ndim == 1:
                arr = arr[:, None]
            common[:, o:o + arr.shape[1]] = arr
        put("gpre", fm(g("g_mix_pre")[l]))
        put("gpost", fm(g("g_mix_post")[l]))
        put("gfpre", fm(g("g_ffn_pre")[l]))
        put("gfpost", fm(g("g_ffn_post")[l]))
        put("bmod", fm(g("b_mod")[l]))
        put("lq1", rep(g("diff_lam_q1")[l]))
        put("lk1", rep(g("diff_lam_k1")[l]))
        put("lq2", rep(g("diff_lam_q2")[l]))
        put("lk2", rep(g("diff_lam_k2")[l]))
        put("gsub", g("diff_g_sub")[l][p64])
        put("sink", rep(g("swa_sink")[l]))
        mgq = np.zeros((128, 2), f32)
        mgq[:, 0] = g("mla_g_q")[l][:128]
        mgq[:64, 1] = g("mla_g_q")[l][128:192]
        put("mgq", mgq)
        put("mgkv", g("mla_g_kv")[l])
        put("ggq", g("gqa_g_q")[l][p64])
        put("ggqp", g("gqa_g_q")[l][p64 ^ 16])
        put("ggk", g("gqa_g_k")[l][p64])
        put("ggkp", g("gqa_g_k")[l][p64 ^ 16])
        cw = g("ffn_conv_w")[l]
        put("cw", np.concatenate([fm(cw[j]) for j in range(3)], axis=1))
        put("cb", fm(g("ffn_conv_b")[l]))
    rope = rope_tables()
    masks = swa_masks()
    shared = {"rope": rope, "masks": masks, "w_mod": g("w_mod"), "w_inx": w_inx, "w_uqx": w_uqx,
              "w_ukvx": w_ukvx, "w_out": g("w_out"), "w_up": g("ffn_w_up"), "w_dn": g("ffn_w_down")}
    maps = []
    for b in range(B):
        v = common.copy()
        cin = np.zeros((128, 8, 2), f32)
        cin[:, :, 0] = fm(c[b])
        cin[:, :, 1] = fm(c_ctx)
        v[:, VOFF["cin"]:VOFF["cin"] + 16] = cin.reshape(128, 16)
        xin = np.ascontiguousarray(np.concatenate([ctx[b].T, x[b].T], axis=1))
        m = dict(shared)
        m["xin"] = xin
        m["vecs"] = v
        maps.append(m)
    return maps


_NC_CACHE = {}


def kernel(**inputs):
    maps = make_inputs(inputs)
    if "nc" not in _NC_CACHE:
        _NC_CACHE["nc"] = build()
    nc = _NC_CACHE["nc"]
    res = run_bass_kernel_spmd(nc, maps, core_ids=list(range(len(maps))))
    out = np.stack([np.ascontiguousarray(r["yT"].T) for r in res.results], axis=0)
    return out.astype(np.float32)
```

```python
import math
from contextlib import ExitStack

import numpy as np
import ml_dtypes

import concourse.bass as bass
import concourse.mybir as mybir
from concourse.bass_utils import run_bass_kernel_spmd

F32 = mybir.dt.float32
BF16 = mybir.dt.bfloat16
ALU = mybir.AluOpType
AF = mybir.ActivationFunctionType

PE, ACT, DVE, POOL, SP = "pe", "act", "dve", "pool", "sp"

D = 1024
S = 4096
LC = 256
TT = S + LC
NKT = TT // 128
DEPTH = 2
EPS = 1e-6
FFN = 2816
NCT = FFN // 128
NEXT = 3072 + 512


class T:
    def __init__(self, ap, name="", psum=False):
        self.ap = ap
        self.name = name
        self.psum = psum
        self.w = None
        self.r = []

    def __getitem__(self, idx):
        return self.ap[idx]


class Prog:
    def __init__(self, nc, n_dma_sems=8):
        self.nc = nc
        self.engs = (PE, ACT, DVE, POOL, SP)
        self.ops = {e: [] for e in self.engs}
        self.cnt = {("e", e): 0 for e in (PE, ACT, DVE, POOL)}
        self.seen = {e: {} for e in self.engs}
        self.n_dma_sems = n_dma_sems
        self.dma_rr = {e: 0 for e in self.engs}
        self.dma_last = {}
        self.nops = 0
        self.maxops = None

    def all_keys(self):
        keys = [("e", e) for e in (PE, ACT, DVE, POOL)]
        for q in (SP, POOL, ACT):
            for i in range(self.n_dma_sems):
                keys.append(("d", q, i))
        return keys

    def _waits_for(self, eng, reads, writes, extra=()):
        need = {}

        def add(tok):
            if tok is None:
                return
            k, v = tok
            if need.get(k, 0) < v:
                need[k] = v

        for t in reads:
            add(t.w)
            if t.psum:
                for x in t.r:
                    if x is not None and x[0] != ("e", eng):
                        add(x)
        for t in writes:
            add(t.w)
            for x in t.r:
                add(x)
        for x in extra:
            add(x)
        out = []
        for k, v in need.items():
            if eng == PE and k == ("e", PE):
                continue
            if self.seen[eng].get(k, 0) < v:
                self.seen[eng][k] = v
                out.append((k, v))
        return out

    def _reg(self, tok, reads, writes):
        for t in reads:
            t.r.append(tok)
            if len(t.r) > 64:
                best = {}
                for (k, v) in t.r:
                    if best.get(k, 0) < v:
                        best[k] = v
                t.r = list(best.items())
        for t in writes:
            t.w = tok
            t.r = []

    def op(self, eng, fn, reads=(), writes=(), extra=()):
        return self.group(eng, [fn], reads, writes, extra)

    def group(self, eng, fns, reads=(), writes=(), extra=()):
        if self.maxops is not None and self.nops >= self.maxops:
            return None
        waits = self._waits_for(eng, reads, writes, extra)
        k = ("e", eng)
        self.cnt[k] += 1
        tok = (k, self.cnt[k])
        n = len(fns)
        for i, fn in enumerate(fns):
            self.ops[eng].append((fn, waits if i == 0 else [], (k, 1) if i == n - 1 else None))
        self.nops += n
        self._reg(tok, reads, writes)
        return tok

    def dma(self, eng, out_ap, in_ap, reads=(), writes=(), extra=(), **kw):
        if self.maxops is not None and self.nops >= self.maxops:
            return None
        i = self.dma_rr[eng]
        self.dma_rr[eng] = (i + 1) % self.n_dma_sems
        k = ("d", eng, i)
        waits = self._waits_for(eng, reads, writes, extra)
        prev = self.dma_last.get(k, 0)
        if prev and self.seen[eng].get(k, 0) < prev:
            self.seen[eng][k] = prev
            waits.append((k, prev))
        val = prev + 16
        self.dma_last[k] = val
        tok = (k, val)

        def fn(e, out_ap=out_ap, in_ap=in_ap, kw=kw):
            return e.dma_start(out=out_ap, in_=in_ap, **kw)

        self.ops[eng].append((fn, waits, (k, 16)))
        self.nops += 1
        self._reg(tok, reads, writes)
        return tok

    def barrier(self):
        toks = [(k, v) for k, v in self.cnt.items() if v > 0]
        toks += [(k, v) for k, v in self.dma_last.items() if v > 0]
        for eng in self.engs:
            waits = []
            for (k, v) in toks:
                if self.seen[eng].get(k, 0) < v:
                    self.seen[eng][k] = v
                    waits.append((k, v))
            if waits:
                self.ops[eng].append((None, waits, None))

    def emit(self, block, sems):
        def run(engname):
            def body(e):
                for (fn, waits, inc) in self.ops[engname]:
                    for (k, v) in waits:
                        e.wait_ge(sems[k], v)
                    if fn is None:
                        continue
                    ins = fn(e)
                    if inc is not None:
                        ins.then_inc(sems[inc[0]], inc[1])
            return body

        block.tensor(run(PE))
        block.scalar(run(ACT))
        block.vector(run(DVE))
        block.gpsimd(run(POOL))
        block.sync(run(SP))
        self.ops = {e: [] for e in self.engs}


def mm(out, lhsT, rhs, start=True, stop=True, **kw):
    return lambda e: e.matmul(out, lhsT, rhs, start=start, stop=stop, **kw)


def act(out, in_, func, bias=None, scale=1.0):
    if bias is None:
        return lambda e: e.activation(out, in_, func, scale=scale)
    return lambda e: e.activation(out, in_, func, bias=bias, scale=scale)


def tt(out, a, b, op):
    return lambda e: e.tensor_tensor(out=out, in0=a, in1=b, op=op)


def ts2(out, a, s1, s2, op0, op1):
    return lambda e: e.tensor_scalar(out, a, s1, s2, op0, op1)


def ts1(out, a, s1, op0):
    return lambda e: e.tensor_scalar(out, a, s1, None, op0)


def stt(out, a, s, b, op0, op1):
    return lambda e: e.scalar_tensor_tensor(out=out, in0=a, scalar=s, in1=b, op0=op0, op1=op1)


def cp(out, a):
    return lambda e: e.tensor_copy(out, a)


def acp(out, a):
    return lambda e: e.copy(out, a)


def memset(out, v):
    return lambda e: e.memset(out, v)


VEC_L = [("gpre", 8), ("gpost", 8), ("gfpre", 8), ("gfpost", 8), ("bmod", 48),
         ("lq1", 32), ("lk1", 32), ("lq2", 32), ("lk2", 32), ("gsub", 1), ("sink", 4),
         ("mgq", 2), ("mgkv", 1), ("ggq", 1), ("ggqp", 1), ("ggk", 1), ("ggkp", 1),
         ("cw", 132), ("cb", 44)]
VEC_G = [("cin", 16)]


def vec_offsets():
    off = {}
    o = 0
    for (n, c) in VEC_G:
        off[n] = o
        o += c
    for l in range(DEPTH):
        for (n, c) in VEC_L:
            off[(n, l)] = o
            o += c
    return off, o


VOFF, NV = vec_offsets()


def fm(v):
    return np.ascontiguousarray(v.reshape(-1, 128).T)


def rope_perm_cols(cols, d):
    n = d // 4
    cols = list(cols)
    out = []
    for i, c in enumerate(cols):
        base = (i // d) * d
        out.append(cols[base + ((i - base) ^ n)])
    return out


def w_in_ext_cols():
    tiles = []
    r = lambda a, b: list(range(a, b))
    dq0, dq1 = r(0, 128), r(128, 256)
    dk0, dk1 = r(256, 384), r(384, 512)
    o1 = 768
    sqh = [r(o1 + h * 64, o1 + (h + 1) * 64) for h in range(4)]
    sq0, sq1 = sqh[0] + sqh[2], sqh[1] + sqh[3]
    sk = r(o1 + 256, o1 + 384)
    o2 = 1280
    mcq0, mcq1 = r(o2, o2 + 128), r(o2 + 128, o2 + 192)
    mckv = r(o2 + 192, o2 + 320)
    kr = r(o2 + 320, o2 + 352)
    mkr = r(o2 + 192, o2 + 256) + kr
    mkrp = r(o2 + 192, o2 + 256) + rope_perm_cols(kr, 32)
    o3 = 1632
    gqh = [r(o3 + h * 64, o3 + (h + 1) * 64) for h in range(4)]
    gq0, gq1 = gqh[0] + gqh[2], gqh[1] + gqh[3]
    gk = r(o3 + 256, o3 + 384)
    P32 = lambda c: rope_perm_cols(c, 32)
    P64 = lambda c: rope_perm_cols(c, 64)
    tiles = [dq0, dq1, P32(dq0), P32(dq1), dk0, dk1, P32(dk0), P32(dk1),
             sq0, sq1, P64(sq0), P64(sq1), sk, P64(sk),
             mcq0, mcq1, mckv, mkr, mkrp,
             gq0, gq1, P64(gq0), P64(gq1), gk, P64(gk)]
    wv = r(512, 768) + r(o1 + 384, o1 + 512) + r(o3 + 384, o3 + 512)
    offs = []
    cols = []
    for t in tiles:
        offs.append((len(cols), len(t)))
        cols += t
    assert len(cols) == 3072
    voff = len(cols)
    cols += wv
    assert len(cols) == NEXT
    return cols, offs, voff


EXT_COLS, EXT_OFFS, EXT_VOFF = w_in_ext_cols()
(T_DQ0, T_DQ1, T_DQ0P, T_DQ1P, T_DK0, T_DK1, T_DK0P, T_DK1P, T_SQ0, T_SQ1, T_SQ0P, T_SQ1P, T_SK, T_SKP,
 T_MCQ0, T_MCQ1, T_MCKV, T_MKR, T_MKRP, T_GQ0, T_GQ1, T_GQ0P, T_GQ1P, T_GK, T_GKP) = range(25)

(Q_DQ0, Q_DQ1, Q_DK0, Q_DK1, Q_SQ0, Q_SQ1, Q_SK, Q_GQ0, Q_GQ1, Q_GK) = range(10)
Q_MQ = 10
Q_MK = 14
NQK = 18


def rope_tables():
    out = np.zeros((4, 128, TT), np.float32)
    s = np.arange(S)
    row = (s // 64).astype(np.float32)
    col = (s % 64).astype(np.float32)
    for fi, d in ((0, 32), (2, 64)):
        n = d // 4
        inv = (np.float32(10000.0) ** (-np.arange(n, dtype=np.float32) / np.float32(n))).astype(np.float32)
        for p in range(128):
            idx = p % d
            axis = idx // (2 * n)
            half = (idx // n) % 2
            i = idx % n
            pos = row if axis == 0 else col
            ang = (pos * inv[i]).astype(np.float32)
            out[fi, p, :LC] = 1.0
            out[fi, p, LC:] = np.cos(ang)
            out[fi + 1, p, :LC] = 0.0
            out[fi + 1, p, LC:] = np.sin(ang) * (-1.0 if half == 0 else 1.0)
    return out


def swa_masks():
    m = np.zeros((6, 128, 512), np.float32)
    kl = np.arange(128)[:, None]
    ql = np.arange(512)[None, :]
    for ri, r in enumerate(range(-1, 5)):
        m[ri] = (np.abs(ql - kl - 128 * r) <= 128).astype(np.float32)
    return m.astype(ml_dtypes.bfloat16)


def build(debug=False, stop_after=None, maxops=None):
    nc = bass.Bass("TRN2", target_bir_lowering=False)
    dk = "ExternalOutput" if debug else "Internal"

    def din(name, shape, dt=F32):
        return nc.dram_tensor(name, shape, dt, kind="ExternalInput").ap()

    dbg_list = []

    def dscr(name, shape, dt):
        ap = nc.dram_tensor(name, shape, dt, kind="Internal").ap()
        if debug and name in debug:
            o = nc.dram_tensor("D_" + name, shape, F32, kind="ExternalOutput").ap()
            dbg_list.append((name, ap, o, shape))
        return ap

    xin = din("xin", [D, TT])
    vecs_d = din("vecs", [128, NV])
    rope_d = din("rope", [4, 128, TT])
    mask_d = din("masks", [6, 128, 512], BF16)
    wmod_d = din("w_mod", [DEPTH, D, 6 * D])
    winx_d = din("w_inx", [DEPTH, D, NEXT])
    wuqx_d = din("w_uqx", [DEPTH, 192, 768])
    wukvx_d = din("w_ukvx", [DEPTH, 128, 512])
    wout_d = din("w_out", [DEPTH, D, D])
    wup_d = din("w_up", [DEPTH, D, 2 * FFN])
    wdn_d = din("w_dn", [DEPTH, FFN, D])
    yT = nc.dram_tensor("yT", [D, S], F32, kind="ExternalOutput").ap()

    XA = T(dscr("XA", [D, TT], F32), "XA")
    XB = T(dscr("XB", [D, TT], F32), "XB")
    XIN = T(xin, "xin")
    YT = T(yT, "yT")
    QK = [T(dscr("QK%d" % i, [128, TT], BF16), "QK%d" % i) for i in range(NQK)]
    VD = T(dscr("VD", [TT, 260], BF16), "VD")
    VM = T(dscr("VM", [TT, 260], BF16), "VM")
    VS = T(dscr("VS", [TT, 130], BF16), "VS")
    VG = T(dscr("VG", [TT, 130], BF16), "VG")
    MIX = T(dscr("MIX", [D, TT], BF16), "MIX")
    WUPB = [T(dscr("WUPB%d" % l, [NCT, 128, 8 * 256], BF16), "WUPB") for l in range(DEPTH)]
    WDNB = [T(dscr("WDNB%d" % l, [8, 128, NCT * 128], BF16), "WDNB") for l in range(DEPTH)]

    P = Prog(nc)
    P.maxops = maxops
    out_toks = []
    uid = [0]

    def sbuf_t(name, shape, dt):
        uid[0] += 1
        return nc.sbuf_tensor("%s_u%d" % (name, uid[0]), shape, dt)

    def psum_t(name, shape, dt):
        uid[0] += 1
        return nc.psum_tensor("%s_u%d" % (name, uid[0]), shape, dt)

    with ExitStack() as es0:
        sems = {k: es0.enter_context(nc.semaphore("s%d" % i)) for i, k in enumerate(P.all_keys())}

        def flush():
            with nc.Block() as block:
                P.emit(block, sems)

        def sb0(name, shape, dt):
            return T(es0.enter_context(sbuf_t(name, shape, dt))[:], name)

        vecs = sb0("vecs", [128, NV], F32)
        mv = sb0("mv", [128, DEPTH * 2 * 4 * 8], F32)
        modsb = sb0("modsb", [128, DEPTH * 96], F32)
        ones_b = sb0("ones_b", [128, 128], BF16)
        bd64 = sb0("bd64", [128, 128], BF16)
        sel = sb0("sel", [128, 64], F32)
        eps_t = sb0("eps_t", [128, 1], F32)
        small = sb0("small", [128, 16 * DEPTH], F32)

        def MV(l, who, kind):
            o = ((l * 2 + who) * 4 + kind) * 8
            return mv[:, o:o + 8]

        def MOD(l, j, who):
            base = l * 96
            return modsb[:, base:base + 96].rearrange("p (i w) -> p i w", w=2)[:, j * 8:(j + 1) * 8, who]

        def V(name, l=None, n=None):
            o = VOFF[name] if l is None else VOFF[(name, l)]
            w = n if n is not None else dict(VEC_L + VEC_G)[name]
            return vecs[:, o:o + w]

        with ExitStack() as es:
            def sb(name, shape, dt):
                return T(es.enter_context(sbuf_t(name, shape, dt))[:], name)

            def ps(name, shape, dt=F32):
                return T(es.enter_context(psum_t(name, shape, dt))[:], name, psum=True)

            P.dma(SP, vecs[:], vecs_d, writes=[vecs])
            P.op(DVE, memset(ones_b[:], 1.0), writes=[ones_b])
            P.op(DVE, memset(bd64[:], 0.0), writes=[bd64])
            P.op(DVE, memset(bd64[0:64, 0:64], 1.0), writes=[bd64])
            P.op(DVE, memset(bd64[64:128, 64:128], 1.0), writes=[bd64])
            P.op(DVE, memset(sel[:], 0.0), writes=[sel])
            P.op(DVE, memset(sel[64:65, :], 1.0), writes=[sel])
            P.op(DVE, memset(eps_t[:], EPS), writes=[eps_t])

            sil = sb("sil", [128, 16], F32)
            P.op(ACT, act(sil[:], V("cin"), AF.Silu), reads=[vecs], writes=[sil])
            wmt = [sb("wmt%d" % i, [128, 8, 512], F32) for i in range(2)]
            pm = ps("pm", [128, 96])
            for l in range(DEPTH):
                for cb in range(12):
                    w = wmt[(l * 12 + cb) % 2]
                    P.dma(SP if cb % 2 == 0 else ACT, w[:],
                          wmod_d[l, :, cb * 512:(cb + 1) * 512].rearrange("(k p) c -> p k c", p=128), writes=[w])
                    fns = []
                    for sub in range(4):
                        idx = cb * 4 + sub
                        for kt in range(8):
                            fns.append(mm(pm[:, idx * 2:idx * 2 + 2], w[:, kt, sub * 128:(sub + 1) * 128],
                                          sil[:, kt * 2:kt * 2 + 2], start=(kt == 0), stop=(kt == 7)))
                    P.group(PE, fns, reads=[w, sil], writes=[pm])
                mo = modsb[:, l * 96:(l + 1) * 96].rearrange("p (i w) -> p i w", w=2)
                pmv = pm[:, :].rearrange("p (i w) -> p i w", w=2)
                for who in range(2):
                    P.op(DVE, tt(mo[:, :, who], pmv[:, :, who], V("bmod", l), ALU.add),
                         reads=[pm, vecs], writes=[modsb])
                for who in range(2):
                    P.op(DVE, stt(MV(l, who, 0), MOD(l, 1, who), 1.0, V("gpre", l), ALU.add, ALU.mult),
                         reads=[modsb, vecs], writes=[mv])
                    P.op(DVE, tt(MV(l, who, 1), MOD(l, 2, who), V("gpost", l), ALU.mult),
                         reads=[modsb, vecs], writes=[mv])
                    P.op(DVE, stt(MV(l, who, 2), MOD(l, 4, who), 1.0, V("gfpre", l), ALU.add, ALU.mult),
                         reads=[modsb, vecs], writes=[mv])
                    P.op(DVE, tt(MV(l, who, 3), MOD(l, 5, who), V("gfpost", l), ALU.mult),
                         reads=[modsb, vecs], writes=[mv])
                lam_init = 0.8 - 0.6 * math.exp(-0.3 * l)
                so = l * 16
                tmp32 = sb("tmp32_%d" % l, [128, 32], F32)
                acc = sb("acc_%d" % l, [128, 4], F32)
                P.op(DVE, tt(tmp32[:], V("lq1", l), V("lk1", l), ALU.mult), reads=[vecs], writes=[tmp32])
                P.op(DVE, lambda e, a=acc, t=tmp32: e.reduce_sum(a[:, 0:1], t[:], mybir.AxisListType.X),
                     reads=[tmp32], writes=[acc])
                P.op(DVE, tt(tmp32[:], V("lq2", l), V("lk2", l), ALU.mult), reads=[vecs, acc], writes=[tmp32])
                P.op(DVE, lambda e, a=acc, t=tmp32: e.reduce_sum(a[:, 1:2], t[:], mybir.AxisListType.X),
                     reads=[tmp32], writes=[acc])
                P.op(ACT, act(acc[:, 2:4], acc[:, 0:2], AF.Exp), reads=[acc], writes=[acc])
                P.op(DVE, stt(small[:, so:so + 1], acc[:, 3:4], -lam_init, acc[:, 2:3], ALU.add, ALU.subtract),
                     reads=[acc], writes=[small])
                P.op(DVE, ts1(small[:, so + 1:so + 2], V("gsub", l), 1.0 - lam_init, ALU.mult),
                     reads=[vecs], writes=[small])
                P.op(ACT, act(small[:, so + 2:so + 6], V("sink", l), AF.Exp), reads=[vecs], writes=[small])

            stg = [sb("stg%d" % i, [128, 8, 256], BF16) for i in range(6)]
            n = 0
            for l in range(DEPTH):
                for ct in range(NCT):
                    s_ = stg[n % 6]
                    n += 1
                    for part in range(2):
                        c0 = part * FFN + ct * 128
                        P.dma(POOL, s_[:, :, part * 128:(part + 1) * 128],
                              wup_d[l, :, c0:c0 + 128].rearrange("(k p) c -> p k c", p=128), writes=[s_])
                    P.dma(SP, WUPB[l][ct].rearrange("p (k c) -> p k c", c=256), s_[:], reads=[s_], writes=[WUPB[l]])
            stg2 = [sb("stgd%d" % i, [128, 1024], BF16) for i in range(6)]
            for l in range(DEPTH):
                for ct in range(NCT):
                    s_ = stg2[n % 6]
                    n += 1
                    P.dma(POOL, s_[:], wdn_d[l, ct * 128:(ct + 1) * 128, :], writes=[s_])
                    P.dma(SP, WDNB[l][:, :, ct * 128:(ct + 1) * 128].rearrange("d p c -> p d c"),
                          s_[:].rearrange("p (d c) -> p d c", c=128), reads=[s_], writes=[WDNB[l]])
            P.barrier()
            flush()

        X_seq = [(XIN, XA, XB), (XB, XA, YT)]
        print("ops after phase0:", P.nops)

        for l in range(DEPTH):
            if stop_after == ("0", 0):
                break
            X0, X1, X2 = X_seq[l]
            need_ctx = l < DEPTH - 1
            with ExitStack() as es:
                def sb(name, shape, dt):
                    return T(es.enter_context(sbuf_t(name, shape, dt))[:], name)

                def ps(name, shape, dt=F32):
                    return T(es.enter_context(psum_t(name, shape, dt))[:], name, psum=True)

                wext_h = es.enter_context(sbuf_t("wext", [128, 8, NEXT], BF16))
                wext = [T(wext_h[:, kt, :], "wext%d" % kt) for kt in range(8)]
                for kt in range(8):
                    for hh in range(2):
                        c0, c1 = hh * (NEXT // 2), (hh + 1) * (NEXT // 2)
                        P.dma(POOL, wext[kt][:, c0:c1], winx_d[l, kt * 128:(kt + 1) * 128, c0:c1], writes=[wext[kt]],
                              max_dma_last_dim=4096)
                wuq0 = sb("wuq0", [128, 768], BF16)
                wuq1 = sb("wuq1", [64, 768], BF16)
                wukv = sb("wukv", [128, 512], BF16)
                P.dma(POOL, wuq0[:], wuqx_d[l, 0:128, :], writes=[wuq0])
                P.dma(POOL, wuq1[:], wuqx_d[l, 128:192, :], writes=[wuq1])
                P.dma(POOL, wukv[:], wukvx_d[l], writes=[wukv])

                xts = [sb("xt%d" % i, [128, 8, 512], F32) for i in range(2)]
                tabs = [sb("tab%d" % i, [128, 4, 512], F32) for i in range(2)]
                sq_h = es.enter_context(sbuf_t("sq", [128, 8, 512], BF16))
                sq = [T(sq_h[:, kt, :], "sq%d" % kt) for kt in range(8)]
                hT_h = [es.enter_context(sbuf_t("hT%d" % i, [128, 8, 512], BF16)) for i in range(2)]
                hTs = [[T(h[:, kt, :], "hT") for kt in range(8)] for h in hT_h]
                rstd = sb("rstd", [128, 512], F32)
                tmpf = [sb("tmpf%d" % i, [128, 512], F32) for i in range(4)]
                ost = [sb("ost%d" % i, [128, 512], BF16) for i in range(6)]
                sqs = [sb("sqs%d" % i, [128, 512], BF16) for i in range(2)]
                nrm = [sb("nrm%d" % i, [128, 512], F32) for i in range(2)]
                cqn0 = sb("cqn0", [128, 512], BF16)
                cqn1 = sb("cqn1", [64, 512], BF16)
                ckvn = sb("ckvn", [128, 512], BF16)
                krr = sb("krr", [128, 512], BF16)
                vst = [[sb("vst%d_%d" % (i, j), [128, 4, 4 * 65 if j < 2 else 2 * 65], BF16) for j in range(4)]
                       for i in range(2)]
                for i in range(2):
                    for j in range(4):
                        P.op(POOL, memset(vst[i][j][:], 1.0), writes=[vst[i][j]])
                pp = [ps("pp%d" % i, [128, 512]) for i in range(8)]
                ppi = [0]

                def nxt():
                    ppi[0] = (ppi[0] + 1) % 8
                    return pp[ppi[0]]

                tfi = [0]

                def ntmp():
                    tfi[0] = (tfi[0] + 1) % 4
                    return tmpf[tfi[0]]

                osi = [0]

                def nost():
                    osi[0] = (osi[0] + 1) % 6
                    return ost[osi[0]]

                blocks = [(0, LC, 1)] + [(LC + i * 512, 512, 0) for i in range(8)]
                for bi, (t0, tw, who) in enumerate(blocks):
                    xt = xts[bi % 2]
                    tab = tabs[bi % 2]
                    hT = hTs[bi % 2]
                    P.dma(SP, xt[:, :, 0:tw], X0[:, t0:t0 + tw].rearrange("(k p) t -> p k t", p=128),
                          reads=[X0], writes=[xt])
                    P.dma(SP, tab[:, :, 0:tw], rope_d[:, :, t0:t0 + tw].rearrange("f p t -> p f t"), writes=[tab])
                    for kt in range(8):
                        P.op(ACT, lambda e, o=sq[kt][:, 0:tw], i=xt[:, kt, 0:tw]: e.square(o, i),
                             reads=[xt], writes=[sq[kt]])
                    pn = nxt()
                    P.group(PE, [mm(pn[:, 0:tw], ones_b[:], sq[kt][:, 0:tw], start=(kt == 0), stop=(kt == 7))
                                 for kt in range(8)], reads=sq + [ones_b], writes=[pn])
                    P.op(ACT, act(rstd[:, 0:tw], pn[:, 0:tw], AF.Ln, bias=eps_t[:, 0:1], scale=1.0 / D),
                         reads=[pn, eps_t], writes=[rstd])
                    P.op(ACT, act(rstd[:, 0:tw], rstd[:, 0:tw], AF.Exp, scale=-0.5), reads=[rstd], writes=[rstd])
                    A1 = MV(l, who, 0)
                    B1 = MOD(l, 0, who)
                    for kt in range(8):
                        tm = ntmp()
                        P.op(DVE, stt(tm[:, 0:tw], xt[:, kt, 0:tw], A1[:, kt:kt + 1], rstd[:, 0:tw], ALU.mult, ALU.mult),
                             reads=[xt, mv, rstd], writes=[tm])
                        P.op(ACT, act(hT[kt][:, 0:tw], tm[:, 0:tw], AF.Identity, bias=B1[:, kt:kt + 1]),
                             reads=[tm, modsb], writes=[hT[kt]])

                    def proj(ti, rows=None):
                        c0, cw = EXT_OFFS[ti]
                        p_ = nxt()
                        P.group(PE, [mm(p_[0:cw, 0:tw], wext[kt][:, c0:c0 + cw], hT[kt][:, 0:tw],
                                        start=(kt == 0), stop=(kt == 7)) for kt in range(8)],
                                reads=wext + hT, writes=[p_])
                        return p_

                    def store_qk(qi, src, rows=128):
                        P.dma(SP, QK[qi][0:rows, t0:t0 + tw], src[0:rows, 0:tw], reads=[src], writes=[QK[qi]])

                    def rope(pq, ppm, fi, dst, r0=0, r1=128, eng2=DVE):
                        a = ntmp()
                        P.op(DVE, tt(a[r0:r1, 0:tw], pq[r0:r1, 0:tw], tab[r0:r1, fi, 0:tw], ALU.mult),
                             reads=[pq, tab], writes=[a])
                        b = ntmp()
                        P.op(DVE, tt(b[r0:r1, 0:tw], ppm[r0:r1, 0:tw], tab[r0:r1, fi + 1, 0:tw], ALU.mult),
                             reads=[ppm, tab], writes=[b])
                        P.op(eng2, tt(dst[r0:r1, 0:tw], a[r0:r1, 0:tw], b[r0:r1, 0:tw], ALU.add),
                             reads=[a, b], writes=[dst])

                    for (ta, tp, fi, qi) in ((T_DQ0, T_DQ0P, 0, Q_DQ0), (T_DQ1, T_DQ1P, 0, Q_DQ1),
                                             (T_DK0, T_DK0P, 0, Q_DK0), (T_DK1, T_DK1P, 0, Q_DK1),
                                             (T_SQ0, T_SQ0P, 2, Q_SQ0), (T_SQ1, T_SQ1P, 2, Q_SQ1),
                                             (T_SK, T_SKP, 2, Q_SK)):
                        pa = proj(ta)
                        pb = proj(tp)
                        o = nost()
                        rope(pa, pb, fi, o)
                        store_qk(qi, o)
                    for (ta, tp, qi, g, gp) in ((T_GQ0, T_GQ0P, Q_GQ0, "ggq", "ggqp"), (T_GQ1, T_GQ1P, Q_GQ1, "ggq", "ggqp"),
                                                (T_GK, T_GKP, Q_GK, "ggk", "ggkp")):
                        pa = proj(ta)
                        pb = proj(tp)
                        s_ = sqs[0]
                        P.op(ACT, lambda e, o=s_[:, 0:tw], i=pa[:, 0:tw]: e.square(o, i), reads=[pa], writes=[s_])
                        pn = nxt()
                        P.op(PE, mm(pn[:, 0:tw], bd64[:], s_[:, 0:tw]), reads=[bd64, s_], writes=[pn])
                        r_ = nrm[0]
                        P.op(ACT, act(r_[:, 0:tw], pn[:, 0:tw], AF.Ln, bias=eps_t[:, 0:1], scale=1.0 / 64),
                             reads=[pn, eps_t], writes=[r_])
                        P.op(ACT, act(r_[:, 0:tw], r_[:, 0:tw], AF.Exp, scale=-0.5), reads=[r_], writes=[r_])
                        a = ntmp()
                        P.op(DVE, stt(a[:, 0:tw], pa[:, 0:tw], V(g, l), tab[:, 2, 0:tw], ALU.mult, ALU.mult),
                             reads=[pa, vecs, tab], writes=[a])
                        b = ntmp()
                        P.op(DVE, stt(b[:, 0:tw], pb[:, 0:tw], V(gp, l), tab[:, 3, 0:tw], ALU.mult, ALU.mult),
                             reads=[pb, vecs, tab], writes=[b])
                        P.op(DVE, tt(a[:, 0:tw], a[:, 0:tw], b[:, 0:tw], ALU.add), reads=[a, b], writes=[a])
                        o = nost()
                        P.op(DVE, tt(o[:, 0:tw], a[:, 0:tw], r_[:, 0:tw], ALU.mult), reads=[a, r_], writes=[o])
                        store_qk(qi, o)
                    pc0 = proj(T_MCQ0)
                    pc1 = proj(T_MCQ1)
                    P.op(ACT, lambda e, o=sqs[0][:, 0:tw], i=pc0[:, 0:tw]: e.square(o, i), reads=[pc0], writes=[sqs[0]])
                    P.op(ACT, lambda e, o=sqs[1][0:64, 0:tw], i=pc1[0:64, 0:tw]: e.square(o, i), reads=[pc1], writes=[sqs[1]])
                    pn = nxt()
                    P.group(PE, [mm(pn[:, 0:tw], ones_b[:, :], sqs[0][:, 0:tw], start=True, stop=False),
                                 mm(pn[:, 0:tw], ones_b[0:64, :], sqs[1][0:64, 0:tw], start=False, stop=True)],
                            reads=[ones_b, sqs[0], sqs[1]], writes=[pn])
                    r_ = nrm[0]
                    P.op(ACT, act(r_[:, 0:tw], pn[:, 0:tw], AF.Ln, bias=eps_t[:, 0:1], scale=1.0 / 192),
                         reads=[pn, eps_t], writes=[r_])
                    P.op(ACT, act(r_[:, 0:tw], r_[:, 0:tw], AF.Exp, scale=-0.5), reads=[r_], writes=[r_])
                    mg = V("mgq", l)
                    P.op(DVE, stt(cqn0[:, 0:tw], pc0[:, 0:tw], mg[:, 0:1], r_[:, 0:tw], ALU.mult, ALU.mult),
                         reads=[pc0, vecs, r_], writes=[cqn0])
                    P.op(DVE, stt(cqn1[0:64, 0:tw], pc1[0:64, 0:tw], mg[0:64, 1:2], r_[0:64, 0:tw], ALU.mult, ALU.mult),
                         reads=[pc1, vecs, r_], writes=[cqn1])
                    pkv = proj(T_MCKV)
                    P.op(ACT, lambda e, o=sqs[0][:, 0:tw], i=pkv[:, 0:tw]: e.square(o, i), reads=[pkv], writes=[sqs[0]])
                    pn = nxt()
                    P.op(PE, mm(pn[:, 0:tw], ones_b[:], sqs[0][:, 0:tw]), reads=[ones_b, sqs[0]], writes=[pn])
                    r2 = nrm[1]
                    P.op(ACT, act(r2[:, 0:tw], pn[:, 0:tw], AF.Ln, bias=eps_t[:, 0:1], scale=1.0 / 128),
                         reads=[pn, eps_t], writes=[r2])
                    P.op(ACT, act(r2[:, 0:tw], r2[:, 0:tw], AF.Exp, scale=-0.5), reads=[r2], writes=[r2])
                    P.op(DVE, stt(ckvn[:, 0:tw], pkv[:, 0:tw], V("mgkv", l), r2[:, 0:tw], ALU.mult, ALU.mult),
                         reads=[pkv, vecs, r2], writes=[ckvn])
                    pk = proj(T_MKR)
                    pkp = proj(T_MKRP)
                    rope(pk, pkp, 0, krr, 64, 96)
                    for h in range(4):
                        pq = nxt()
                        P.group(PE, [mm(pq[0:96, 0:tw], wuq0[:, h * 192:h * 192 + 96], cqn0[:, 0:tw], start=True, stop=False),
                                     mm(pq[0:96, 0:tw], wuq1[0:64, h * 192:h * 192 + 96], cqn1[0:64, 0:tw], start=False, stop=True)],
                                reads=[wuq0, wuq1, cqn0, cqn1], writes=[pq])
                        pqp = nxt()
                        P.group(PE, [mm(pqp[0:96, 0:tw], wuq0[:, h * 192 + 96:h * 192 + 192], cqn0[:, 0:tw], start=True, stop=False),
                                     mm(pqp[0:96, 0:tw], wuq1[0:64, h * 192 + 96:h * 192 + 192], cqn1[0:64, 0:tw], start=False, stop=True)],
                                reads=[wuq0, wuq1, cqn0, cqn1], writes=[pqp])
                        o = nost()
                        P.op(ACT, acp(o[0:64, 0:tw], pq[0:64, 0:tw]), reads=[pq], writes=[o])
                        rope(pq, pqp, 0, o, 64, 96)
                        store_qk(Q_MQ + h, o, 96)
                        pkn = nxt()
                        P.op(PE, mm(pkn[0:64, 0:tw], wukv[:, h * 64:(h + 1) * 64], ckvn[:, 0:tw]),
                             reads=[wukv, ckvn], writes=[pkn])
                        o2 = nost()
                        P.op(ACT, acp(o2[0:64, 0:tw], pkn[0:64, 0:tw]), reads=[pkn], writes=[o2])
                        P.op(POOL, cp(o2[64:96, 0:tw], krr[64:96, 0:tw]), reads=[krr], writes=[o2])
                        store_qk(Q_MK + h, o2, 96)
                    vs_ = vst[bi % 2]
                    nsub = tw // 128
                    for sub in range(nsub):
                        pv = nxt()
                        P.group(PE, [mm(pv[:, 0:512], hT[kt][:, sub * 128:(sub + 1) * 128],
                                        wext[kt][:, EXT_VOFF:EXT_VOFF + 512], start=(kt == 0), stop=(kt == 7))
                                     for kt in range(8)], reads=wext + hT, writes=[pv])
                        P.op(DVE, cp(vs_[0][:, sub, :].rearrange("p (h c) -> p h c", c=65)[:, :, 0:64],
                                     pv[:, 0:256].rearrange("p (h c) -> p h c", c=64)), reads=[pv], writes=[vs_[0]])
                        P.op(DVE, cp(vs_[2][:, sub, :].rearrange("p (h c) -> p h c", c=65)[:, :, 0:64],
                                     pv[:, 256:384].rearrange("p (h c) -> p h c", c=64)), reads=[pv], writes=[vs_[2]])
                        P.op(DVE, cp(vs_[3][:, sub, :].rearrange("p (h c) -> p h c", c=65)[:, :, 0:64],
                                     pv[:, 384:512].rearrange("p (h c) -> p h c", c=64)), reads=[pv], writes=[vs_[3]])
                        pm_ = nxt()
                        P.op(PE, mm(pm_[:, 0:256], ckvn[:, sub * 128:(sub + 1) * 128], wukv[:, 256:512]),
                             reads=[ckvn, wukv], writes=[pm_])
                        P.op(DVE, cp(vs_[1][:, sub, :].rearrange("p (h c) -> p h c", c=65)[:, :, 0:64],
                                     pm_[:, 0:256].rearrange("p (h c) -> p h c", c=64)), reads=[pm_], writes=[vs_[1]])
                    for (j, VT_) in ((0, VD), (1, VM), (2, VS), (3, VG)):
                        P.dma(SP, VT_[t0:t0 + tw, :].rearrange("(s p) c -> p s c", p=128), vs_[j][:, 0:nsub, :],
                              reads=[vs_[j]], writes=[VT_])
                P.barrier()
                flush()
                print("ops after P", l, P.nops)
            if stop_after == ("P", l):
                break

            with ExitStack() as es:
                def sb(name, shape, dt):
                    return T(es.enter_context(sbuf_t(name, shape, dt))[:], name)

                def ps(name, shape, dt=F32):
                    return T(es.enter_context(psum_t(name, shape, dt))[:], name, psum=True)

                Sps = [ps("Sps%d" % i, [128, 2, 512]) for i in range(2)]
                Ops = [ps("Ops%d" % i, [128, 512]) for i in range(2)]
                BCp = ps("BCp", [128, 512])
                SSp = BCp
                JK = ps("JK", [128, 512])
                jrhs = sb("jrhs", [128, 512], BF16)
                P.op(POOL, memset(jrhs[:], 0.0), writes=[jrhs])
                kres = [sb("kres%d" % i, [128, TT], BF16) for i in range(4)]
                vres = sb("vres", [128, NKT, 260], BF16)
                qts = [[sb("qt%d_%d" % (i, j), [128, 512], BF16) for j in range(8)] for i in range(2)]
                for i in (2, 3):
                    P.op(POOL, memset(kres[i][:], 0.0), writes=[kres[i]])
                pts = [sb("pt%d" % i, [128, 2, 512], BF16) for i in range(3)]
                osb = [sb("osb%d" % i, [128, 512], F32) for i in range(2)]
                rec = [sb("rec%d" % i, [128, 512], F32) for i in range(2)]
                a1 = sb("a1", [128, 512], F32)
                od = sb("od", [128, 512], F32)
                osq = sb("osq", [128, 512], BF16)
                rs2 = sb("rs2", [128, 512], F32)
                outb = [sb("outb%d" % i, [128, 512], BF16) for i in range(3)]
                msk = sb("msk", [128, 6, 512], BF16)
                P.dma(SP, msk[:], mask_d.rearrange("r p q -> p r q"), writes=[msk])
                for i in range(2):
                    P.op(POOL, memset(osb[i][:], 0.0), writes=[osb[i]])
                so = l * 16
                cnt = {"pt": 0, "S": 0, "O": 0, "ob": 0, "q": 0}

                qblocks = [(LC + i * 512, 512, False) for i in range(8)]
                if need_ctx:
                    qblocks = [(0, LC, True)] + qblocks

                mixers = []
                mixers.append(dict(name="diff", ktiles=[Q_DK0, Q_DK1], krows=[128, 128], V=VD, vw=260,
                                   qtiles=[Q_DQ0, Q_DQ1], qrows=[128, 128],
                                   heads=[dict(qt=j // 4, kt=j // 4, r0=(j % 4) * 32, r1=(j % 4) * 32 + 32, vh=j // 2,
                                               scale=32 ** -0.5, feat=(j // 2) * 64, diff=j % 2) for j in range(8)]))
                mixers.append(dict(name="swa", ktiles=[Q_SK], krows=[128], V=VS, vw=130,
                                   qtiles=[Q_SQ0, Q_SQ1], qrows=[128, 128],
                                   heads=[dict(qt=h % 2, kt=0, r0=(h // 2) * 64, r1=(h // 2) * 64 + 64, vh=h // 2,
                                               scale=64 ** -0.5, feat=256 + h * 64, sink=h) for h in range(4)]))
                mixers.append(dict(name="mla", ktiles=[Q_MK + h for h in range(4)], krows=[96] * 4, V=VM, vw=260,
                                   qtiles=[Q_MQ + h for h in range(4)], qrows=[96] * 4,
                                   heads=[dict(qt=h, kt=h, r0=0, r1=96, vh=h, scale=96 ** -0.5, feat=512 + h * 64)
                                          for h in range(4)]))
                mixers.append(dict(name="gqa", ktiles=[Q_GK], krows=[128], V=VG, vw=130,
                                   qtiles=[Q_GQ0, Q_GQ1], qrows=[128, 128],
                                   heads=[dict(qt=h % 2, kt=0, r0=(h // 2) * 64, r1=(h // 2) * 64 + 64, vh=h // 2,
                                               scale=64 ** -0.5, feat=768 + h * 64) for h in range(4)]))

                for mx in mixers:
                    for hi_, hd_ in enumerate(mx["heads"]):
                        hd_["hi"] = hi_
                    for i in range(2):
                        for j in range(len(mx["heads"])):
                            P.op(POOL, memset(qts[i][j][:], 0.0), writes=[qts[i][j]])
                    for i, qi in enumerate(mx["ktiles"]):
                        r = mx["krows"][i]
                        P.dma(SP, kres[i][0:r, :], QK[qi][0:r, :], reads=[QK[qi]], writes=[kres[i]])
                    vw = mx["vw"]
                    P.dma(SP, vres[:, :, 0:vw], mx["V"][:, :].rearrange("(s p) c -> p s c", p=128),
                          reads=[mx["V"]], writes=[vres])
                    jobs = []
                    for (t0, qw, isctx) in qblocks:
                        qs = qts[cnt["q"] % 2]
                        cnt["q"] += 1
                        qload = []
                        for hd_ in mx["heads"]:
                            qload.append((qs[hd_["hi"]], hd_["r0"], hd_["r1"], mx["qtiles"][hd_["qt"]]))
                        for hd in mx["heads"]:
                            if isctx:
                                kts = [(0, None), (1, None)]
                            elif mx["name"] == "swa":
                                qb = (t0 - LC) // 128
                                kts = [(0, None), (1, None)]
                                for r in range(-1, 5):
                                    kt_ = qb + r
                                    if 0 <= kt_ < 32:
                                        kts.append((2 + kt_, r + 1))
                            else:
                                kts = [(i, None) for i in range(NKT)]
                            pairs = [kts[i:i + 2] for i in range(0, len(kts), 2)]
                            for pi, pr in enumerate(pairs):
                                jobs.append(dict(t0=t0, qw=qw, hd=hd, pr=pr, first=(pi == 0), last=(pi == len(pairs) - 1),
                                                 qs=qs, qload=qload if (hd is mx["heads"][0] and pi == 0) else None,
                                                 isctx=isctx))

                    def do_S(jb):
                        if jb["qload"] is not None:
                            for (qt_, ra, rb, qi) in jb["qload"]:
                                P.dma(SP, qt_[ra:rb, 0:jb["qw"]], QK[qi][ra:rb, jb["t0"]:jb["t0"] + jb["qw"]],
                                      reads=[QK[qi]], writes=[qt_])
                        hd = jb["hd"]
                        Sp = Sps[cnt["S"] % 2]
                        cnt["S"] += 1
                        jb["Sp"] = Sp
                        qt_ = jb["qs"][hd["hi"]]
                        kr = kres[hd["kt"]]
                        fns = []
                        for i, (kt_, _) in enumerate(jb["pr"]):
                            fns.append(mm(Sp[:, i, 0:jb["qw"]], kr[:, kt_ * 128:(kt_ + 1) * 128],
                                          qt_[:, 0:jb["qw"]]))
                        P.group(PE, fns, reads=[kr, qt_], writes=[Sp])

                    def do_exp(jb):
                        hd = jb["hd"]
                        qw = jb["qw"]
                        npair = len(jb["pr"])
                        pt = pts[cnt["pt"] % 3]
                        cnt["pt"] += 1
                        jb["pt"] = pt
                        Sp = jb["Sp"]
                        if qw == 512:
                            P.op(ACT, act(pt[:, 0:npair, :].rearrange("p a b -> p (a b)"),
                                          Sp[:, 0:npair, :].rearrange("p a b -> p (a b)"), AF.Exp, scale=hd["scale"]),
                                 reads=[Sp], writes=[pt])
                        else:
                            for i in range(npair):
                                P.op(ACT, act(pt[:, i, 0:qw], Sp[:, i, 0:qw], AF.Exp, scale=hd["scale"]),
                                     reads=[Sp], writes=[pt])
                        for i, (kt_, mi) in enumerate(jb["pr"]):
                            if mi is not None:
                                P.op(DVE, tt(pt[:, i, 0:qw], pt[:, i, 0:qw], msk[:, mi, 0:qw], ALU.mult),
                                     reads=[pt, msk], writes=[pt])

                    def do_pv(jb):
                        hd = jb["hd"]
                        qw = jb["qw"]
                        npair = len(jb["pr"])
                        pt = jb["pt"]
                        if jb["first"]:
                            cnt["O"] += 1
                        Op = Ops[cnt["O"] % 2]
                        vh = hd["vh"]
                        fns = []
                        for i, (kt_, _) in enumerate(jb["pr"]):
                            fns.append(mm(Op[0:65, 0:qw], vres[:, kt_, vh * 65:(vh + 1) * 65], pt[:, i, 0:qw],
                                          start=(jb["first"] and i == 0), stop=(jb["last"] and i == npair - 1)))
                        P.group(PE, fns, reads=[vres, pt], writes=[Op])
                        if jb["last"]:
                            finalize(jb, Op)

                    deferred = []

                    def tick():
                        ready = []
                        for it in deferred:
                            it[0] -= 1
                        while deferred and deferred[0][0] <= 0:
                            ready.append(deferred.pop(0)[1])
                        for fn_ in ready:
                            fn_()

                    def finalize(jb, Op):
                        hd = jb["hd"]
                        qw = jb["qw"]
                        t0 = jb["t0"]
                        par = cnt["O"] % 2
                        ob_ = osb[par]
                        rc = rec[par]
                        P.op(DVE, cp(ob_[0:65, 0:qw], Op[0:65, 0:qw]), reads=[Op], writes=[ob_])
                        f0 = hd["feat"]

                        def stage_b():
                            P.op(PE, mm(BCp[0:64, 0:qw], sel[:, 0:64], ob_[:, 0:qw]), reads=[sel, ob_], writes=[BCp])
                            if "sink" in hd:
                                c_ = so + 2 + hd["sink"]
                                P.op(DVE, ts1(rc[0:64, 0:qw], BCp[0:64, 0:qw], small[0:64, c_:c_ + 1], ALU.add),
                                     reads=[BCp, small], writes=[rc])
                                P.op(DVE, lambda e, o=rc[0:64, 0:qw]: e.reciprocal(o, o), reads=[rc], writes=[rc])
                            else:
                                P.op(DVE, lambda e, o=rc[0:64, 0:qw], i=BCp[0:64, 0:qw]: e.reciprocal(o, i),
                                     reads=[BCp], writes=[rc])
                            if "diff" not in hd:
                                ob2 = outb[cnt["ob"] % 3]
                                cnt["ob"] += 1
                                P.op(DVE, tt(ob2[0:64, 0:qw], ob_[0:64, 0:qw], rc[0:64, 0:qw], ALU.mult),
                                     reads=[ob_, rc], writes=[ob2])
                                P.dma(SP, MIX[f0:f0 + 64, t0:t0 + qw], ob2[0:64, 0:qw], reads=[ob2], writes=[MIX])
                            elif hd["diff"] == 0:
                                P.op(DVE, tt(a1[0:64, 0:qw], ob_[0:64, 0:qw], rc[0:64, 0:qw], ALU.mult),
                                     reads=[ob_, rc], writes=[a1])
                            else:
                                P.op(DVE, stt(od[0:64, 0:qw], ob_[0:64, 0:qw], small[0:64, so:so + 1], rc[0:64, 0:qw],
                                              ALU.mult, ALU.mult), reads=[ob_, small, rc], writes=[od])
                                P.op(DVE, tt(od[0:64, 0:qw], od[0:64, 0:qw], a1[0:64, 0:qw], ALU.add),
                                     reads=[od, a1], writes=[od])
                                P.op(POOL, tt(osq[0:64, 0:qw], od[0:64, 0:qw], od[0:64, 0:qw], ALU.mult),
                                     reads=[od], writes=[osq])
                                deferred.append([2, stage_c])

                        def stage_c():
                            P.op(PE, mm(SSp[0:64, 0:qw], ones_b[0:64, 0:64], osq[0:64, 0:qw]),
                                 reads=[ones_b, osq], writes=[SSp])
                            P.op(ACT, act(rs2[0:64, 0:qw], SSp[0:64, 0:qw], AF.Ln, bias=eps_t[0:64, 0:1], scale=1.0 / 64),
                                 reads=[SSp, eps_t], writes=[rs2])
                            P.op(ACT, act(rs2[0:64, 0:qw], rs2[0:64, 0:qw], AF.Exp, scale=-0.5), reads=[rs2], writes=[rs2])
                            ob2 = outb[cnt["ob"] % 3]
                            cnt["ob"] += 1
                            P.op(DVE, stt(ob2[0:64, 0:qw], od[0:64, 0:qw], small[0:64, so + 1:so + 2], rs2[0:64, 0:qw],
                                          ALU.mult, ALU.mult), reads=[od, small, rs2], writes=[ob2])
                            P.dma(SP, MIX[f0:f0 + 64, t0:t0 + qw], ob2[0:64, 0:qw], reads=[ob2], writes=[MIX])

                        deferred.append([2, stage_b])

                    if jobs:
                        do_S(jobs[0])
                        if len(jobs) > 1:
                            do_S(jobs[1])
                        for ji, jb in enumerate(jobs):
                            do_exp(jb)
                            if ji + 2 < len(jobs):
                                do_S(jobs[ji + 2])
                            tick()
                            do_pv(jb)
                        for _ in range(4):
                            tick()
                    flush()
                P.barrier()
                flush()
            if stop_after == ("A", l):
                break

            with ExitStack() as es:
                def sb(name, shape, dt):
                    return T(es.enter_context(sbuf_t(name, shape, dt))[:], name)

                def ps(name, shape, dt=F32):
                    return T(es.enter_context(psum_t(name, shape, dt))[:], name, psum=True)

                wo_h = es.enter_context(sbuf_t("wo", [128, 8, D], BF16))
                wo = [T(wo_h[:, kt, :], "wo%d" % kt) for kt in range(8)]
                for kt in range(8):
                    P.dma(POOL, wo[kt][:], wout_d[l, kt * 128:(kt + 1) * 128, :], writes=[wo[kt]])
                xts = [sb("xo%d" % i, [128, 8, 512], F32) for i in range(2)]
                mxs = [sb("mx%d" % i, [128, 8, 512], BF16) for i in range(2)]
                ysb_h = es.enter_context(sbuf_t("ysb", [128, 8, 512], F32))
                ysb = [T(ysb_h[:, dt, :], "ysb") for dt in range(8)]
                ysq_h = es.enter_context(sbuf_t("ysq", [128, 8, 512], BF16))
                ysq = [T(ysq_h[:, dt, :], "ysq") for dt in range(8)]
                rstd = sb("rstdo", [128, 512], F32)
                tmpf = [sb("tmpo%d" % i, [128, 512], F32) for i in range(3)]
                pp = [ps("po%d" % i, [128, 512]) for i in range(6)]
                blocks = [(LC + i * 512, 512, 0) for i in range(8)]
                if need_ctx:
                    blocks = [(0, LC, 1)] + blocks
                k = 0
                for bi, (t0, tw, who) in enumerate(blocks):
                    xt = xts[bi % 2]
                    mx_ = mxs[bi % 2]
                    P.dma(SP, xt[:, :, 0:tw], X0[:, t0:t0 + tw].rearrange("(k p) t -> p k t", p=128),
                          reads=[X0], writes=[xt])
                    P.dma(SP, mx_[:, :, 0:tw], MIX[:, t0:t0 + tw].rearrange("(k p) t -> p k t", p=128),
                          reads=[MIX], writes=[mx_])
                    for dt in range(8):
                        p_ = pp[k % 5]
                        k += 1
                        P.group(PE, [mm(p_[:, 0:tw], wo[mt][:, dt * 128:(dt + 1) * 128], mx_[:, mt, 0:tw],
                                        start=(mt == 0), stop=(mt == 7)) for mt in range(8)],
                                reads=wo + [mx_], writes=[p_])
                        P.op(ACT, lambda e, o=ysq[dt][:, 0:tw], i=p_[:, 0:tw]: e.square(o, i), reads=[p_], writes=[ysq[dt]])
                        P.op(DVE, cp(ysb[dt][:, 0:tw], p_[:, 0:tw]), reads=[p_], writes=[ysb[dt]])
                    pn = pp[5]
                    P.group(PE, [mm(pn[:, 0:tw], ones_b[:], ysq[dt][:, 0:tw], start=(dt == 0), stop=(dt == 7))
                                 for dt in range(8)], reads=ysq + [ones_b], writes=[pn])
                    P.op(ACT, act(rstd[:, 0:tw], pn[:, 0:tw], AF.Ln, bias=eps_t[:, 0:1], scale=1.0 / D),
                         reads=[pn, eps_t], writes=[rstd])
                    P.op(ACT, act(rstd[:, 0:tw], rstd[:, 0:tw], AF.Exp, scale=-0.5), reads=[rstd], writes=[rstd])
                    G2 = MV(l, who, 1)
                    for dt in range(8):
                        tm = tmpf[dt % 3]
                        P.op(DVE, stt(tm[:, 0:tw], ysb[dt][:, 0:tw], G2[:, dt:dt + 1], rstd[:, 0:tw], ALU.mult, ALU.mult),
                             reads=[ysb[dt], mv, rstd], writes=[tm])
                        P.op(DVE, tt(xt[:, dt, 0:tw], xt[:, dt, 0:tw], tm[:, 0:tw], ALU.add), reads=[tm, xt], writes=[xt])
                    P.dma(SP, X1[:, t0:t0 + tw].rearrange("(k p) t -> p k t", p=128), xt[:, :, 0:tw],
                          reads=[xt], writes=[X1])
                P.barrier()
                flush()
            if stop_after == ("O", l):
                break

            with ExitStack() as es:
                def sb(name, shape, dt):
                    return T(es.enter_context(sbuf_t(name, shape, dt))[:], name)

                def ps(name, shape, dt=F32):
                    return T(es.enter_context(psum_t(name, shape, dt))[:], name, psum=True)

                CW = 1024
                xt = sb("xf", [128, 8, CW + 2], F32)
                fsb_v = [T(xt[:, dt, 0:512], "fsb") for dt in range(8)]
                xre_v = T(xt[:, :, 512:1024], "xre")
                sq_h = es.enter_context(sbuf_t("sqf", [128, 8, CW + 2], BF16))
                sqf = [T(sq_h[:, kt, :], "sqf") for kt in range(8)]
                hT_h = es.enter_context(sbuf_t("hTf", [128, 8, CW + 2], BF16))
                hTf = [T(hT_h[:, kt, :], "hTf") for kt in range(8)]
                g_h = es.enter_context(sbuf_t("gf", [128, NCT, CW], BF16))
                gf = [T(g_h[:, ct, :], "gf") for ct in range(NCT)]
                rstd = sb("rstdf", [128, CW + 2], F32)
                tmpf = [sb("tmpff%d" % i, [128, 512], F32) for i in range(3)]
                U = [[sb("U%d_%d" % (i, j), [128, CW + 2], F32) for j in range(2)] for i in range(2)]
                Y = [[sb("Y%d_%d" % (i, j), [128, CW], F32) for j in range(2)] for i in range(2)]
                wup = [sb("wup%d" % i, [128, 8, 256], BF16) for i in range(3)]
                wdn = [sb("wdn%d" % i, [128, NCT, 128], BF16) for i in range(3)]
                pp = [ps("pf%d" % i, [128, 512]) for i in range(8)]
                ppi = [0]

                def nxt():
                    ppi[0] = (ppi[0] + 1) % 8
                    return pp[ppi[0]]

                chunks = [(LC + i * CW, CW, 0) for i in range(S // CW)]
                if need_ctx:
                    chunks = [(0, LC, 1)] + chunks
                prev_done = []
                nw = 0
                nd = 0
                for (t0, cw, who) in chunks:
                    lo = 0 if who == 1 else LC
                    hi = LC if who == 1 else TT
                    hasl = t0 - 1 >= lo
                    hasr = t0 + cw < hi
                    a0 = t0 - 1 if hasl else t0
                    a1_ = t0 + cw + 1 if hasr else t0 + cw
                    c0 = 0 if hasl else 1
                    c1 = c0 + (a1_ - a0)
                    P.dma(SP, xt[:, :, c0:c1], X1[:, a0:a1_].rearrange("(k p) t -> p k t", p=128),
                          reads=[X1], writes=[xt], extra=prev_done)
                    pieces = [(s_, min(s_ + 512, c1)) for s_ in range(c0, c1, 512)]
                    xre = T(xt[:, :, 512:512 + min(512, cw)], "xre")
                    for kt in range(8):
                        if kt % 2 == 0:
                            P.op(ACT, lambda e, o=sqf[kt][:, c0:c1], i=xt[:, kt, c0:c1]: e.square(o, i),
                                 reads=[xt], writes=[sqf[kt]])
                        else:
                            P.op(POOL, tt(sqf[kt][:, c0:c1], xt[:, kt, c0:c1], xt[:, kt, c0:c1], ALU.mult),
                                 reads=[xt], writes=[sqf[kt]])
                    for (s_, e_) in pieces:
                        pn = nxt()
                        P.group(PE, [mm(pn[:, 0:e_ - s_], ones_b[:], sqf[kt][:, s_:e_], start=(kt == 0), stop=(kt == 7))
                                     for kt in range(8)], reads=sqf + [ones_b], writes=[pn])
                        P.op(ACT, act(rstd[:, s_:e_], pn[:, 0:e_ - s_], AF.Ln, bias=eps_t[:, 0:1], scale=1.0 / D),
                             reads=[pn, eps_t], writes=[rstd])
                    P.op(ACT, act(rstd[:, c0:c1], rstd[:, c0:c1], AF.Exp, scale=-0.5), reads=[rstd], writes=[rstd])
                    A4 = MV(l, who, 2)
                    B4 = MOD(l, 3, who)
                    for kt in range(8):
                        for (s_, e_) in pieces:
                            tm = tmpf[(kt + s_ // 512) % 3]
                            P.op(DVE, stt(tm[:, 0:e_ - s_], xt[:, kt, s_:e_], A4[:, kt:kt + 1], rstd[:, s_:e_],
                                          ALU.mult, ALU.mult), reads=[xt, mv, rstd], writes=[tm])
                            P.op(ACT, act(hTf[kt][:, s_:e_], tm[:, 0:e_ - s_], AF.Identity, bias=B4[:, kt:kt + 1]),
                                 reads=[tm, modsb], writes=[hTf[kt]])
                        if not hasl:
                            P.op(POOL, memset(hTf[kt][:, 0:1], 0.0), writes=[hTf[kt]])
                        if not hasr:
                            P.op(POOL, memset(hTf[kt][:, cw + 1:cw + 2], 0.0), writes=[hTf[kt]])
                    mpieces = [(s_, min(s_ + 512, cw + 2)) for s_ in range(0, cw + 2, 512)]
                    pend_gate = None
                    for ct in range(NCT):
                        w_ = wup[nw % 3]
                        nw += 1
                        P.dma(SP if ct % 2 == 0 else ACT, w_[:], WUPB[l][ct].rearrange("p (k c) -> p k c", c=256),
                              reads=[WUPB[l]], writes=[w_])
                        ys = []
                        for part in range(2):
                            u = U[ct % 2][part]
                            for (s_, e_) in mpieces:
                                p_ = nxt()
                                P.group(PE, [mm(p_[:, 0:e_ - s_], w_[:, kt, part * 128:(part + 1) * 128], hTf[kt][:, s_:e_],
                                                start=(kt == 0), stop=(kt == 7)) for kt in range(8)],
                                        reads=[w_] + hTf, writes=[p_])
                                P.op(ACT, acp(u[:, s_:e_], p_[:, 0:e_ - s_]), reads=[p_], writes=[u])
                            y = Y[ct % 2][part]
                            ti = part * NCT + ct
                            cwv = V("cw", l)
                            cbv = V("cb", l)
                            eng = DVE
                            P.op(ACT, act(y[:, 0:cw], u[:, 0:cw], AF.Identity, bias=cbv[:, ti:ti + 1], scale=cwv[:, ti:ti + 1]),
                                 reads=[u, vecs], writes=[y])
                            P.op(eng, stt(y[:, 0:cw], u[:, 1:cw + 1], cwv[:, 44 + ti:44 + ti + 1], y[:, 0:cw], ALU.mult, ALU.add),
                                 reads=[u, vecs, y], writes=[y])
                            P.op(eng, stt(y[:, 0:cw], u[:, 2:cw + 2], cwv[:, 88 + ti:88 + ti + 1], y[:, 0:cw], ALU.mult, ALU.add),
                                 reads=[u, vecs, y], writes=[y])
                            ys.append(y)
                        if pend_gate is not None:
                            pend_gate()

                        def gate(ct=ct, ys=ys, cw=cw):
                            ya, yv = ys
                            sa = U[ct % 2][0]
                            P.op(ACT, act(sa[:, 0:cw], ya[:, 0:cw], AF.Silu), reads=[ya], writes=[sa])
                            P.op(DVE, tt(gf[ct][:, 0:cw], sa[:, 0:cw], yv[:, 0:cw], ALU.mult), reads=[sa, yv], writes=[gf[ct]])
                        pend_gate = gate
                    pend_gate()
                    G5 = MV(l, who, 3)
                    done = []
                    for s_ in range(0, cw, 512):
                        hw_ = min(512, cw - s_)
                        tx = P.dma(SP, xre[:], X1[:, t0 + s_:t0 + s_ + hw_].rearrange("(k p) t -> p k t", p=128),
                                   reads=[X1] + hTf, writes=[xre], extra=[xt.w] + list(xt.r))
                        for dt in range(8):
                            wd = wdn[nd % 3]
                            nd += 1
                            P.dma(ACT if dt % 2 == 0 else SP, wd[:], WDNB[l][dt].rearrange("p (c k) -> p c k", k=128),
                                  reads=[WDNB[l]], writes=[wd])
                            p_ = nxt()
                            P.group(PE, [mm(p_[:, 0:hw_], wd[:, ct, :], gf[ct][:, s_:s_ + hw_], start=(ct == 0), stop=(ct == NCT - 1))
                                         for ct in range(NCT)], reads=[wd] + gf, writes=[p_])
                            P.op(ACT, lambda e, o=sqf[dt][:, 0:hw_], i=p_[:, 0:hw_]: e.square(o, i), reads=[p_], writes=[sqf[dt]])
                            P.op(DVE, cp(fsb_v[dt][:, 0:hw_], p_[:, 0:hw_]), reads=[p_], writes=[fsb_v[dt]],
                                 extra=[xt.w] + list(xt.r))
                        pn = nxt()
                        P.group(PE, [mm(pn[:, 0:hw_], ones_b[:], sqf[dt][:, 0:hw_], start=(dt == 0), stop=(dt == 7))
                                     for dt in range(8)], reads=sqf + [ones_b], writes=[pn])
                        P.op(ACT, act(rstd[:, 0:hw_], pn[:, 0:hw_], AF.Ln, bias=eps_t[:, 0:1], scale=1.0 / D),
                             reads=[pn, eps_t], writes=[rstd])
                        P.op(ACT, act(rstd[:, 0:hw_], rstd[:, 0:hw_], AF.Exp, scale=-0.5), reads=[rstd], writes=[rstd])
                        for dt in range(8):
                            tm = tmpf[dt % 3]
                            P.op(DVE, stt(tm[:, 0:hw_], fsb_v[dt][:, 0:hw_], G5[:, dt:dt + 1], rstd[:, 0:hw_], ALU.mult, ALU.mult),
                                 reads=[fsb_v[dt], mv, rstd], writes=[tm])
                            P.op(DVE, tt(xre[:, dt, :], xre[:, dt, :], tm[:, 0:hw_], ALU.add), reads=[tm, xre], writes=[xre])
                        if X2 is YT:
                            dst = X2[:, t0 - LC + s_:t0 - LC + s_ + hw_]
                        else:
                            dst = X2[:, t0 + s_:t0 + s_ + hw_]
                        tk = P.dma(SP, dst.rearrange("(k p) t -> p k t", p=128), xre[:], reads=[xre], writes=[X2])
                        done.append(tk)
                        if X2 is YT:
                            out_toks.append(tk)
                        done += [f.w for f in fsb_v if f.w is not None]
                        for f in fsb_v:
                            done += list(f.r)
                    prev_done = [d for d in done if d is not None]
                P.barrier()
                flush()
            if stop_after == ("F", l):
                break

        P.maxops = None
        P.barrier()
        for (name, ap, o, shape) in dbg_list:
            rows = shape[0]
            for r0 in range(0, rows, 128):
                r1 = min(rows, r0 + 128)
                P.dma(POOL, o[r0:r1], ap[r0:r1], max_dma_last_dim=2048)
        P.barrier()
        flush()
    return nc


def make_inputs(inp):
    f32 = np.float32
    g = lambda n: np.asarray(inp[n], f32)
    x, c, ctx, c_ctx = g("x"), g("c"), g("ctx"), g("c_ctx")
    B = x.shape[0]
    w_in = g("w_in")
    w_inx = np.ascontiguousarray(w_in[:, :, EXT_COLS])
    w_uq = g("mla_w_uq")
    cols = []
    for h in range(4):
        a = list(range(h * 96, (h + 1) * 96))
        b = a[:64] + rope_perm_cols(a[64:], 32)
        cols += a + b
    w_uqx = np.ascontiguousarray(w_uq[:, :, cols])
    w_ukv = g("mla_w_ukv")
    kc, vc = [], []
    for h in range(4):
        kc += list(range(h * 128, h * 128 + 64))
        vc += list(range(h * 128 + 64, h * 128 + 128))
    w_ukvx = np.ascontiguousarray(w_ukv[:, :, kc + vc])

    rep = lambda v: np.broadcast_to(v[None, :], (128, v.shape[0]))
    p64 = np.arange(128) % 64
    common = np.zeros((128, NV), f32)
    for l in range(DEPTH):
        def put(name, arr):
            o = VOFF[(name, l)]
            arr = np.asarray(arr, f32)
            if arr.ndim == 1:
                arr = arr[:, None]
            common[:, o:o + arr.shape[1]] = arr
        put("gpre", fm(g("g_mix_pre")[l]))
        put("gpost", fm(g("g_mix_post")[l]))
        put("gfpre", fm(g("g_ffn_pre")[l]))
        put("gfpost", fm(g("g_ffn_post")[l]))
        put("bmod", fm(g("b_mod")[l]))
        put("lq1", rep(g("diff_lam_q1")[l]))
        put("lk1", rep(g("diff_lam_k1")[l]))
        put("lq2", rep(g("diff_lam_q2")[l]))
        put("lk2", rep(g("diff_lam_k2")[l]))
        put("gsub", g("diff_g_sub")[l][p64])
        put("sink", rep(g("swa_sink")[l]))
        mgq = np.zeros((128, 2), f32)
        mgq[:, 0] = g("mla_g_q")[l][:128]
        mgq[:64, 1] = g("mla_g_q")[l][128:192]
        put("mgq", mgq)
        put("mgkv", g("mla_g_kv")[l])
        put("ggq", g("gqa_g_q")[l][p64])
        put("ggqp", g("gqa_g_q")[l][p64 ^ 16])
        put("ggk", g("gqa_g_k")[l][p64])
        put("ggkp", g("gqa_g_k")[l][p64 ^ 16])
        cw = g("ffn_conv_w")[l]
        put("cw", np.concatenate([fm(cw[j]) for j in range(3)], axis=1))
        put("cb", fm(g("ffn_conv_b")[l]))
    rope = rope_tables()
    masks = swa_masks()
    shared = {"rope": rope, "masks": masks, "w_mod": g("w_mod"), "w_inx": w_inx, "w_uqx": w_uqx,
              "w_ukvx": w_ukvx, "w_out": g("w_out"), "w_up": g("ffn_w_up"), "w_dn": g("ffn_w_down")}
    maps = []
    for b in range(B):
        v = common.copy()
        cin = np.zeros((128, 8, 2), f32)
        cin[:, :, 0] = fm(c[b])
        cin[:, :, 1] = fm(c_ctx)
        v[:, VOFF["cin"]:VOFF["cin"] + 16] = cin.reshape(128, 16)
        xin = np.ascontiguousarray(np.concatenate([ctx[b].T, x[b].T], axis=1))
        m = dict(shared)
        m["xin"] = xin
        m["vecs"] = v
        maps.append(m)
    return maps


_NC_CACHE = {}


def kernel(**inputs):
    maps = make_inputs(inputs)
    if "nc" not in _NC_CACHE:
        _NC_CACHE["nc"] = build()
    nc = _NC_CACHE["nc"]
    res = run_bass_kernel_spmd(nc, maps, core_ids=list(range(len(maps))))
    out = np.stack([np.ascontiguousarray(r["yT"].T) for r in res.results], axis=0)
    return out.astype(np.float32)
```

```python
import math
from contextlib import ExitStack

import numpy as np
import ml_dtypes

import concourse.bass as bass
import concourse.mybir as mybir
from concourse.bass_utils import run_bass_kernel_spmd

F32 = mybir.dt.float32
BF16 = mybir.dt.bfloat16
ALU = mybir.AluOpType
AF = mybir.ActivationFunctionType

PE, ACT, DVE, POOL, SP = "pe", "act", "dve", "pool", "sp"

D = 1024
S = 4096
LC = 256
TT = S + LC
NKT = TT // 128
DEPTH = 2
EPS = 1e-6
FFN = 2816
NCT = FFN // 128
NEXT = 3072 + 512


class T:
    def __init__(self, ap, name="", psum=False):
        self.ap = ap
        self.name = name
        self.psum = psum
        self.w = None
        self.r = []

    def __getitem__(self, idx):
        return self.ap[idx]


class Prog:
    def __init__(self, nc, n_dma_sems=8):
        self.nc = nc
        self.engs = (PE, ACT, DVE, POOL, SP)
        self.ops = {e: [] for e in self.engs}
        self.cnt = {("e", e): 0 for e in (PE, ACT, DVE, POOL)}
        self.seen = {e: {} for e in self.engs}
        self.n_dma_sems = n_dma_sems
        self.dma_rr = {e: 0 for e in self.engs}
        self.dma_last = {}
        self.nops = 0
        self.maxops = None

    def all_keys(self):
        keys = [("e", e) for e in (PE, ACT, DVE, POOL)]
        for q in (SP, POOL, ACT):
            for i in range(self.n_dma_sems):
                keys.append(("d", q, i))
        return keys

    def _waits_for(self, eng, reads, writes, extra=()):
        need = {}

        def add(tok):
            if tok is None:
                return
            k, v = tok
            if need.get(k, 0) < v:
                need[k] = v

        for t in reads:
            add(t.w)
            if t.psum:
                for x in t.r:
                    if x is not None and x[0] != ("e", eng):
                        add(x)
        for t in writes:
            add(t.w)
            for x in t.r:
                add(x)
        for x in extra:
            add(x)
        out = []
        for k, v in need.items():
            if eng == PE and k == ("e", PE):
                continue
            if self.seen[eng].get(k, 0) < v:
                self.seen[eng][k] = v
                out.append((k, v))
        return out

    def _reg(self, tok, reads, writes):
        for t in reads:
            t.r.append(tok)
            if len(t.r) > 64:
                best = {}
                for (k, v) in t.r:
                    if best.get(k, 0) < v:
                        best[k] = v
                t.r = list(best.items())
        for t in writes:
            t.w = tok
            t.r = []

    def op(self, eng, fn, reads=(), writes=(), extra=()):
        return self.group(eng, [fn], reads, writes, extra)

    def group(self, eng, fns, reads=(), writes=(), extra=()):
        if self.maxops is not None and self.nops >= self.maxops:
            return None
        waits = self._waits_for(eng, reads, writes, extra)
        k = ("e", eng)
        self.cnt[k] += 1
        tok = (k, self.cnt[k])
        n = len(fns)
        for i, fn in enumerate(fns):
            self.ops[eng].append((fn, waits if i == 0 else [], (k, 1) if i == n - 1 else None))
        self.nops += n
        self._reg(tok, reads, writes)
        return tok

    def dma(self, eng, out_ap, in_ap, reads=(), writes=(), extra=(), **kw):
        if self.maxops is not None and self.nops >= self.maxops:
            return None
        i = self.dma_rr[eng]
        self.dma_rr[eng] = (i + 1) % self.n_dma_sems
        k = ("d", eng, i)
        waits = self._waits_for(eng, reads, writes, extra)
        prev = self.dma_last.get(k, 0)
        if prev and self.seen[eng].get(k, 0) < prev:
            self.seen[eng][k] = prev
            waits.append((k, prev))
        val = prev + 16
        self.dma_last[k] = val
        tok = (k, val)

        def fn(e, out_ap=out_ap, in_ap=in_ap, kw=kw):
            return e.dma_start(out=out_ap, in_=in_ap, **kw)

        self.ops[eng].append((fn, waits, (k, 16)))
        self.nops += 1
        self._reg(tok, reads, writes)
        return tok

    def barrier(self):
        toks = [(k, v) for k, v in self.cnt.items() if v > 0]
        toks += [(k, v) for k, v in self.dma_last.items() if v > 0]
        for eng in self.engs:
            waits = []
            for (k, v) in toks:
                if self.seen[eng].get(k, 0) < v:
                    self.seen[eng][k] = v
                    waits.append((k, v))
            if waits:
                self.ops[eng].append((None, waits, None))

    def emit(self, block, sems):
        def run(engname):
            def body(e):
                for (fn, waits, inc) in self.ops[engname]:
                    for (k, v) in waits:
                        e.wait_ge(sems[k], v)
                    if fn is None:
                        continue
                    ins = fn(e)
                    if inc is not None:
                        ins.then_inc(sems[inc[0]], inc[1])
            return body

        block.tensor(run(PE))
        block.scalar(run(ACT))
        block.vector(run(DVE))
        block.gpsimd(run(POOL))
        block.sync(run(SP))
        self.ops = {e: [] for e in self.engs}


def mm(out, lhsT, rhs, start=True, stop=True, **kw):
    return lambda e: e.matmul(out, lhsT, rhs, start=start, stop=stop, **kw)


def act(out, in_, func, bias=None, scale=1.0):
    if bias is None:
        return lambda e: e.activation(out, in_, func, scale=scale)
    return lambda e: e.activation(out, in_, func, bias=bias, scale=scale)


def tt(out, a, b, op):
    return lambda e: e.tensor_tensor(out=out, in0=a, in1=b, op=op)


def ts2(out, a, s1, s2, op0, op1):
    return lambda e: e.tensor_scalar(out, a, s1, s2, op0, op1)


def ts1(out, a, s1, op0):
    return lambda e: e.tensor_scalar(out, a, s1, None, op0)


def stt(out, a, s, b, op0, op1):
    return lambda e: e.scalar_tensor_tensor(out=out, in0=a, scalar=s, in1=b, op0=op0, op1=op1)


def cp(out, a):
    return lambda e: e.tensor_copy(out, a)


def acp(out, a):
    return lambda e: e.copy(out, a)


def memset(out, v):
    return lambda e: e.memset(out, v)


VEC_L = [("gpre", 8), ("gpost", 8), ("gfpre", 8), ("gfpost", 8), ("bmod", 48),
         ("lq1", 32), ("lk1", 32), ("lq2", 32), ("lk2", 32), ("gsub", 1), ("sink", 4),
         ("mgq", 2), ("mgkv", 1), ("ggq", 1), ("ggqp", 1), ("ggk", 1), ("ggkp", 1),
         ("cw", 132), ("cb", 44)]
VEC_G = [("cin", 16)]


def vec_offsets():
    off = {}
    o = 0
    for (n, c) in VEC_G:
        off[n] = o
        o += c
    for l in range(DEPTH):
        for (n, c) in VEC_L:
            off[(n, l)] = o
            o += c
    return off, o


VOFF, NV = vec_offsets()


def fm(v):
    return np.ascontiguousarray(v.reshape(-1, 128).T)


def rope_perm_cols(cols, d):
    n = d // 4
    cols = list(cols)
    out = []
    for i, c in enumerate(cols):
        base = (i // d) * d
        out.append(cols[base + ((i - base) ^ n)])
    return out


def w_in_ext_cols():
    tiles = []
    r = lambda a, b: list(range(a, b))
    dq0, dq1 = r(0, 128), r(128, 256)
    dk0, dk1 = r(256, 384), r(384, 512)
    o1 = 768
    sqh = [r(o1 + h * 64, o1 + (h + 1) * 64) for h in range(4)]
    sq0, sq1 = sqh[0] + sqh[2], sqh[1] + sqh[3]
    sk = r(o1 + 256, o1 + 384)
    o2 = 1280
    mcq0, mcq1 = r(o2, o2 + 128), r(o2 + 128, o2 + 192)
    mckv = r(o2 + 192, o2 + 320)
    kr = r(o2 + 320, o2 + 352)
    mkr = r(o2 + 192, o2 + 256) + kr
    mkrp = r(o2 + 192, o2 + 256) + rope_perm_cols(kr, 32)
    o3 = 1632
    gqh = [r(o3 + h * 64, o3 + (h + 1) * 64) for h in range(4)]
    gq0, gq1 = gqh[0] + gqh[2], gqh[1] + gqh[3]
    gk = r(o3 + 256, o3 + 384)
    P32 = lambda c: rope_perm_cols(c, 32)
    P64 = lambda c: rope_perm_cols(c, 64)
    tiles = [dq0, dq1, P32(dq0), P32(dq1), dk0, dk1, P32(dk0), P32(dk1),
             sq0, sq1, P64(sq0), P64(sq1), sk, P64(sk),
             mcq0, mcq1, mckv, mkr, mkrp,
             gq0, gq1, P64(gq0), P64(gq1), gk, P64(gk)]
    wv = r(512, 768) + r(o1 + 384, o1 + 512) + r(o3 + 384, o3 + 512)
    offs = []
    cols = []
    for t in tiles:
        offs.append((len(cols), len(t)))
        cols += t
    assert len(cols) == 3072
    voff = len(cols)
    cols += wv
    assert len(cols) == NEXT
    return cols, offs, voff


EXT_COLS, EXT_OFFS, EXT_VOFF = w_in_ext_cols()
(T_DQ0, T_DQ1, T_DQ0P, T_DQ1P, T_DK0, T_DK1, T_DK0P, T_DK1P, T_SQ0, T_SQ1, T_SQ0P, T_SQ1P, T_SK, T_SKP,
 T_MCQ0, T_MCQ1, T_MCKV, T_MKR, T_MKRP, T_GQ0, T_GQ1, T_GQ0P, T_GQ1P, T_GK, T_GKP) = range(25)

(Q_DQ0, Q_DQ1, Q_DK0, Q_DK1, Q_SQ0, Q_SQ1, Q_SK, Q_GQ0, Q_GQ1, Q_GK) = range(10)
Q_MQ = 10
Q_MK = 14
NQK = 18


def rope_tables():
    out = np.zeros((4, 128, TT), np.float32)
    s = np.arange(S)
    row = (s // 64).astype(np.float32)
    col = (s % 64).astype(np.float32)
    for fi, d in ((0, 32), (2, 64)):
        n = d // 4
        inv = (np.float32(10000.0) ** (-np.arange(n, dtype=np.float32) / np.float32(n))).astype(np.float32)
        for p in range(128):
            idx = p % d
            axis = idx // (2 * n)
            half = (idx // n) % 2
            i = idx % n
            pos = row if axis == 0 else col
            ang = (pos * inv[i]).astype(np.float32)
            out[fi, p, :LC] = 1.0
            out[fi, p, LC:] = np.cos(ang)
            out[fi + 1, p, :LC] = 0.0
            out[fi + 1, p, LC:] = np.sin(ang) * (-1.0 if half == 0 else 1.0)
    return out


def swa_masks():
    m = np.zeros((6, 128, 512), np.float32)
    kl = np.arange(128)[:, None]
    ql = np.arange(512)[None, :]
    for ri, r in enumerate(range(-1, 5)):
        m[ri] = (np.abs(ql - kl - 128 * r) <= 128).astype(np.float32)
    return m.astype(ml_dtypes.bfloat16)


def build(debug=False, stop_after=None, maxops=None):
    nc = bass.Bass("TRN2", target_bir_lowering=False)
    dk = "ExternalOutput" if debug else "Internal"

    def din(name, shape, dt=F32):
        return nc.dram_tensor(name, shape, dt, kind="ExternalInput").ap()

    dbg_list = []

    def dscr(name, shape, dt):
        ap = nc.dram_tensor(name, shape, dt, kind="Internal").ap()
        if debug and name in debug:
            o = nc.dram_tensor("D_" + name, shape, F32, kind="ExternalOutput").ap()
            dbg_list.append((name, ap, o, shape))
        return ap

    xin = din("xin", [D, TT])
    vecs_d = din("vecs", [128, NV])
    rope_d = din("rope", [4, 128, TT])
    mask_d = din("masks", [6, 128, 512], BF16)
    wmod_d = din("w_mod", [DEPTH, D, 6 * D])
    winx_d = din("w_inx", [DEPTH, D, NEXT])
    wuqx_d = din("w_uqx", [DEPTH, 192, 768])
    wukvx_d = din("w_ukvx", [DEPTH, 128, 512])
    wout_d = din("w_out", [DEPTH, D, D])
    wup_d = din("w_up", [DEPTH, D, 2 * FFN])
    wdn_d = din("w_dn", [DEPTH, FFN, D])
    yT = nc.dram_tensor("yT", [D, S], F32, kind="ExternalOutput").ap()

    XA = T(dscr("XA", [D, TT], F32), "XA")
    XB = T(dscr("XB", [D, TT], F32), "XB")
    XIN = T(xin, "xin")
    YT = T(yT, "yT")
    QK = [T(dscr("QK%d" % i, [128, TT], BF16), "QK%d" % i) for i in range(NQK)]
    VD = T(dscr("VD", [TT, 260], BF16), "VD")
    VM = T(dscr("VM", [TT, 260], BF16), "VM")
    VS = T(dscr("VS", [TT, 130], BF16), "VS")
    VG = T(dscr("VG", [TT, 130], BF16), "VG")
    MIX = T(dscr("MIX", [D, TT], BF16), "MIX")
    WUPB = [T(dscr("WUPB%d" % l, [NCT, 128, 8 * 256], BF16), "WUPB") for l in range(DEPTH)]
    WDNB = [T(dscr("WDNB%d" % l, [8, 128, NCT * 128], BF16), "WDNB") for l in range(DEPTH)]

    P = Prog(nc)
    P.maxops = maxops
    out_toks = []
    uid = [0]

    def sbuf_t(name, shape, dt):
        uid[0] += 1
        return nc.sbuf_tensor("%s_u%d" % (name, uid[0]), shape, dt)

    def psum_t(name, shape, dt):
        uid[0] += 1
        return nc.psum_tensor("%s_u%d" % (name, uid[0]), shape, dt)

    with ExitStack() as es0:
        sems = {k: es0.enter_context(nc.semaphore("s%d" % i)) for i, k in enumerate(P.all_keys())}

        def flush():
            with nc.Block() as block:
                P.emit(block, sems)

        def sb0(name, shape, dt):
            return T(es0.enter_context(sbuf_t(name, shape, dt))[:], name)

        vecs = sb0("vecs", [128, NV], F32)
        mv = sb0("mv", [128, DEPTH * 2 * 4 * 8], F32)
        modsb = sb0("modsb", [128, DEPTH * 96], F32)
        ones_b = sb0("ones_b", [128, 128], BF16)
        bd64 = sb0("bd64", [128, 128], BF16)
        sel = sb0("sel", [128, 64], F32)
        eps_t = sb0("eps_t", [128, 1], F32)
        small = sb0("small", [128, 16 * DEPTH], F32)

        def MV(l, who, kind):
            o = ((l * 2 + who) * 4 + kind) * 8
            return mv[:, o:o + 8]

        def MOD(l, j, who):
            base = l * 96
            return modsb[:, base:base + 96].rearrange("p (i w) -> p i w", w=2)[:, j * 8:(j + 1) * 8, who]

        def V(name, l=None, n=None):
            o = VOFF[name] if l is None else VOFF[(name, l)]
            w = n if n is not None else dict(VEC_L + VEC_G)[name]
            return vecs[:, o:o + w]

        with ExitStack() as es:
            def sb(name, shape, dt):
                return T(es.enter_context(sbuf_t(name, shape, dt))[:], name)

            def ps(name, shape, dt=F32):
                return T(es.enter_context(psum_t(name, shape, dt))[:], name, psum=True)

            P.dma(SP, vecs[:], vecs_d, writes=[vecs])
            P.op(DVE, memset(ones_b[:], 1.0), writes=[ones_b])
            P.op(DVE, memset(bd64[:], 0.0), writes=[bd64])
            P.op(DVE, memset(bd64[0:64, 0:64], 1.0), writes=[bd64])
            P.op(DVE, memset(bd64[64:128, 64:128], 1.0), writes=[bd64])
            P.op(DVE, memset(sel[:], 0.0), writes=[sel])
            P.op(DVE, memset(sel[64:65, :], 1.0), writes=[sel])
            P.op(DVE, memset(eps_t[:], EPS), writes=[eps_t])

            sil = sb("sil", [128, 16], F32)
            P.op(ACT, act(sil[:], V("cin"), AF.Silu), reads=[vecs], writes=[sil])
            wmt = [sb("wmt%d" % i, [128, 8, 512], F32) for i in range(2)]
            pm = ps("pm", [128, 96])
            for l in range(DEPTH):
                for cb in range(12):
                    w = wmt[(l * 12 + cb) % 2]
                    P.dma(SP if cb % 2 == 0 else ACT, w[:],
                          wmod_d[l, :, cb * 512:(cb + 1) * 512].rearrange("(k p) c -> p k c", p=128), writes=[w])
                    fns = []
                    for sub in range(4):
                        idx = cb * 4 + sub
                        for kt in range(8):
                            fns.append(mm(pm[:, idx * 2:idx * 2 + 2], w[:, kt, sub * 128:(sub + 1) * 128],
                                          sil[:, kt * 2:kt * 2 + 2], start=(kt == 0), stop=(kt == 7)))
                    P.group(PE, fns, reads=[w, sil], writes=[pm])
                mo = modsb[:, l * 96:(l + 1) * 96].rearrange("p (i w) -> p i w", w=2)
                pmv = pm[:, :].rearrange("p (i w) -> p i w", w=2)
                for who in range(2):
                    P.op(DVE, tt(mo[:, :, who], pmv[:, :, who], V("bmod", l), ALU.add),
                         reads=[pm, vecs], writes=[modsb])
                for who in range(2):
                    P.op(DVE, stt(MV(l, who, 0), MOD(l, 1, who), 1.0, V("gpre", l), ALU.add, ALU.mult),
                         reads=[modsb, vecs], writes=[mv])
                    P.op(DVE, tt(MV(l, who, 1), MOD(l, 2, who), V("gpost", l), ALU.mult),
                         reads=[modsb, vecs], writes=[mv])
                    P.op(DVE, stt(MV(l, who, 2), MOD(l, 4, who), 1.0, V("gfpre", l), ALU.add, ALU.mult),
                         reads=[modsb, vecs], writes=[mv])
                    P.op(DVE, tt(MV(l, who, 3), MOD(l, 5, who), V("gfpost", l), ALU.mult),
                         reads=[modsb, vecs], writes=[mv])
                lam_init = 0.8 - 0.6 * math.exp(-0.3 * l)
                so = l * 16
                tmp32 = sb("tmp32_%d" % l, [128, 32], F32)
                acc = sb("acc_%d" % l, [128, 4], F32)
                P.op(DVE, tt(tmp32[:], V("lq1", l), V("lk1", l), ALU.mult), reads=[vecs], writes=[tmp32])
                P.op(DVE, lambda e, a=acc, t=tmp32: e.reduce_sum(a[:, 0:1], t[:], mybir.AxisListType.X),
                     reads=[tmp32], writes=[acc])
                P.op(DVE, tt(tmp32[:], V("lq2", l), V("lk2", l), ALU.mult), reads=[vecs, acc], writes=[tmp32])
                P.op(DVE, lambda e, a=acc, t=tmp32: e.reduce_sum(a[:, 1:2], t[:], mybir.AxisListType.X),
                     reads=[tmp32], writes=[acc])
                P.op(ACT, act(acc[:, 2:4], acc[:, 0:2], AF.Exp), reads=[acc], writes=[acc])
                P.op(DVE, stt(small[:, so:so + 1], acc[:, 3:4], -lam_init, acc[:, 2:3], ALU.add, ALU.subtract),
                     reads=[acc], writes=[small])
                P.op(DVE, ts1(small[:, so + 1:so + 2], V("gsub", l), 1.0 - lam_init, ALU.mult),
                     reads=[vecs], writes=[small])
                P.op(ACT, act(small[:, so + 2:so + 6], V("sink", l), AF.Exp), reads=[vecs], writes=[small])

            stg = [sb("stg%d" % i, [128, 8, 256], BF16) for i in range(6)]
            n = 0
            for l in range(DEPTH):
                for ct in range(NCT):
                    s_ = stg[n % 6]
                    n += 1
                    for part in range(2):
                        c0 = part * FFN + ct * 128
                        P.dma(POOL, s_[:, :, part * 128:(part + 1) * 128],
                              wup_d[l, :, c0:c0 + 128].rearrange("(k p) c -> p k c", p=128), writes=[s_])
                    P.dma(SP, WUPB[l][ct].rearrange("p (k c) -> p k c", c=256), s_[:], reads=[s_], writes=[WUPB[l]])
            stg2 = [sb("stgd%d" % i, [128, 1024], BF16) for i in range(6)]
            for l in range(DEPTH):
                for ct in range(NCT):
                    s_ = stg2[n % 6]
                    n += 1
                    P.dma(POOL, s_[:], wdn_d[l, ct * 128:(ct + 1) * 128, :], writes=[s_])
                    P.dma(SP, WDNB[l][:, :, ct * 128:(ct + 1) * 128].rearrange("d p c -> p d c"),
                          s_[:].rearrange("p (d c) -> p d c", c=128), reads=[s_], writes=[WDNB[l]])
            P.barrier()
            flush()

        X_seq = [(XIN, XA, XB), (XB, XA, YT)]
        print("ops after phase0:", P.nops)

        for l in range(DEPTH):
            if stop_after == ("0", 0):
                break
            X0, X1, X2 = X_seq[l]
            need_ctx = l < DEPTH - 1
            with ExitStack() as es:
                def sb(name, shape, dt):
                    return T(es.enter_context(sbuf_t(name, shape, dt))[:], name)

                def ps(name, shape, dt=F32):
                    return T(es.enter_context(psum_t(name, shape, dt))[:], name, psum=True)

                wext_h = es.enter_context(sbuf_t("wext", [128, 8, NEXT], BF16))
                wext = [T(wext_h[:, kt, :], "wext%d" % kt) for kt in range(8)]
                for kt in range(8):
                    for hh in range(2):
                        c0, c1 = hh * (NEXT // 2), (hh + 1) * (NEXT // 2)
                        P.dma(POOL, wext[kt][:, c0:c1], winx_d[l, kt * 128:(kt + 1) * 128, c0:c1], writes=[wext[kt]],
                              max_dma_last_dim=4096)
                wuq0 = sb("wuq0", [128, 768], BF16)
                wuq1 = sb("wuq1", [64, 768], BF16)
                wukv = sb("wukv", [128, 512], BF16)
                P.dma(POOL, wuq0[:], wuqx_d[l, 0:128, :], writes=[wuq0])
                P.dma(POOL, wuq1[:], wuqx_d[l, 128:192, :], writes=[wuq1])
                P.dma(POOL, wukv[:], wukvx_d[l], writes=[wukv])

                xts = [sb("xt%d" % i, [128, 8, 512], F32) for i in range(2)]
                tabs = [sb("tab%d" % i, [128, 4, 512], F32) for i in range(2)]
                sq_h = es.enter_context(sbuf_t("sq", [128, 8, 512], BF16))
                sq = [T(sq_h[:, kt, :], "sq%d" % kt) for kt in range(8)]
                hT_h = [es.enter_context(sbuf_t("hT%d" % i, [128, 8, 512], BF16)) for i in range(2)]
                hTs = [[T(h[:, kt, :], "hT") for kt in range(8)] for h in hT_h]
                rstd = sb("rstd", [128, 512], F32)
                tmpf = [sb("tmpf%d" % i, [128, 512], F32) for i in range(4)]
                ost = [sb("ost%d" % i, [128, 512], BF16) for i in range(6)]
                sqs = [sb("sqs%d" % i, [128, 512], BF16) for i in range(2)]
                nrm = [sb("nrm%d" % i, [128, 512], F32) for i in range(2)]
                cqn0 = sb("cqn0", [128, 512], BF16)
                cqn1 = sb("cqn1", [64, 512], BF16)
                ckvn = sb("ckvn", [128, 512], BF16)
                krr = sb("krr", [128, 512], BF16)
                vst = [[sb("vst%d_%d" % (i, j), [128, 4, 4 * 65 if j < 2 else 2 * 65], BF16) for j in range(4)]
                       for i in range(2)]
                for i in range(2):
                    for j in range(4):
                        P.op(POOL, memset(vst[i][j][:], 1.0), writes=[vst[i][j]])
                pp = [ps("pp%d" % i, [128, 512]) for i in range(8)]
                ppi = [0]

                def nxt():
                    ppi[0] = (ppi[0] + 1) % 8
                    return pp[ppi[0]]

                tfi = [0]

                def ntmp():
                    tfi[0] = (tfi[0] + 1) % 4
                    return tmpf[tfi[0]]

                osi = [0]

                def nost():
                    osi[0] = (osi[0] + 1) % 6
                    return ost[osi[0]]

                blocks = [(0, LC, 1)] + [(LC + i * 512, 512, 0) for i in range(8)]
                for bi, (t0, tw, who) in enumerate(blocks):
                    xt = xts[bi % 2]
                    tab = tabs[bi % 2]
                    hT = hTs[bi % 2]
                    P.dma(SP, xt[:, :, 0:tw], X0[:, t0:t0 + tw].rearrange("(k p) t -> p k t", p=128),
                          reads=[X0], writes=[xt])
                    P.dma(SP, tab[:, :, 0:tw], rope_d[:, :, t0:t0 + tw].rearrange("f p t -> p f t"), writes=[tab])
                    for kt in range(8):
                        P.op(ACT, lambda e, o=sq[kt][:, 0:tw], i=xt[:, kt, 0:tw]: e.square(o, i),
                             reads=[xt], writes=[sq[kt]])
                    pn = nxt()
                    P.group(PE, [mm(pn[:, 0:tw], ones_b[:], sq[kt][:, 0:tw], start=(kt == 0), stop=(kt == 7))
                                 for kt in range(8)], reads=sq + [ones_b], writes=[pn])
                    P.op(ACT, act(rstd[:, 0:tw], pn[:, 0:tw], AF.Ln, bias=eps_t[:, 0:1], scale=1.0 / D),
                         reads=[pn, eps_t], writes=[rstd])
                    P.op(ACT, act(rstd[:, 0:tw], rstd[:, 0:tw], AF.Exp, scale=-0.5), reads=[rstd], writes=[rstd])
                    A1 = MV(l, who, 0)
                    B1 = MOD(l, 0, who)
                    for kt in range(8):
                        tm = ntmp()
                        P.op(DVE, stt(tm[:, 0:tw], xt[:, kt, 0:tw], A1[:, kt:kt + 1], rstd[:, 0:tw], ALU.mult, ALU.mult),
                             reads=[xt, mv, rstd], writes=[tm])
                        P.op(ACT, act(hT[kt][:, 0:tw], tm[:, 0:tw], AF.Identity, bias=B1[:, kt:kt + 1]),
                             reads=[tm, modsb], writes=[hT[kt]])

                    def proj(ti, rows=None):
                        c0, cw = EXT_OFFS[ti]
                        p_ = nxt()
                        P.group(PE, [mm(p_[0:cw, 0:tw], wext[kt][:, c0:c0 + cw], hT[kt][:, 0:tw],
                                        start=(kt == 0), stop=(kt == 7)) for kt in range(8)],
                                reads=wext + hT, writes=[p_])
                        return p_

                    def store_qk(qi, src, rows=128):
                        P.dma(SP, QK[qi][0:rows, t0:t0 + tw], src[0:rows, 0:tw], reads=[src], writes=[QK[qi]])

                    def rope(pq, ppm, fi, dst, r0=0, r1=128, eng2=DVE):
                        a = ntmp()
                        P.op(DVE, tt(a[r0:r1, 0:tw], pq[r0:r1, 0:tw], tab[r0:r1, fi, 0:tw], ALU.mult),
                             reads=[pq, tab], writes=[a])
                        b = ntmp()
                        P.op(DVE, tt(b[r0:r1, 0:tw], ppm[r0:r1, 0:tw], tab[r0:r1, fi + 1, 0:tw], ALU.mult),
                             reads=[ppm, tab], writes=[b])
                        P.op(eng2, tt(dst[r0:r1, 0:tw], a[r0:r1, 0:tw], b[r0:r1, 0:tw], ALU.add),
                             reads=[a, b], writes=[dst])

                    for (ta, tp, fi, qi) in ((T_DQ0, T_DQ0P, 0, Q_DQ0), (T_DQ1, T_DQ1P, 0, Q_DQ1),
                                             (T_DK0, T_DK0P, 0, Q_DK0), (T_DK1, T_DK1P, 0, Q_DK1),
                                             (T_SQ0, T_SQ0P, 2, Q_SQ0), (T_SQ1, T_SQ1P, 2, Q_SQ1),
                                             (T_SK, T_SKP, 2, Q_SK)):
                        pa = proj(ta)
                        pb = proj(tp)
                        o = nost()
                        rope(pa, pb, fi, o)
                        store_qk(qi, o)
                    for (ta, tp, qi, g, gp) in ((T_GQ0, T_GQ0P, Q_GQ0, "ggq", "ggqp"), (T_GQ1, T_GQ1P, Q_GQ1, "ggq", "ggqp"),
                                                (T_GK, T_GKP, Q_GK, "ggk", "ggkp")):
                        pa = proj(ta)
                        pb = proj(tp)
                        s_ = sqs[0]
                        P.op(ACT, lambda e, o=s_[:, 0:tw], i=pa[:, 0:tw]: e.square(o, i), reads=[pa], writes=[s_])
                        pn = nxt()
                        P.op(PE, mm(pn[:, 0:tw], bd64[:], s_[:, 0:tw]), reads=[bd64, s_], writes=[pn])
                        r_ = nrm[0]
                        P.op(ACT, act(r_[:, 0:tw], pn[:, 0:tw], AF.Ln, bias=eps_t[:, 0:1], scale=1.0 / 64),
                             reads=[pn, eps_t], writes=[r_])
                        P.op(ACT, act(r_[:, 0:tw], r_[:, 0:tw], AF.Exp, scale=-0.5), reads=[r_], writes=[r_])
                        a = ntmp()
                        P.op(DVE, stt(a[:, 0:tw], pa[:, 0:tw], V(g, l), tab[:, 2, 0:tw], ALU.mult, ALU.mult),
                             reads=[pa, vecs, tab], writes=[a])
                        b = ntmp()
                        P.op(DVE, stt(b[:, 0:tw], pb[:, 0:tw], V(gp, l), tab[:, 3, 0:tw], ALU.mult, ALU.mult),
                             reads=[pb, vecs, tab], writes=[b])
                        P.op(DVE, tt(a[:, 0:tw], a[:, 0:tw], b[:, 0:tw], ALU.add), reads=[a, b], writes=[a])
                        o = nost()
                        P.op(DVE, tt(o[:, 0:tw], a[:, 0:tw], r_[:, 0:tw], ALU.mult), reads=[a, r_], writes=[o])
                        store_qk(qi, o)
                    pc0 = proj(T_MCQ0)
                    pc1 = proj(T_MCQ1)
                    P.op(ACT, lambda e, o=sqs[0][:, 0:tw], i=pc0[:, 0:tw]: e.square(o, i), reads=[pc0], writes=[sqs[0]])
                    P.op(ACT, lambda e, o=sqs[1][0:64, 0:tw], i=pc1[0:64, 0:tw]: e.square(o, i), reads=[pc1], writes=[sqs[1]])
                    pn = nxt()
                    P.group(PE, [mm(pn[:, 0:tw], ones_b[:, :], sqs[0][:, 0:tw], start=True, stop=False),
                                 mm(pn[:, 0:tw], ones_b[0:64, :], sqs[1][0:64, 0:tw], start=False, stop=True)],
                            reads=[ones_b, sqs[0], sqs[1]], writes=[pn])
                    r_ = nrm[0]
                    P.op(ACT, act(r_[:, 0:tw], pn[:, 0:tw], AF.Ln, bias=eps_t[:, 0:1], scale=1.0 / 192),
                         reads=[pn, eps_t], writes=[r_])
                    P.op(ACT, act(r_[:, 0:tw], r_[:, 0:tw], AF.Exp, scale=-0.5), reads=[r_], writes=[r_])
                    mg = V("mgq", l)
                    P.op(DVE, stt(cqn0[:, 0:tw], pc0[:, 0:tw], mg[:, 0:1], r_[:, 0:tw], ALU.mult, ALU.mult),
                         reads=[pc0, vecs, r_], writes=[cqn0])
                    P.op(DVE, stt(cqn1[0:64, 0:tw], pc1[0:64, 0:tw], mg[0:64, 1:2], r_[0:64, 0:tw], ALU.mult, ALU.mult),
                         reads=[pc1, vecs, r_], writes=[cqn1])
                    pkv = proj(T_MCKV)
                    P.op(ACT, lambda e, o=sqs[0][:, 0:tw], i=pkv[:, 0:tw]: e.square(o, i), reads=[pkv], writes=[sqs[0]])
                    pn = nxt()
                    P.op(PE, mm(pn[:, 0:tw], ones_b[:], sqs[0][:, 0:tw]), reads=[ones_b, sqs[0]], writes=[pn])
                    r2 = nrm[1]
                    P.op(ACT, act(r2[:, 0:tw], pn[:, 0:tw], AF.Ln, bias=eps_t[:, 0:1], scale=1.0 / 128),
                         reads=[pn, eps_t], writes=[r2])
                    P.op(ACT, act(r2[:, 0:tw], r2[:, 0:tw], AF.Exp, scale=-0.5), reads=[r2], writes=[r2])
                    P.op(DVE, stt(ckvn[:, 0:tw], pkv[:, 0:tw], V("mgkv", l), r2[:, 0:tw], ALU.mult, ALU.mult),
                         reads=[pkv, vecs, r2], writes=[ckvn])
                    pk = proj(T_MKR)
                    pkp = proj(T_MKRP)
                    rope(pk, pkp, 0, krr, 64, 96)
                    for h in range(4):
                        pq = nxt()
                        P.group(PE, [mm(pq[0:96, 0:tw], wuq0[:, h * 192:h * 192 + 96], cqn0[:, 0:tw], start=True, stop=False),
                                     mm(pq[0:96, 0:tw], wuq1[0:64, h * 192:h * 192 + 96], cqn1[0:64, 0:tw], start=False, stop=True)],
                                reads=[wuq0, wuq1, cqn0, cqn1], writes=[pq])
                        pqp = nxt()
                        P.group(PE, [mm(pqp[0:96, 0:tw], wuq0[:, h * 192 + 96:h * 192 + 192], cqn0[:, 0:tw], start=True, stop=False),
                                     mm(pqp[0:96, 0:tw], wuq1[0:64, h * 192 + 96:h * 192 + 192], cqn1[0:64, 0:tw], start=False, stop=True)],
                                reads=[wuq0, wuq1, cqn0, cqn1], writes=[pqp])
                        o = nost()
                        P.op(ACT, acp(o[0:64, 0:tw], pq[0:64, 0:tw]), reads=[pq], writes=[o])
                        rope(pq, pqp, 0, o, 64, 96)
                        store_qk(Q_MQ + h, o, 96)
                        pkn = nxt()
                        P.op(PE, mm(pkn[0:64, 0:tw], wukv[:, h * 64:(h + 1) * 64], ckvn[:, 0:tw]),
                             reads=[wukv, ckvn], writes=[pkn])
                        o2 = nost()
                        P.op(ACT, acp(o2[0:64, 0:tw], pkn[0:64, 0:tw]), reads=[pkn], writes=[o2])
                        P.op(POOL, cp(o2[64:96, 0:tw], krr[64:96, 0:tw]), reads=[krr], writes=[o2])
                        store_qk(Q_MK + h, o2, 96)
                    vs_ = vst[bi % 2]
                    nsub = tw // 128
                    for sub in range(nsub):
                        pv = nxt()
                        P.group(PE, [mm(pv[:, 0:512], hT[kt][:, sub * 128:(sub + 1) * 128],
                                        wext[kt][:, EXT_VOFF:EXT_VOFF + 512], start=(kt == 0), stop=(kt == 7))
                                     for kt in range(8)], reads=wext + hT, writes=[pv])
                        P.op(DVE, cp(vs_[0][:, sub, :].rearrange("p (h c) -> p h c", c=65)[:, :, 0:64],
                                     pv[:, 0:256].rearrange("p (h c) -> p h c", c=64)), reads=[pv], writes=[vs_[0]])
                        P.op(DVE, cp(vs_[2][:, sub, :].rearrange("p (h c) -> p h c", c=65)[:, :, 0:64],
                                     pv[:, 256:384].rearrange("p (h c) -> p h c", c=64)), reads=[pv], writes=[vs_[2]])
                        P.op(DVE, cp(vs_[3][:, sub, :].rearrange("p (h c) -> p h c", c=65)[:, :, 0:64],
                                     pv[:, 384:512].rearrange("p (h c) -> p h c", c=64)), reads=[pv], writes=[vs_[3]])
                        pm_ = nxt()
                        P.op(PE, mm(pm_[:, 0:256], ckvn[:, sub * 128:(sub + 1) * 128], wukv[:, 256:512]),
                             reads=[ckvn, wukv], writes=[pm_])
                        P.op(DVE, cp(vs_[1][:, sub, :].rearrange("p (h c) -> p h c", c=65)[:, :, 0:64],
                                     pm_[:, 0:256].rearrange("p (h c) -> p h c", c=64)), reads=[pm_], writes=[vs_[1]])
                    for (j, VT_) in ((0, VD), (1, VM), (2, VS), (3, VG)):
                        P.dma(SP, VT_[t0:t0 + tw, :].rearrange("(s p) c -> p s c", p=128), vs_[j][:, 0:nsub, :],
                              reads=[vs_[j]], writes=[VT_])
                P.barrier()
                flush()
                print("ops after P", l, P.nops)
            if stop_after == ("P", l):
                break

            with ExitStack() as es:
                def sb(name, shape, dt):
                    return T(es.enter_context(sbuf_t(name, shape, dt))[:], name)

                def ps(name, shape, dt=F32):
                    return T(es.enter_context(psum_t(name, shape, dt))[:], name, psum=True)

                Sps = [ps("Sps%d" % i, [128, 2, 512]) for i in range(2)]
                Ops = [ps("Ops%d" % i, [128, 512]) for i in range(2)]
                BCp = ps("BCp", [128, 512])
                SSp = BCp
                JK = ps("JK", [128, 512])
                jrhs = sb("jrhs", [128, 512], BF16)
                P.op(POOL, memset(jrhs[:], 0.0), writes=[jrhs])
                kres = [sb("kres%d" % i, [128, TT], BF16) for i in range(4)]
                vres = sb("vres", [128, NKT, 260], BF16)
                qts = [[sb("qt%d_%d" % (i, j), [128, 512], BF16) for j in range(8)] for i in range(2)]
                for i in (2, 3):
                    P.op(POOL, memset(kres[i][:], 0.0), writes=[kres[i]])
                pts = [sb("pt%d" % i, [128, 2, 512], BF16) for i in range(3)]
                osb = [sb("osb%d" % i, [128, 512], F32) for i in range(2)]
                rec = [sb("rec%d" % i, [128, 512], F32) for i in range(2)]
                a1 = sb("a1", [128, 512], F32)
                od = sb("od", [128, 512], F32)
                osq = sb("osq", [128, 512], BF16)
                rs2 = sb("rs2", [128, 512], F32)
                outb = [sb("outb%d" % i, [128, 512], BF16) for i in range(3)]
                msk = sb("msk", [128, 6, 512], BF16)
                P.dma(SP, msk[:], mask_d.rearrange("r p q -> p r q"), writes=[msk])
                for i in range(2):
                    P.op(POOL, memset(osb[i][:], 0.0), writes=[osb[i]])
                so = l * 16
                cnt = {"pt": 0, "S": 0, "O": 0, "ob": 0, "q": 0}

                qblocks = [(LC + i * 512, 512, False) for i in range(8)]
                if need_ctx:
                    qblocks = [(0, LC, True)] + qblocks

                mixers = []
                mixers.append(dict(name="diff", ktiles=[Q_DK0, Q_DK1], krows=[128, 128], V=VD, vw=260,
                                   qtiles=[Q_DQ0, Q_DQ1], qrows=[128, 128],
                                   heads=[dict(qt=j // 4, kt=j // 4, r0=(j % 4) * 32, r1=(j % 4) * 32 + 32, vh=j // 2,
                                               scale=32 ** -0.5, feat=(j // 2) * 64, diff=j % 2) for j in range(8)]))
                mixers.append(dict(name="swa", ktiles=[Q_SK], krows=[128], V=VS, vw=130,
                                   qtiles=[Q_SQ0, Q_SQ1], qrows=[128, 128],
                                   heads=[dict(qt=h % 2, kt=0, r0=(h // 2) * 64, r1=(h // 2) * 64 + 64, vh=h // 2,
                                               scale=64 ** -0.5, feat=256 + h * 64, sink=h) for h in range(4)]))
                mixers.append(dict(name="mla", ktiles=[Q_MK + h for h in range(4)], krows=[96] * 4, V=VM, vw=260,
                                   qtiles=[Q_MQ + h for h in range(4)], qrows=[96] * 4,
                                   heads=[dict(qt=h, kt=h, r0=0, r1=96, vh=h, scale=96 ** -0.5, feat=512 + h * 64)
                                          for h in range(4)]))
                mixers.append(dict(name="gqa", ktiles=[Q_GK], krows=[128], V=VG, vw=130,
                                   qtiles=[Q_GQ0, Q_GQ1], qrows=[128, 128],
                                   heads=[dict(qt=h % 2, kt=0, r0=(h // 2) * 64, r1=(h // 2) * 64 + 64, vh=h // 2,
                                               scale=64 ** -0.5, feat=768 + h * 64) for h in range(4)]))

                for mx in mixers:
                    for hi_, hd_ in enumerate(mx["heads"]):
                        hd_["hi"] = hi_
                    for i in range(2):
                        for j in range(len(mx["heads"])):
                            P.op(POOL, memset(qts[i][j][:], 0.0), writes=[qts[i][j]])
                    for i, qi in enumerate(mx["ktiles"]):
                        r = mx["krows"][i]
                        P.dma(SP, kres[i][0:r, :], QK[qi][0:r, :], reads=[QK[qi]], writes=[kres[i]])
                    vw = mx["vw"]
                    P.dma(SP, vres[:, :, 0:vw], mx["V"][:, :].rearrange("(s p) c -> p s c", p=128),
                          reads=[mx["V"]], writes=[vres])
                    jobs = []
                    for (t0, qw, isctx) in qblocks:
                        qs = qts[cnt["q"] % 2]
                        cnt["q"] += 1
                        qload = []
                        for hd_ in mx["heads"]:
                            qload.append((qs[hd_["hi"]], hd_["r0"], hd_["r1"], mx["qtiles"][hd_["qt"]]))
                        for hd in mx["heads"]:
                            if isctx:
                                kts = [(0, None), (1, None)]
                            elif mx["name"] == "swa":
                                qb = (t0 - LC) // 128
                                kts = [(0, None), (1, None)]
                                for r in range(-1, 5):
                                    kt_ = qb + r
                                    if 0 <= kt_ < 32:
                                        kts.append((2 + kt_, r + 1))
                            else:
                                kts = [(i, None) for i in range(NKT)]
                            pairs = [kts[i:i + 2] for i in range(0, len(kts), 2)]
                            for pi, pr in enumerate(pairs):
                                jobs.append(dict(t0=t0, qw=qw, hd=hd, pr=pr, first=(pi == 0), last=(pi == len(pairs) - 1),
                                                 qs=qs, qload=qload if (hd is mx["heads"][0] and pi == 0) else None,
                                                 isctx=isctx))

                    def do_S(jb):
                        if jb["qload"] is not None:
                            for (qt_, ra, rb, qi) in jb["qload"]:
                                P.dma(SP, qt_[ra:rb, 0:jb["qw"]], QK[qi][ra:rb, jb["t0"]:jb["t0"] + jb["qw"]],
                                      reads=[QK[qi]], writes=[qt_])
                        hd = jb["hd"]
                        Sp = Sps[cnt["S"] % 2]
                        cnt["S"] += 1
                        jb["Sp"] = Sp
                        qt_ = jb["qs"][hd["hi"]]
                        kr = kres[hd["kt"]]
                        fns = []
                        for i, (kt_, _) in enumerate(jb["pr"]):
                            fns.append(mm(Sp[:, i, 0:jb["qw"]], kr[:, kt_ * 128:(kt_ + 1) * 128],
                                          qt_[:, 0:jb["qw"]]))
                        P.group(PE, fns, reads=[kr, qt_], writes=[Sp])

                    def do_exp(jb):
                        hd = jb["hd"]
                        qw = jb["qw"]
                        npair = len(jb["pr"])
                        pt = pts[cnt["pt"] % 3]
                        cnt["pt"] += 1
                        jb["pt"] = pt
                        Sp = jb["Sp"]
                        if qw == 512:
                            P.op(ACT, act(pt[:, 0:npair, :].rearrange("p a b -> p (a b)"),
                                          Sp[:, 0:npair, :].rearrange("p a b -> p (a b)"), AF.Exp, scale=hd["scale"]),
                                 reads=[Sp], writes=[pt])
                        else:
                            for i in range(npair):
                                P.op(ACT, act(pt[:, i, 0:qw], Sp[:, i, 0:qw], AF.Exp, scale=hd["scale"]),
                                     reads=[Sp], writes=[pt])
                        for i, (kt_, mi) in enumerate(jb["pr"]):
                            if mi is not None:
                                P.op(DVE, tt(pt[:, i, 0:qw], pt[:, i, 0:qw], msk[:, mi, 0:qw], ALU.mult),
                                     reads=[pt, msk], writes=[pt])

                    def do_pv(jb):
                        hd = jb["hd"]
                        qw = jb["qw"]
                        npair = len(jb["pr"])
                        pt = jb["pt"]
                        if jb["first"]:
                            cnt["O"] += 1
                        Op = Ops[cnt["O"] % 2]
                        vh = hd["vh"]
                        fns = []
                        for i, (kt_, _) in enumerate(jb["pr"]):
                            fns.append(mm(Op[0:65, 0:qw], vres[:, kt_, vh * 65:(vh + 1) * 65], pt[:, i, 0:qw],
                                          start=(jb["first"] and i == 0), stop=(jb["last"] and i == npair - 1)))
                        P.group(PE, fns, reads=[vres, pt], writes=[Op])
                        if jb["last"]:
                            finalize(jb, Op)

                    deferred = []

                    def tick():
                        ready = []
                        for it in deferred:
                            it[0] -= 1
                        while deferred and deferred[0][0] <= 0:
                            ready.append(deferred.pop(0)[1])
                        for fn_ in ready:
                            fn_()

                    def finalize(jb, Op):
                        hd = jb["hd"]
                        qw = jb["qw"]
                        t0 = jb["t0"]
                        par = cnt["O"] % 2
                        ob_ = osb[par]
                        rc = rec[par]
                        P.op(DVE, cp(ob_[0:65, 0:qw], Op[0:65, 0:qw]), reads=[Op], writes=[ob_])
                        f0 = hd["feat"]

                        def stage_b():
                            P.op(PE, mm(BCp[0:64, 0:qw], sel[:, 0:64], ob_[:, 0:qw]), reads=[sel, ob_], writes=[BCp])
                            if "sink" in hd:
                                c_ = so + 2 + hd["sink"]
                                P.op(DVE, ts1(rc[0:64, 0:qw], BCp[0:64, 0:qw], small[0:64, c_:c_ + 1], ALU.add),
                                     reads=[BCp, small], writes=[rc])
                                P.op(DVE, lambda e, o=rc[0:64, 0:qw]: e.reciprocal(o, o), reads=[rc], writes=[rc])
                            else:
                                P.op(DVE, lambda e, o=rc[0:64, 0:qw], i=BCp[0:64, 0:qw]: e.reciprocal(o, i),
                                     reads=[BCp], writes=[rc])
                            if "diff" not in hd:
                                ob2 = outb[cnt["ob"] % 3]
                                cnt["ob"] += 1
                                P.op(DVE, tt(ob2[0:64, 0:qw], ob_[0:64, 0:qw], rc[0:64, 0:qw], ALU.mult),
                                     reads=[ob_, rc], writes=[ob2])
                                P.dma(SP, MIX[f0:f0 + 64, t0:t0 + qw], ob2[0:64, 0:qw], reads=[ob2], writes=[MIX])
                            elif hd["diff"] == 0:
                                P.op(DVE, tt(a1[0:64, 0:qw], ob_[0:64, 0:qw], rc[0:64, 0:qw], ALU.mult),
                                     reads=[ob_, rc], writes=[a1])
                            else:
                                P.op(DVE, stt(od[0:64, 0:qw], ob_[0:64, 0:qw], small[0:64, so:so + 1], rc[0:64, 0:qw],
                                              ALU.mult, ALU.mult), reads=[ob_, small, rc], writes=[od])
                                P.op(DVE, tt(od[0:64, 0:qw], od[0:64, 0:qw], a1[0:64, 0:qw], ALU.add),
                                     reads=[od, a1], writes=[od])
                                P.op(POOL, tt(osq[0:64, 0:qw], od[0:64, 0:qw], od[0:64, 0:qw], ALU.mult),
                                     reads=[od], writes=[osq])
                                deferred.append([2, stage_c])

                        def stage_c():
                            P.op(PE, mm(SSp[0:64, 0:qw], ones_b[0:64, 0:64], osq[0:64, 0:qw]),
                                 reads=[ones_b, osq], writes=[SSp])
                            P.op(ACT, act(rs2[0:64, 0:qw], SSp[0:64, 0:qw], AF.Ln, bias=eps_t[0:64, 0:1], scale=1.0 / 64),
                                 reads=[SSp, eps_t], writes=[rs2])
                            P.op(ACT, act(rs2[0:64, 0:qw], rs2[0:64, 0:qw], AF.Exp, scale=-0.5), reads=[rs2], writes=[rs2])
                            ob2 = outb[cnt["ob"] % 3]
                            cnt["ob"] += 1
                            P.op(DVE, stt(ob2[0:64, 0:qw], od[0:64, 0:qw], small[0:64, so + 1:so + 2], rs2[0:64, 0:qw],
                                          ALU.mult, ALU.mult), reads=[od, small, rs2], writes=[ob2])
                            P.dma(SP, MIX[f0:f0 + 64, t0:t0 + qw], ob2[0:64, 0:qw], reads=[ob2], writes=[MIX])

                        deferred.append([2, stage_b])

                    if jobs:
                        do_S(jobs[0])
                        if len(jobs) > 1:
                            do_S(jobs[1])
                        for ji, jb in enumerate(jobs):
                            do_exp(jb)
                            if ji + 2 < len(jobs):
                                do_S(jobs[ji + 2])
                            tick()
                            do_pv(jb)
                        for _ in range(4):
                            tick()
                    flush()
                P.barrier()
                flush()
            if stop_after == ("A", l):
                break

            with ExitStack() as es:
                def sb(name, shape, dt):
                    return T(es.enter_context(sbuf_t(name, shape, dt))[:], name)

                def ps(name, shape, dt=F32):
                    return T(es.enter_context(psum_t(name, shape, dt))[:], name, psum=True)

                wo_h = es.enter_context(sbuf_t("wo", [128, 8, D], BF16))
                wo = [T(wo_h[:, kt, :], "wo%d" % kt) for kt in range(8)]
                for kt in range(8):
                    P.dma(POOL, wo[kt][:], wout_d[l, kt * 128:(kt + 1) * 128, :], writes=[wo[kt]])
                xts = [sb("xo%d" % i, [128, 8, 512], F32) for i in range(2)]
                mxs = [sb("mx%d" % i, [128, 8, 512], BF16) for i in range(2)]
                ysb_h = es.enter_context(sbuf_t("ysb", [128, 8, 512], F32))
                ysb = [T(ysb_h[:, dt, :], "ysb") for dt in range(8)]
                ysq_h = es.enter_context(sbuf_t("ysq", [128, 8, 512], BF16))
                ysq = [T(ysq_h[:, dt, :], "ysq") for dt in range(8)]
                rstd = sb("rstdo", [128, 512], F32)
                tmpf = [sb("tmpo%d" % i, [128, 512], F32) for i in range(3)]
                pp = [ps("po%d" % i, [128, 512]) for i in range(6)]
                blocks = [(LC + i * 512, 512, 0) for i in range(8)]
                if need_ctx:
                    blocks = [(0, LC, 1)] + blocks
                k = 0
                for bi, (t0, tw, who) in enumerate(blocks):
                    xt = xts[bi % 2]
                    mx_ = mxs[bi % 2]
                    P.dma(SP, xt[:, :, 0:tw], X0[:, t0:t0 + tw].rearrange("(k p) t -> p k t", p=128),
                          reads=[X0], writes=[xt])
                    P.dma(SP, mx_[:, :, 0:tw], MIX[:, t0:t0 + tw].rearrange("(k p) t -> p k t", p=128),
                          reads=[MIX], writes=[mx_])
                    for dt in range(8):
                        p_ = pp[k % 5]
                        k += 1
                        P.group(PE, [mm(p_[:, 0:tw], wo[mt][:, dt * 128:(dt + 1) * 128], mx_[:, mt, 0:tw],
                                        start=(mt == 0), stop=(mt == 7)) for mt in range(8)],
                                reads=wo + [mx_], writes=[p_])
                        P.op(ACT, lambda e, o=ysq[dt][:, 0:tw], i=p_[:, 0:tw]: e.square(o, i), reads=[p_], writes=[ysq[dt]])
                        P.op(DVE, cp(ysb[dt][:, 0:tw], p_[:, 0:tw]), reads=[p_], writes=[ysb[dt]])
                    pn = pp[5]
                    P.group(PE, [mm(pn[:, 0:tw], ones_b[:], ysq[dt][:, 0:tw], start=(dt == 0), stop=(dt == 7))
                                 for dt in range(8)], reads=ysq + [ones_b], writes=[pn])
                    P.op(ACT, act(rstd[:, 0:tw], pn[:, 0:tw], AF.Ln, bias=eps_t[:, 0:1], scale=1.0 / D),
                         reads=[pn, eps_t], writes=[rstd])
                    P.op(ACT, act(rstd[:, 0:tw], rstd[:, 0:tw], AF.Exp, scale=-0.5), reads=[rstd], writes=[rstd])
                    G2 = MV(l, who, 1)
                    for dt in range(8):
                        tm = tmpf[dt % 3]
                        P.op(DVE, stt(tm[:, 0:tw], ysb[dt][:, 0:tw], G2[:, dt:dt + 1], rstd[:, 0:tw], ALU.mult, ALU.mult),
                             reads=[ysb[dt], mv, rstd], writes=[tm])
                        P.op(DVE, tt(xt[:, dt, 0:tw], xt[:, dt, 0:tw], tm[:, 0:tw], ALU.add), reads=[tm, xt], writes=[xt])
                    P.dma(SP, X1[:, t0:t0 + tw].rearrange("(k p) t -> p k t", p=128), xt[:, :, 0:tw],
                          reads=[xt], writes=[X1])
                P.barrier()
                flush()
            if stop_after == ("O", l):
                break

            with ExitStack() as es:
                def sb(name, shape, dt):
                    return T(es.enter_context(sbuf_t(name, shape, dt))[:], name)

                def ps(name, shape, dt=F32):
                    return T(es.enter_context(psum_t(name, shape, dt))[:], name, psum=True)

                CW = 1024
                xt = sb("xf", [128, 8, CW + 2], F32)
                fsb_v = [T(xt[:, dt, 0:512], "fsb") for dt in range(8)]
                xre_v = T(xt[:, :, 512:1024], "xre")
                sq_h = es.enter_context(sbuf_t("sqf", [128, 8, CW + 2], BF16))
                sqf = [T(sq_h[:, kt, :], "sqf") for kt in range(8)]
                hT_h = es.enter_context(sbuf_t("hTf", [128, 8, CW + 2], BF16))
                hTf = [T(hT_h[:, kt, :], "hTf") for kt in range(8)]
                g_h = es.enter_context(sbuf_t("gf", [128, NCT, CW], BF16))
                gf = [T(g_h[:, ct, :], "gf") for ct in range(NCT)]
                rstd = sb("rstdf", [128, CW + 2], F32)
                tmpf = [sb("tmpff%d" % i, [128, 512], F32) for i in range(3)]
                U = [[sb("U%d_%d" % (i, j), [128, CW + 2], F32) for j in range(2)] for i in range(2)]
                Y = [[sb("Y%d_%d" % (i, j), [128, CW], F32) for j in range(2)] for i in range(2)]
                wup = [sb("wup%d" % i, [128, 8, 256], BF16) for i in range(4)]
                wdn = [sb("wdn%d" % i, [128, NCT, 128], BF16) for i in range(3)]
                pp = [ps("pf%d" % i, [128, 512]) for i in range(8)]
                ppi = [0]

                def nxt():
                    ppi[0] = (ppi[0] + 1) % 8
                    return pp[ppi[0]]

                chunks = [(LC + i * CW, CW, 0) for i in range(S // CW)]
                if need_ctx:
                    chunks = [(0, LC, 1)] + chunks
                prev_done = []
                nw = 0
                nd = 0
                for (t0, cw, who) in chunks:
                    lo = 0 if who == 1 else LC
                    hi = LC if who == 1 else TT
                    hasl = t0 - 1 >= lo
                    hasr = t0 + cw < hi
                    a0 = t0 - 1 if hasl else t0
                    a1_ = t0 + cw + 1 if hasr else t0 + cw
                    c0 = 0 if hasl else 1
                    c1 = c0 + (a1_ - a0)
                    P.dma(SP, xt[:, :, c0:c1], X1[:, a0:a1_].rearrange("(k p) t -> p k t", p=128),
                          reads=[X1], writes=[xt], extra=prev_done)
                    pieces = [(s_, min(s_ + 512, c1)) for s_ in range(c0, c1, 512)]
                    xre = T(xt[:, :, 512:512 + min(512, cw)], "xre")
                    for kt in range(8):
                        if kt % 2 == 0:
                            P.op(ACT, lambda e, o=sqf[kt][:, c0:c1], i=xt[:, kt, c0:c1]: e.square(o, i),
                                 reads=[xt], writes=[sqf[kt]])
                        else:
                            P.op(POOL, tt(sqf[kt][:, c0:c1], xt[:, kt, c0:c1], xt[:, kt, c0:c1], ALU.mult),
                                 reads=[xt], writes=[sqf[kt]])
                    for (s_, e_) in pieces:
                        pn = nxt()
                        P.group(PE, [mm(pn[:, 0:e_ - s_], ones_b[:], sqf[kt][:, s_:e_], start=(kt == 0), stop=(kt == 7))
                                     for kt in range(8)], reads=sqf + [ones_b], writes=[pn])
                        P.op(ACT, act(rstd[:, s_:e_], pn[:, 0:e_ - s_], AF.Ln, bias=eps_t[:, 0:1], scale=1.0 / D),
                             reads=[pn, eps_t], writes=[rstd])
                    P.op(ACT, act(rstd[:, c0:c1], rstd[:, c0:c1], AF.Exp, scale=-0.5), reads=[rstd], writes=[rstd])
                    A4 = MV(l, who, 2)
                    B4 = MOD(l, 3, who)
                    for kt in range(8):
                        for (s_, e_) in pieces:
                            tm = tmpf[(kt + s_ // 512) % 3]
                            P.op(DVE, stt(tm[:, 0:e_ - s_], xt[:, kt, s_:e_], A4[:, kt:kt + 1], rstd[:, s_:e_],
                                          ALU.mult, ALU.mult), reads=[xt, mv, rstd], writes=[tm])
                            P.op(ACT, act(hTf[kt][:, s_:e_], tm[:, 0:e_ - s_], AF.Identity, bias=B4[:, kt:kt + 1]),
                                 reads=[tm, modsb], writes=[hTf[kt]])
                        if not hasl:
                            P.op(POOL, memset(hTf[kt][:, 0:1], 0.0), writes=[hTf[kt]])
                        if not hasr:
                            P.op(POOL, memset(hTf[kt][:, cw + 1:cw + 2], 0.0), writes=[hTf[kt]])
                    mpieces = [(s_, min(s_ + 512, cw + 2)) for s_ in range(0, cw + 2, 512)]
                    pend_gate = None
                    for ct in range(NCT):
                        w_ = wup[nw % 4]
                        nw += 1
                        P.dma(SP if ct % 2 == 0 else ACT, w_[:], WUPB[l][ct].rearrange("p (k c) -> p k c", c=256),
                              reads=[WUPB[l]], writes=[w_])
                        ys = []
                        for part in range(2):
                            u = U[ct % 2][part]
                            for (s_, e_) in mpieces:
                                p_ = nxt()
                                P.group(PE, [mm(p_[:, 0:e_ - s_], w_[:, kt, part * 128:(part + 1) * 128], hTf[kt][:, s_:e_],
                                                start=(kt == 0), stop=(kt == 7)) for kt in range(8)],
                                        reads=[w_] + hTf, writes=[p_])
                                P.op(ACT, acp(u[:, s_:e_], p_[:, 0:e_ - s_]), reads=[p_], writes=[u])
                            y = Y[ct % 2][part]
                            ti = part * NCT + ct
                            cwv = V("cw", l)
                            cbv = V("cb", l)
                            eng = DVE
                            P.op(ACT, act(y[:, 0:cw], u[:, 0:cw], AF.Identity, bias=cbv[:, ti:ti + 1], scale=cwv[:, ti:ti + 1]),
                                 reads=[u, vecs], writes=[y])
                            P.op(eng, stt(y[:, 0:cw], u[:, 1:cw + 1], cwv[:, 44 + ti:44 + ti + 1], y[:, 0:cw], ALU.mult, ALU.add),
                                 reads=[u, vecs, y], writes=[y])
                            P.op(eng, stt(y[:, 0:cw], u[:, 2:cw + 2], cwv[:, 88 + ti:88 + ti + 1], y[:, 0:cw], ALU.mult, ALU.add),
                                 reads=[u, vecs, y], writes=[y])
                            ys.append(y)
                        if pend_gate is not None:
                            pend_gate()

                        def gate(ct=ct, ys=ys, cw=cw):
                            ya, yv = ys
                            sa = U[ct % 2][0]
                            P.op(ACT, act(sa[:, 0:cw], ya[:, 0:cw], AF.Silu), reads=[ya], writes=[sa])
                            P.op(DVE, tt(gf[ct][:, 0:cw], sa[:, 0:cw], yv[:, 0:cw], ALU.mult), reads=[sa, yv], writes=[gf[ct]])
                        pend_gate = gate
                    pend_gate()
                    G5 = MV(l, who, 3)
                    done = []
                    for s_ in range(0, cw, 512):
                        hw_ = min(512, cw - s_)
                        tx = P.dma(SP, xre[:], X1[:, t0 + s_:t0 + s_ + hw_].rearrange("(k p) t -> p k t", p=128),
                                   reads=[X1] + hTf, writes=[xre], extra=[xt.w] + list(xt.r))
                        for dt in range(8):
                            wd = wdn[nd % 3]
                            nd += 1
                            P.dma(ACT if dt % 2 == 0 else SP, wd[:], WDNB[l][dt].rearrange("p (c k) -> p c k", k=128),
                                  reads=[WDNB[l]], writes=[wd])
                            p_ = nxt()
                            P.group(PE, [mm(p_[:, 0:hw_], wd[:, ct, :], gf[ct][:, s_:s_ + hw_], start=(ct == 0), stop=(ct == NCT - 1))
                                         for ct in range(NCT)], reads=[wd] + gf, writes=[p_])
                            P.op(ACT, lambda e, o=sqf[dt][:, 0:hw_], i=p_[:, 0:hw_]: e.square(o, i), reads=[p_], writes=[sqf[dt]])
                            P.op(DVE, cp(fsb_v[dt][:, 0:hw_], p_[:, 0:hw_]), reads=[p_], writes=[fsb_v[dt]],
                                 extra=[xt.w] + list(xt.r))
                        pn = nxt()
                        P.group(PE, [mm(pn[:, 0:hw_], ones_b[:], sqf[dt][:, 0:hw_], start=(dt == 0), stop=(dt == 7))
                                     for dt in range(8)], reads=sqf + [ones_b], writes=[pn])
                        P.op(ACT, act(rstd[:, 0:hw_], pn[:, 0:hw_], AF.Ln, bias=eps_t[:, 0:1], scale=1.0 / D),
                             reads=[pn, eps_t], writes=[rstd])
                        P.op(ACT, act(rstd[:, 0:hw_], rstd[:, 0:hw_], AF.Exp, scale=-0.5), reads=[rstd], writes=[rstd])
                        for dt in range(8):
                            tm = tmpf[dt % 3]
                            P.op(DVE, stt(tm[:, 0:hw_], fsb_v[dt][:, 0:hw_], G5[:, dt:dt + 1], rstd[:, 0:hw_], ALU.mult, ALU.mult),
                                 reads=[fsb_v[dt], mv, rstd], writes=[tm])
                            P.op(DVE, tt(xre[:, dt, :], xre[:, dt, :], tm[:, 0:hw_], ALU.add), reads=[tm, xre], writes=[xre])
                        if X2 is YT:
                            dst = X2[:, t0 - LC + s_:t0 - LC + s_ + hw_]
                        else:
                            dst = X2[:, t0 + s_:t0 + s_ + hw_]
                        tk = P.dma(SP, dst.rearrange("(k p) t -> p k t", p=128), xre[:], reads=[xre], writes=[X2])
                        done.append(tk)
                        if X2 is YT:
                            out_toks.append(tk)
                        done += [f.w for f in fsb_v if f.w is not None]
                        for f in fsb_v:
                            done += list(f.r)
                    prev_done = [d for d in done if d is not None]
                P.barrier()
                flush()
            if stop_after == ("F", l):
                break

        P.maxops = None
        P.barrier()
        for (name, ap, o, shape) in dbg_list:
            rows = shape[0]
            for r0 in range(0, rows, 128):
                r1 = min(rows, r0 + 128)
                P.dma(POOL, o[r0:r1], ap[r0:r1], max_dma_last_dim=2048)
        P.barrier()
        flush()
    return nc


def make_inputs(inp):
    f32 = np.float32
    g = lambda n: np.asarray(inp[n], f32)
    x, c, ctx, c_ctx = g("x"), g("c"), g("ctx"), g("c_ctx")
    B = x.shape[0]
    w_in = g("w_in")
    w_inx = np.ascontiguousarray(w_in[:, :, EXT_COLS])
    w_uq = g("mla_w_uq")
    cols = []
    for h in range(4):
        a = list(range(h * 96, (h + 1) * 96))
        b = a[:64] + rope_perm_cols(a[64:], 32)
        cols += a + b
    w_uqx = np.ascontiguousarray(w_uq[:, :, cols])
    w_ukv = g("mla_w_ukv")
    kc, vc = [], []
    for h in range(4):
        kc += list(range(h * 128, h * 128 + 64))
        vc += list(range(h * 128 + 64, h * 128 + 128))
    w_ukvx = np.ascontiguousarray(w_ukv[:, :, kc + vc])

    rep = lambda v: np.broadcast_to(v[None, :], (128, v.shape[0]))
    p64 = np.arange(128) % 64
    common = np.zeros((128, NV), f32)
    for l in range(DEPTH):
        def put(name, arr):
            o = VOFF[(name, l)]
            arr = np.asarray(arr, f32)
            if arr.ndim == 1:
                arr = arr[:, None]
            common[:, o:o + arr.shape[1]] = arr
        put("gpre", fm(g("g_mix_pre")[l]))
        put("gpost", fm(g("g_mix_post")[l]))
        put("gfpre", fm(g("g_ffn_pre")[l]))
        put("gfpost", fm(g("g_ffn_post")[l]))
        put("bmod", fm(g("b_mod")[l]))
        put("lq1", rep(g("diff_lam_q1")[l]))
        put("lk1", rep(g("diff_lam_k1")[l]))
        put("lq2", rep(g("diff_lam_q2")[l]))
        put("lk2", rep(g("diff_lam_k2")[l]))
        put("gsub", g("diff_g_sub")[l][p64])
        put("sink", rep(g("swa_sink")[l]))
        mgq = np.zeros((128, 2), f32)
        mgq[:, 0] = g("mla_g_q")[l][:128]
        mgq[:64, 1] = g("mla_g_q")[l][128:192]
        put("mgq", mgq)
        put("mgkv", g("mla_g_kv")[l])
        put("ggq", g("gqa_g_q")[l][p64])
        put("ggqp", g("gqa_g_q")[l][p64 ^ 16])
        put("ggk", g("gqa_g_k")[l][p64])
        put("ggkp", g("gqa_g_k")[l][p64 ^ 16])
        cw = g("ffn_conv_w")[l]
        put("cw", np.concatenate([fm(cw[j]) for j in range(3)], axis=1))
        put("cb", fm(g("ffn_conv_b")[l]))
    rope = rope_tables()
    masks = swa_masks()
    shared = {"rope": rope, "masks": masks, "w_mod": g("w_mod"), "w_inx": w_inx, "w_uqx": w_uqx,
              "w_ukvx": w_ukvx, "w_out": g("w_out"), "w_up": g("ffn_w_up"), "w_dn": g("ffn_w_down")}
    maps = []
    for b in range(B):
        v = common.copy()
        cin = np.zeros((128, 8, 2), f32)
        cin[:, :, 0] = fm(c[b])
        cin[:, :, 1] = fm(c_ctx)
        v[:, VOFF["cin"]:VOFF["cin"] + 16] = cin.reshape(128, 16)
        xin = np.ascontiguousarray(np.concatenate([ctx[b].T, x[b].T], axis=1))
        m = dict(shared)
        m["xin"] = xin
        m["vecs"] = v
        maps.append(m)
    return maps


_NC_CACHE = {}


def kernel(**inputs):
    maps = make_inputs(inputs)
    if "nc" not in _NC_CACHE:
        _NC_CACHE["nc"] = build()
    nc = _NC_CACHE["nc"]
    res = run_bass_kernel_spmd(nc, maps, core_ids=list(range(len(maps))))
    out = np.stack([np.ascontiguousarray(r["yT"].T) for r in res.results], axis=0)
    return out.astype(np.float32)
```

```python
import math
from contextlib import ExitStack

import numpy as np
import ml_dtypes

import concourse.bass as bass
import concourse.mybir as mybir
from concourse.bass_utils import run_bass_kernel_spmd

F32 = mybir.dt.float32
BF16 = mybir.dt.bfloat16
ALU = mybir.AluOpType
AF = mybir.ActivationFunctionType

PE, ACT, DVE, POOL, SP = "pe", "act", "dve", "pool", "sp"

D = 1024
S = 4096
LC = 256
TT = S + LC
NKT = TT // 128
DEPTH = 2
EPS = 1e-6
FFN = 2816
NCT = FFN // 128
NEXT = 3072 + 512


class T:
    def __init__(self, ap, name="", psum=False):
        self.ap = ap
        self.name = name
        self.psum = psum
        self.w = None
        self.r = []

    def __getitem__(self, idx):
        return self.ap[idx]


class Prog:
    def __init__(self, nc, n_dma_sems=8):
        self.nc = nc
        self.engs = (PE, ACT, DVE, POOL, SP)
        self.ops = {e: [] for e in self.engs}
        self.cnt = {("e", e): 0 for e in (PE, ACT, DVE, POOL)}
        self.seen = {e: {} for e in self.engs}
        self.n_dma_sems = n_dma_sems
        self.dma_rr = {e: 0 for e in self.engs}
        self.dma_last = {}
        self.nops = 0
        self.maxops = None

    def all_keys(self):
        keys = [("e", e) for e in (PE, ACT, DVE, POOL)]
        for q in (SP, POOL, ACT):
            for i in range(self.n_dma_sems):
                keys.append(("d", q, i))
        return keys

    def _waits_for(self, eng, reads, writes, extra=()):
        need = {}

        def add(tok):
            if tok is None:
                return
            k, v = tok
            if need.get(k, 0) < v:
                need[k] = v

        for t in reads:
            add(t.w)
            if t.psum:
                for x in t.r:
                    if x is not None and x[0] != ("e", eng):
                        add(x)
        for t in writes:
            add(t.w)
            for x in t.r:
                add(x)
        for x in extra:
            add(x)
        out = []
        for k, v in need.items():
            if eng == PE and k == ("e", PE):
                continue
            if self.seen[eng].get(k, 0) < v:
                self.seen[eng][k] = v
                out.append((k, v))
        return out

    def _reg(self, tok, reads, writes):
        for t in reads:
            t.r.append(tok)
            if len(t.r) > 64:
                best = {}
                for (k, v) in t.r:
                    if best.get(k, 0) < v:
                        best[k] = v
                t.r = list(best.items())
        for t in writes:
            t.w = tok
            t.r = []

    def op(self, eng, fn, reads=(), writes=(), extra=()):
        return self.group(eng, [fn], reads, writes, extra)

    def group(self, eng, fns, reads=(), writes=(), extra=()):
        if self.maxops is not None and self.nops >= self.maxops:
            return None
        waits = self._waits_for(eng, reads, writes, extra)
        k = ("e", eng)
        self.cnt[k] += 1
        tok = (k, self.cnt[k])
        n = len(fns)
        for i, fn in enumerate(fns):
            self.ops[eng].append((fn, waits if i == 0 else [], (k, 1) if i == n - 1 else None))
        self.nops += n
        self._reg(tok, reads, writes)
        return tok

    def dma(self, eng, out_ap, in_ap, reads=(), writes=(), extra=(), **kw):
        if self.maxops is not None and self.nops >= self.maxops:
            return None
        i = self.dma_rr[eng]
        self.dma_rr[eng] = (i + 1) % self.n_dma_sems
        k = ("d", eng, i)
        waits = self._waits_for(eng, reads, writes, extra)
        prev = self.dma_last.get(k, 0)
        if prev and self.seen[eng].get(k, 0) < prev:
            self.seen[eng][k] = prev
            waits.append((k, prev))
        val = prev + 16
        self.dma_last[k] = val
        tok = (k, val)

        def fn(e, out_ap=out_ap, in_ap=in_ap, kw=kw):
            return e.dma_start(out=out_ap, in_=in_ap, **kw)

        self.ops[eng].append((fn, waits, (k, 16)))
        self.nops += 1
        self._reg(tok, reads, writes)
        return tok

    def barrier(self):
        toks = [(k, v) for k, v in self.cnt.items() if v > 0]
        toks += [(k, v) for k, v in self.dma_last.items() if v > 0]
        for eng in self.engs:
            waits = []
            for (k, v) in toks:
                if self.seen[eng].get(k, 0) < v:
                    self.seen[eng][k] = v
                    waits.append((k, v))
            if waits:
                self.ops[eng].append((None, waits, None))

    def emit(self, block, sems):
        def run(engname):
            def body(e):
                for (fn, waits, inc) in self.ops[engname]:
                    for (k, v) in waits:
                        e.wait_ge(sems[k], v)
                    if fn is None:
                        continue
                    ins = fn(e)
                    if inc is not None:
                        ins.then_inc(sems[inc[0]], inc[1])
            return body

        block.tensor(run(PE))
        block.scalar(run(ACT))
        block.vector(run(DVE))
        block.gpsimd(run(POOL))
        block.sync(run(SP))
        self.ops = {e: [] for e in self.engs}


def mm(out, lhsT, rhs, start=True, stop=True, **kw):
    return lambda e: e.matmul(out, lhsT, rhs, start=start, stop=stop, **kw)


def act(out, in_, func, bias=None, scale=1.0):
    if bias is None:
        return lambda e: e.activation(out, in_, func, scale=scale)
    return lambda e: e.activation(out, in_, func, bias=bias, scale=scale)


def tt(out, a, b, op):
    return lambda e: e.tensor_tensor(out=out, in0=a, in1=b, op=op)


def ts2(out, a, s1, s2, op0, op1):
    return lambda e: e.tensor_scalar(out, a, s1, s2, op0, op1)


def ts1(out, a, s1, op0):
    return lambda e: e.tensor_scalar(out, a, s1, None, op0)


def stt(out, a, s, b, op0, op1):
    return lambda e: e.scalar_tensor_tensor(out=out, in0=a, scalar=s, in1=b, op0=op0, op1=op1)


def cp(out, a):
    return lambda e: e.tensor_copy(out, a)


def acp(out, a):
    return lambda e: e.copy(out, a)


def memset(out, v):
    return lambda e: e.memset(out, v)


VEC_L = [("gpre", 8), ("gpost", 8), ("gfpre", 8), ("gfpost", 8), ("bmod", 48),
         ("lq1", 32), ("lk1", 32), ("lq2", 32), ("lk2", 32), ("gsub", 1), ("sink", 4),
         ("mgq", 2), ("mgkv", 1), ("ggq", 1), ("ggqp", 1), ("ggk", 1), ("ggkp", 1),
         ("cw", 132), ("cb", 44)]
VEC_G = [("cin", 16)]


def vec_offsets():
    off = {}
    o = 0
    for (n, c) in VEC_G:
        off[n] = o
        o += c
    for l in range(DEPTH):
        for (n, c) in VEC_L:
            off[(n, l)] = o
            o += c
    return off, o


VOFF, NV = vec_offsets()


def fm(v):
    return np.ascontiguousarray(v.reshape(-1, 128).T)


def rope_perm_cols(cols, d):
    n = d // 4
    cols = list(cols)
    out = []
    for i, c in enumerate(cols):
        base = (i // d) * d
        out.append(cols[base + ((i - base) ^ n)])
    return out


def w_in_ext_cols():
    tiles = []
    r = lambda a, b: list(range(a, b))
    dq0, dq1 = r(0, 128), r(128, 256)
    dk0, dk1 = r(256, 384), r(384, 512)
    o1 = 768
    sqh = [r(o1 + h * 64, o1 + (h + 1) * 64) for h in range(4)]
    sq0, sq1 = sqh[0] + sqh[2], sqh[1] + sqh[3]
    sk = r(o1 + 256, o1 + 384)
    o2 = 1280
    mcq0, mcq1 = r(o2, o2 + 128), r(o2 + 128, o2 + 192)
    mckv = r(o2 + 192, o2 + 320)
    kr = r(o2 + 320, o2 + 352)
    mkr = r(o2 + 192, o2 + 256) + kr
    mkrp = r(o2 + 192, o2 + 256) + rope_perm_cols(kr, 32)
    o3 = 1632
    gqh = [r(o3 + h * 64, o3 + (h + 1) * 64) for h in range(4)]
    gq0, gq1 = gqh[0] + gqh[2], gqh[1] + gqh[3]
    gk = r(o3 + 256, o3 + 384)
    P32 = lambda c: rope_perm_cols(c, 32)
    P64 = lambda c: rope_perm_cols(c, 64)
    tiles = [dq0, dq1, P32(dq0), P32(dq1), dk0, dk1, P32(dk0), P32(dk1),
             sq0, sq1, P64(sq0), P64(sq1), sk, P64(sk),
             mcq0, mcq1, mckv, mkr, mkrp,
             gq0, gq1, P64(gq0), P64(gq1), gk, P64(gk)]
    wv = r(512, 768) + r(o1 + 384, o1 + 512) + r(o3 + 384, o3 + 512)
    offs = []
    cols = []
    for t in tiles:
        offs.append((len(cols), len(t)))
        cols += t
    assert len(cols) == 3072
    voff = len(cols)
    cols += wv
    assert len(cols) == NEXT
    return cols, offs, voff


EXT_COLS, EXT_OFFS, EXT_VOFF = w_in_ext_cols()
(T_DQ0, T_DQ1, T_DQ0P, T_DQ1P, T_DK0, T_DK1, T_DK0P, T_DK1P, T_SQ0, T_SQ1, T_SQ0P, T_SQ1P, T_SK, T_SKP,
 T_MCQ0, T_MCQ1, T_MCKV, T_MKR, T_MKRP, T_GQ0, T_GQ1, T_GQ0P, T_GQ1P, T_GK, T_GKP) = range(25)

(Q_DQ0, Q_DQ1, Q_DK0, Q_DK1, Q_SQ0, Q_SQ1, Q_SK, Q_GQ0, Q_GQ1, Q_GK) = range(10)
Q_MQ = 10
Q_MK = 14
NQK = 18


def rope_tables():
    out = np.zeros((4, 128, TT), np.float32)
    s = np.arange(S)
    row = (s // 64).astype(np.float32)
    col = (s % 64).astype(np.float32)
    for fi, d in ((0, 32), (2, 64)):
        n = d // 4
        inv = (np.float32(10000.0) ** (-np.arange(n, dtype=np.float32) / np.float32(n))).astype(np.float32)
        for p in range(128):
            idx = p % d
            axis = idx // (2 * n)
            half = (idx // n) % 2
            i = idx % n
            pos = row if axis == 0 else col
            ang = (pos * inv[i]).astype(np.float32)
            out[fi, p, :LC] = 1.0
            out[fi, p, LC:] = np.cos(ang)
            out[fi + 1, p, :LC] = 0.0
            out[fi + 1, p, LC:] = np.sin(ang) * (-1.0 if half == 0 else 1.0)
    return out


def swa_masks():
    m = np.zeros((6, 128, 512), np.float32)
    kl = np.arange(128)[:, None]
    ql = np.arange(512)[None, :]
    for ri, r in enumerate(range(-1, 5)):
        m[ri] = (np.abs(ql - kl - 128 * r) <= 128).astype(np.float32)
    return m.astype(ml_dtypes.bfloat16)


def build(debug=False, stop_after=None, maxops=None):
    nc = bass.Bass("TRN2", target_bir_lowering=False)
    dk = "ExternalOutput" if debug else "Internal"

    def din(name, shape, dt=F32):
        return nc.dram_tensor(name, shape, dt, kind="ExternalInput").ap()

    dbg_list = []

    def dscr(name, shape, dt):
        ap = nc.dram_tensor(name, shape, dt, kind="Internal").ap()
        if debug and name in debug:
            o = nc.dram_tensor("D_" + name, shape, F32, kind="ExternalOutput").ap()
            dbg_list.append((name, ap, o, shape))
        return ap

    xin = din("xin", [D, TT])
    vecs_d = din("vecs", [128, NV])
    rope_d = din("rope", [4, 128, TT])
    mask_d = din("masks", [6, 128, 512], BF16)
    wmod_d = din("w_mod", [DEPTH, D, 6 * D])
    winx_d = din("w_inx", [DEPTH, D, NEXT])
    wuqx_d = din("w_uqx", [DEPTH, 192, 768])
    wukvx_d = din("w_ukvx", [DEPTH, 128, 512])
    wout_d = din("w_out", [DEPTH, D, D])
    wup_d = din("w_up", [DEPTH, D, 2 * FFN])
    wdn_d = din("w_dn", [DEPTH, FFN, D])
    yT = nc.dram_tensor("yT", [D, S], F32, kind="ExternalOutput").ap()

    XA = T(dscr("XA", [D, TT], F32), "XA")
    XB = T(dscr("XB", [D, TT], F32), "XB")
    XIN = T(xin, "xin")
    YT = T(yT, "yT")
    QK = [T(dscr("QK%d" % i, [128, TT], BF16), "QK%d" % i) for i in range(NQK)]
    VD = T(dscr("VD", [TT, 260], BF16), "VD")
    VM = T(dscr("VM", [TT, 260], BF16), "VM")
    VS = T(dscr("VS", [TT, 130], BF16), "VS")
    VG = T(dscr("VG", [TT, 130], BF16), "VG")
    MIX = T(dscr("MIX", [D, TT], BF16), "MIX")
    WUPB = [T(dscr("WUPB%d" % l, [NCT, 128, 8 * 256], BF16), "WUPB") for l in range(DEPTH)]
    WDNB = [T(dscr("WDNB%d" % l, [8, 128, NCT * 128], BF16), "WDNB") for l in range(DEPTH)]

    P = Prog(nc, n_dma_sems=16)
    P.maxops = maxops
    out_toks = []
    uid = [0]

    def sbuf_t(name, shape, dt):
        uid[0] += 1
        return nc.sbuf_tensor("%s_u%d" % (name, uid[0]), shape, dt)

    def psum_t(name, shape, dt):
        uid[0] += 1
        return nc.psum_tensor("%s_u%d" % (name, uid[0]), shape, dt)

    with ExitStack() as es0:
        sems = {k: es0.enter_context(nc.semaphore("s%d" % i)) for i, k in enumerate(P.all_keys())}

        def flush():
            with nc.Block() as block:
                P.emit(block, sems)

        def sb0(name, shape, dt):
            return T(es0.enter_context(sbuf_t(name, shape, dt))[:], name)

        vecs = sb0("vecs", [128, NV], F32)
        mv = sb0("mv", [128, DEPTH * 2 * 4 * 8], F32)
        modsb = sb0("modsb", [128, DEPTH * 96], F32)
        ones_b = sb0("ones_b", [128, 128], BF16)
        bd64 = sb0("bd64", [128, 128], BF16)
        sel = sb0("sel", [128, 64], F32)
        eps_t = sb0("eps_t", [128, 1], F32)
        small = sb0("small", [128, 16 * DEPTH], F32)

        def MV(l, who, kind):
            o = ((l * 2 + who) * 4 + kind) * 8
            return mv[:, o:o + 8]

        def MOD(l, j, who):
            base = l * 96
            return modsb[:, base:base + 96].rearrange("p (i w) -> p i w", w=2)[:, j * 8:(j + 1) * 8, who]

        def V(name, l=None, n=None):
            o = VOFF[name] if l is None else VOFF[(name, l)]
            w = n if n is not None else dict(VEC_L + VEC_G)[name]
            return vecs[:, o:o + w]

        with ExitStack() as es:
            def sb(name, shape, dt):
                return T(es.enter_context(sbuf_t(name, shape, dt))[:], name)

            def ps(name, shape, dt=F32):
                return T(es.enter_context(psum_t(name, shape, dt))[:], name, psum=True)

            P.dma(SP, vecs[:], vecs_d, writes=[vecs])
            P.op(DVE, memset(ones_b[:], 1.0), writes=[ones_b])
            P.op(DVE, memset(bd64[:], 0.0), writes=[bd64])
            P.op(DVE, memset(bd64[0:64, 0:64], 1.0), writes=[bd64])
            P.op(DVE, memset(bd64[64:128, 64:128], 1.0), writes=[bd64])
            P.op(DVE, memset(sel[:], 0.0), writes=[sel])
            P.op(DVE, memset(sel[64:65, :], 1.0), writes=[sel])
            P.op(DVE, memset(eps_t[:], EPS), writes=[eps_t])

            sil = sb("sil", [128, 16], F32)
            P.op(ACT, act(sil[:], V("cin"), AF.Silu), reads=[vecs], writes=[sil])
            wmt = [sb("wmt%d" % i, [128, 8, 512], F32) for i in range(2)]
            pm = ps("pm", [128, 96])
            for l in range(DEPTH):
                for cb in range(12):
                    w = wmt[(l * 12 + cb) % 2]
                    P.dma(SP if cb % 2 == 0 else ACT, w[:],
                          wmod_d[l, :, cb * 512:(cb + 1) * 512].rearrange("(k p) c -> p k c", p=128), writes=[w])
                    fns = []
                    for sub in range(4):
                        idx = cb * 4 + sub
                        for kt in range(8):
                            fns.append(mm(pm[:, idx * 2:idx * 2 + 2], w[:, kt, sub * 128:(sub + 1) * 128],
                                          sil[:, kt * 2:kt * 2 + 2], start=(kt == 0), stop=(kt == 7)))
                    P.group(PE, fns, reads=[w, sil], writes=[pm])
                mo = modsb[:, l * 96:(l + 1) * 96].rearrange("p (i w) -> p i w", w=2)
                pmv = pm[:, :].rearrange("p (i w) -> p i w", w=2)
                for who in range(2):
                    P.op(DVE, tt(mo[:, :, who], pmv[:, :, who], V("bmod", l), ALU.add),
                         reads=[pm, vecs], writes=[modsb])
                for who in range(2):
                    P.op(DVE, stt(MV(l, who, 0), MOD(l, 1, who), 1.0, V("gpre", l), ALU.add, ALU.mult),
                         reads=[modsb, vecs], writes=[mv])
                    P.op(DVE, tt(MV(l, who, 1), MOD(l, 2, who), V("gpost", l), ALU.mult),
                         reads=[modsb, vecs], writes=[mv])
                    P.op(DVE, stt(MV(l, who, 2), MOD(l, 4, who), 1.0, V("gfpre", l), ALU.add, ALU.mult),
                         reads=[modsb, vecs], writes=[mv])
                    P.op(DVE, tt(MV(l, who, 3), MOD(l, 5, who), V("gfpost", l), ALU.mult),
                         reads=[modsb, vecs], writes=[mv])
                lam_init = 0.8 - 0.6 * math.exp(-0.3 * l)
                so = l * 16
                tmp32 = sb("tmp32_%d" % l, [128, 32], F32)
                acc = sb("acc_%d" % l, [128, 4], F32)
                P.op(DVE, tt(tmp32[:], V("lq1", l), V("lk1", l), ALU.mult), reads=[vecs], writes=[tmp32])
                P.op(DVE, lambda e, a=acc, t=tmp32: e.reduce_sum(a[:, 0:1], t[:], mybir.AxisListType.X),
                     reads=[tmp32], writes=[acc])
                P.op(DVE, tt(tmp32[:], V("lq2", l), V("lk2", l), ALU.mult), reads=[vecs, acc], writes=[tmp32])
                P.op(DVE, lambda e, a=acc, t=tmp32: e.reduce_sum(a[:, 1:2], t[:], mybir.AxisListType.X),
                     reads=[tmp32], writes=[acc])
                P.op(ACT, act(acc[:, 2:4], acc[:, 0:2], AF.Exp), reads=[acc], writes=[acc])
                P.op(DVE, stt(small[:, so:so + 1], acc[:, 3:4], -lam_init, acc[:, 2:3], ALU.add, ALU.subtract),
                     reads=[acc], writes=[small])
                P.op(DVE, ts1(small[:, so + 1:so + 2], V("gsub", l), 1.0 - lam_init, ALU.mult),
                     reads=[vecs], writes=[small])
                P.op(ACT, act(small[:, so + 2:so + 6], V("sink", l), AF.Exp), reads=[vecs], writes=[small])

            stg = [sb("stg%d" % i, [128, 8, 256], BF16) for i in range(6)]
            n = 0
            for l in range(DEPTH):
                for ct in range(NCT):
                    s_ = stg[n % 6]
                    n += 1
                    for part in range(2):
                        c0 = part * FFN + ct * 128
                        P.dma(POOL, s_[:, :, part * 128:(part + 1) * 128],
                              wup_d[l, :, c0:c0 + 128].rearrange("(k p) c -> p k c", p=128), writes=[s_])
                    P.dma(SP, WUPB[l][ct].rearrange("p (k c) -> p k c", c=256), s_[:], reads=[s_], writes=[WUPB[l]])
            stg2 = [sb("stgd%d" % i, [128, 1024], BF16) for i in range(6)]
            for l in range(DEPTH):
                for ct in range(NCT):
                    s_ = stg2[n % 6]
                    n += 1
                    P.dma(POOL, s_[:], wdn_d[l, ct * 128:(ct + 1) * 128, :], writes=[s_])
                    P.dma(SP, WDNB[l][:, :, ct * 128:(ct + 1) * 128].rearrange("d p c -> p d c"),
                          s_[:].rearrange("p (d c) -> p d c", c=128), reads=[s_], writes=[WDNB[l]])
            P.barrier()
            flush()

        X_seq = [(XIN, XA, XB), (XB, XA, YT)]
        print("ops after phase0:", P.nops)

        for l in range(DEPTH):
            if stop_after == ("0", 0):
                break
            X0, X1, X2 = X_seq[l]
            need_ctx = l < DEPTH - 1
            with ExitStack() as es:
                def sb(name, shape, dt):
                    return T(es.enter_context(sbuf_t(name, shape, dt))[:], name)

                def ps(name, shape, dt=F32):
                    return T(es.enter_context(psum_t(name, shape, dt))[:], name, psum=True)

                wext_h = es.enter_context(sbuf_t("wext", [128, 8, NEXT], BF16))
                wext = [T(wext_h[:, kt, :], "wext%d" % kt) for kt in range(8)]
                for kt in range(8):
                    for hh in range(2):
                        c0, c1 = hh * (NEXT // 2), (hh + 1) * (NEXT // 2)
                        P.dma(POOL, wext[kt][:, c0:c1], winx_d[l, kt * 128:(kt + 1) * 128, c0:c1], writes=[wext[kt]],
                              max_dma_last_dim=4096)
                wuq0 = sb("wuq0", [128, 768], BF16)
                wuq1 = sb("wuq1", [64, 768], BF16)
                wukv = sb("wukv", [128, 512], BF16)
                P.dma(POOL, wuq0[:], wuqx_d[l, 0:128, :], writes=[wuq0])
                P.dma(POOL, wuq1[:], wuqx_d[l, 128:192, :], writes=[wuq1])
                P.dma(POOL, wukv[:], wukvx_d[l], writes=[wukv])

                xts = [sb("xt%d" % i, [128, 8, 512], F32) for i in range(2)]
                tabs = [sb("tab%d" % i, [128, 4, 512], F32) for i in range(2)]
                sq_h = es.enter_context(sbuf_t("sq", [128, 8, 512], BF16))
                sq = [T(sq_h[:, kt, :], "sq%d" % kt) for kt in range(8)]
                hT_h = [es.enter_context(sbuf_t("hT%d" % i, [128, 8, 512], BF16)) for i in range(2)]
                hTs = [[T(h[:, kt, :], "hT") for kt in range(8)] for h in hT_h]
                rstd = sb("rstd", [128, 512], F32)
                tmpf = [sb("tmpf%d" % i, [128, 512], F32) for i in range(4)]
                ost = [sb("ost%d" % i, [128, 512], BF16) for i in range(6)]
                sqs = [sb("sqs%d" % i, [128, 512], BF16) for i in range(2)]
                nrm = [sb("nrm%d" % i, [128, 512], F32) for i in range(2)]
                cqn0 = sb("cqn0", [128, 512], BF16)
                cqn1 = sb("cqn1", [64, 512], BF16)
                ckvn = sb("ckvn", [128, 512], BF16)
                krr = sb("krr", [128, 512], BF16)
                vst = [[sb("vst%d_%d" % (i, j), [128, 4, 4 * 65 if j < 2 else 2 * 65], BF16) for j in range(4)]
                       for i in range(2)]
                for i in range(2):
                    for j in range(4):
                        P.op(POOL, memset(vst[i][j][:], 1.0), writes=[vst[i][j]])
                pp = [ps("pp%d" % i, [128, 512]) for i in range(8)]
                ppi = [0]

                def nxt():
                    ppi[0] = (ppi[0] + 1) % 8
                    return pp[ppi[0]]

                tfi = [0]

                def ntmp():
                    tfi[0] = (tfi[0] + 1) % 4
                    return tmpf[tfi[0]]

                osi = [0]

                def nost():
                    osi[0] = (osi[0] + 1) % 6
                    return ost[osi[0]]

                blocks = [(0, LC, 1)] + [(LC + i * 512, 512, 0) for i in range(8)]
                for bi, (t0, tw, who) in enumerate(blocks):
                    xt = xts[bi % 2]
                    tab = tabs[bi % 2]
                    hT = hTs[bi % 2]
                    P.dma(SP, xt[:, :, 0:tw], X0[:, t0:t0 + tw].rearrange("(k p) t -> p k t", p=128),
                          reads=[X0], writes=[xt])
                    P.dma(SP, tab[:, :, 0:tw], rope_d[:, :, t0:t0 + tw].rearrange("f p t -> p f t"), writes=[tab])
                    for kt in range(8):
                        P.op(ACT, lambda e, o=sq[kt][:, 0:tw], i=xt[:, kt, 0:tw]: e.square(o, i),
                             reads=[xt], writes=[sq[kt]])
                    pn = nxt()
                    P.group(PE, [mm(pn[:, 0:tw], ones_b[:], sq[kt][:, 0:tw], start=(kt == 0), stop=(kt == 7))
                                 for kt in range(8)], reads=sq + [ones_b], writes=[pn])
                    P.op(ACT, act(rstd[:, 0:tw], pn[:, 0:tw], AF.Ln, bias=eps_t[:, 0:1], scale=1.0 / D),
                         reads=[pn, eps_t], writes=[rstd])
                    P.op(ACT, act(rstd[:, 0:tw], rstd[:, 0:tw], AF.Exp, scale=-0.5), reads=[rstd], writes=[rstd])
                    A1 = MV(l, who, 0)
                    B1 = MOD(l, 0, who)
                    for kt in range(8):
                        tm = ntmp()
                        P.op(DVE, stt(tm[:, 0:tw], xt[:, kt, 0:tw], A1[:, kt:kt + 1], rstd[:, 0:tw], ALU.mult, ALU.mult),
                             reads=[xt, mv, rstd], writes=[tm])
                        P.op(ACT, act(hT[kt][:, 0:tw], tm[:, 0:tw], AF.Identity, bias=B1[:, kt:kt + 1]),
                             reads=[tm, modsb], writes=[hT[kt]])

                    def proj(ti, rows=None):
                        c0, cw = EXT_OFFS[ti]
                        p_ = nxt()
                        P.group(PE, [mm(p_[0:cw, 0:tw], wext[kt][:, c0:c0 + cw], hT[kt][:, 0:tw],
                                        start=(kt == 0), stop=(kt == 7)) for kt in range(8)],
                                reads=wext + hT, writes=[p_])
                        return p_

                    def store_qk(qi, src, rows=128):
                        P.dma(SP, QK[qi][0:rows, t0:t0 + tw], src[0:rows, 0:tw], reads=[src], writes=[QK[qi]])

                    def rope(pq, ppm, fi, dst, r0=0, r1=128, eng2=DVE):
                        a = ntmp()
                        P.op(DVE, tt(a[r0:r1, 0:tw], pq[r0:r1, 0:tw], tab[r0:r1, fi, 0:tw], ALU.mult),
                             reads=[pq, tab], writes=[a])
                        b = ntmp()
                        P.op(DVE, tt(b[r0:r1, 0:tw], ppm[r0:r1, 0:tw], tab[r0:r1, fi + 1, 0:tw], ALU.mult),
                             reads=[ppm, tab], writes=[b])
                        P.op(eng2, tt(dst[r0:r1, 0:tw], a[r0:r1, 0:tw], b[r0:r1, 0:tw], ALU.add),
                             reads=[a, b], writes=[dst])

                    for (ta, tp, fi, qi) in ((T_DQ0, T_DQ0P, 0, Q_DQ0), (T_DQ1, T_DQ1P, 0, Q_DQ1),
                                             (T_DK0, T_DK0P, 0, Q_DK0), (T_DK1, T_DK1P, 0, Q_DK1),
                                             (T_SQ0, T_SQ0P, 2, Q_SQ0), (T_SQ1, T_SQ1P, 2, Q_SQ1),
                                             (T_SK, T_SKP, 2, Q_SK)):
                        pa = proj(ta)
                        pb = proj(tp)
                        o = nost()
                        rope(pa, pb, fi, o)
                        store_qk(qi, o)
                    for (ta, tp, qi, g, gp) in ((T_GQ0, T_GQ0P, Q_GQ0, "ggq", "ggqp"), (T_GQ1, T_GQ1P, Q_GQ1, "ggq", "ggqp"),
                                                (T_GK, T_GKP, Q_GK, "ggk", "ggkp")):
                        pa = proj(ta)
                        pb = proj(tp)
                        s_ = sqs[0]
                        P.op(ACT, lambda e, o=s_[:, 0:tw], i=pa[:, 0:tw]: e.square(o, i), reads=[pa], writes=[s_])
                        pn = nxt()
                        P.op(PE, mm(pn[:, 0:tw], bd64[:], s_[:, 0:tw]), reads=[bd64, s_], writes=[pn])
                        r_ = nrm[0]
                        P.op(ACT, act(r_[:, 0:tw], pn[:, 0:tw], AF.Ln, bias=eps_t[:, 0:1], scale=1.0 / 64),
                             reads=[pn, eps_t], writes=[r_])
                        P.op(ACT, act(r_[:, 0:tw], r_[:, 0:tw], AF.Exp, scale=-0.5), reads=[r_], writes=[r_])
                        a = ntmp()
                        P.op(DVE, stt(a[:, 0:tw], pa[:, 0:tw], V(g, l), tab[:, 2, 0:tw], ALU.mult, ALU.mult),
                             reads=[pa, vecs, tab], writes=[a])
                        b = ntmp()
                        P.op(DVE, stt(b[:, 0:tw], pb[:, 0:tw], V(gp, l), tab[:, 3, 0:tw], ALU.mult, ALU.mult),
                             reads=[pb, vecs, tab], writes=[b])
                        P.op(DVE, tt(a[:, 0:tw], a[:, 0:tw], b[:, 0:tw], ALU.add), reads=[a, b], writes=[a])
                        o = nost()
                        P.op(DVE, tt(o[:, 0:tw], a[:, 0:tw], r_[:, 0:tw], ALU.mult), reads=[a, r_], writes=[o])
                        store_qk(qi, o)
                    pc0 = proj(T_MCQ0)
                    pc1 = proj(T_MCQ1)
                    P.op(ACT, lambda e, o=sqs[0][:, 0:tw], i=pc0[:, 0:tw]: e.square(o, i), reads=[pc0], writes=[sqs[0]])
                    P.op(ACT, lambda e, o=sqs[1][0:64, 0:tw], i=pc1[0:64, 0:tw]: e.square(o, i), reads=[pc1], writes=[sqs[1]])
                    pn = nxt()
                    P.group(PE, [mm(pn[:, 0:tw], ones_b[:, :], sqs[0][:, 0:tw], start=True, stop=False),
                                 mm(pn[:, 0:tw], ones_b[0:64, :], sqs[1][0:64, 0:tw], start=False, stop=True)],
                            reads=[ones_b, sqs[0], sqs[1]], writes=[pn])
                    r_ = nrm[0]
                    P.op(ACT, act(r_[:, 0:tw], pn[:, 0:tw], AF.Ln, bias=eps_t[:, 0:1], scale=1.0 / 192),
                         reads=[pn, eps_t], writes=[r_])
                    P.op(ACT, act(r_[:, 0:tw], r_[:, 0:tw], AF.Exp, scale=-0.5), reads=[r_], writes=[r_])
                    mg = V("mgq", l)
                    P.op(DVE, stt(cqn0[:, 0:tw], pc0[:, 0:tw], mg[:, 0:1], r_[:, 0:tw], ALU.mult, ALU.mult),
                         reads=[pc0, vecs, r_], writes=[cqn0])
                    P.op(DVE, stt(cqn1[0:64, 0:tw], pc1[0:64, 0:tw], mg[0:64, 1:2], r_[0:64, 0:tw], ALU.mult, ALU.mult),
                         reads=[pc1, vecs, r_], writes=[cqn1])
                    pkv = proj(T_MCKV)
                    P.op(ACT, lambda e, o=sqs[0][:, 0:tw], i=pkv[:, 0:tw]: e.square(o, i), reads=[pkv], writes=[sqs[0]])
                    pn = nxt()
                    P.op(PE, mm(pn[:, 0:tw], ones_b[:], sqs[0][:, 0:tw]), reads=[ones_b, sqs[0]], writes=[pn])
                    r2 = nrm[1]
                    P.op(ACT, act(r2[:, 0:tw], pn[:, 0:tw], AF.Ln, bias=eps_t[:, 0:1], scale=1.0 / 128),
                         reads=[pn, eps_t], writes=[r2])
                    P.op(ACT, act(r2[:, 0:tw], r2[:, 0:tw], AF.Exp, scale=-0.5), reads=[r2], writes=[r2])
                    P.op(DVE, stt(ckvn[:, 0:tw], pkv[:, 0:tw], V("mgkv", l), r2[:, 0:tw], ALU.mult, ALU.mult),
                         reads=[pkv, vecs, r2], writes=[ckvn])
                    pk = proj(T_MKR)
                    pkp = proj(T_MKRP)
                    rope(pk, pkp, 0, krr, 64, 96)
                    for h in range(4):
                        pq = nxt()
                        P.group(PE, [mm(pq[0:96, 0:tw], wuq0[:, h * 192:h * 192 + 96], cqn0[:, 0:tw], start=True, stop=False),
                                     mm(pq[0:96, 0:tw], wuq1[0:64, h * 192:h * 192 + 96], cqn1[0:64, 0:tw], start=False, stop=True)],
                                reads=[wuq0, wuq1, cqn0, cqn1], writes=[pq])
                        pqp = nxt()
                        P.group(PE, [mm(pqp[0:96, 0:tw], wuq0[:, h * 192 + 96:h * 192 + 192], cqn0[:, 0:tw], start=True, stop=False),
                                     mm(pqp[0:96, 0:tw], wuq1[0:64, h * 192 + 96:h * 192 + 192], cqn1[0:64, 0:tw], start=False, stop=True)],
                                reads=[wuq0, wuq1, cqn0, cqn1], writes=[pqp])
                        o = nost()
                        P.op(ACT, acp(o[0:64, 0:tw], pq[0:64, 0:tw]), reads=[pq], writes=[o])
                        rope(pq, pqp, 0, o, 64, 96)
                        store_qk(Q_MQ + h, o, 96)
                        pkn = nxt()
                        P.op(PE, mm(pkn[0:64, 0:tw], wukv[:, h * 64:(h + 1) * 64], ckvn[:, 0:tw]),
                             reads=[wukv, ckvn], writes=[pkn])
                        o2 = nost()
                        P.op(ACT, acp(o2[0:64, 0:tw], pkn[0:64, 0:tw]), reads=[pkn], writes=[o2])
                        P.op(POOL, cp(o2[64:96, 0:tw], krr[64:96, 0:tw]), reads=[krr], writes=[o2])
                        store_qk(Q_MK + h, o2, 96)
                    vs_ = vst[bi % 2]
                    nsub = tw // 128
                    for sub in range(nsub):
                        pv = nxt()
                        P.group(PE, [mm(pv[:, 0:512], hT[kt][:, sub * 128:(sub + 1) * 128],
                                        wext[kt][:, EXT_VOFF:EXT_VOFF + 512], start=(kt == 0), stop=(kt == 7))
                                     for kt in range(8)], reads=wext + hT, writes=[pv])
                        P.op(DVE, cp(vs_[0][:, sub, :].rearrange("p (h c) -> p h c", c=65)[:, :, 0:64],
                                     pv[:, 0:256].rearrange("p (h c) -> p h c", c=64)), reads=[pv], writes=[vs_[0]])
                        P.op(DVE, cp(vs_[2][:, sub, :].rearrange("p (h c) -> p h c", c=65)[:, :, 0:64],
                                     pv[:, 256:384].rearrange("p (h c) -> p h c", c=64)), reads=[pv], writes=[vs_[2]])
                        P.op(DVE, cp(vs_[3][:, sub, :].rearrange("p (h c) -> p h c", c=65)[:, :, 0:64],
                                     pv[:, 384:512].rearrange("p (h c) -> p h c", c=64)), reads=[pv], writes=[vs_[3]])
                        pm_ = nxt()
                        P.op(PE, mm(pm_[:, 0:256], ckvn[:, sub * 128:(sub + 1) * 128], wukv[:, 256:512]),
                             reads=[ckvn, wukv], writes=[pm_])
                        P.op(DVE, cp(vs_[1][:, sub, :].rearrange("p (h c) -> p h c", c=65)[:, :, 0:64],
                                     pm_[:, 0:256].rearrange("p (h c) -> p h c", c=64)), reads=[pm_], writes=[vs_[1]])
                    for (j, VT_) in ((0, VD), (1, VM), (2, VS), (3, VG)):
                        P.dma(SP, VT_[t0:t0 + tw, :].rearrange("(s p) c -> p s c", p=128), vs_[j][:, 0:nsub, :],
                              reads=[vs_[j]], writes=[VT_])
                P.barrier()
                flush()
                print("ops after P", l, P.nops)
            if stop_after == ("P", l):
                break

            with ExitStack() as es:
                def sb(name, shape, dt):
                    return T(es.enter_context(sbuf_t(name, shape, dt))[:], name)

                def ps(name, shape, dt=F32):
                    return T(es.enter_context(psum_t(name, shape, dt))[:], name, psum=True)

                Sps = [ps("Sps%d" % i, [128, 2, 512]) for i in range(2)]
                Ops = [ps("Ops%d" % i, [128, 512]) for i in range(2)]
                BCp = ps("BCp", [128, 512])
                SSp = BCp
                JK = ps("JK", [128, 512])
                jrhs = sb("jrhs", [128, 512], BF16)
                P.op(POOL, memset(jrhs[:], 0.0), writes=[jrhs])
                kres = [sb("kres%d" % i, [128, TT], BF16) for i in range(4)]
                vres = sb("vres", [128, NKT, 260], BF16)
                qts = [[sb("qt%d_%d" % (i, j), [128, 512], BF16) for j in range(8)] for i in range(2)]
                for i in (2, 3):
                    P.op(POOL, memset(kres[i][:], 0.0), writes=[kres[i]])
                pts = [sb("pt%d" % i, [128, 2, 512], BF16) for i in range(3)]
                osb = [sb("osb%d" % i, [128, 512], F32) for i in range(2)]
                rec = [sb("rec%d" % i, [128, 512], F32) for i in range(2)]
                a1 = sb("a1", [128, 512], F32)
                od = sb("od", [128, 512], F32)
                osq = sb("osq", [128, 512], BF16)
                rs2 = sb("rs2", [128, 512], F32)
                outb = [sb("outb%d" % i, [128, 512], BF16) for i in range(3)]
                msk = sb("msk", [128, 6, 512], BF16)
                P.dma(SP, msk[:], mask_d.rearrange("r p q -> p r q"), writes=[msk])
                for i in range(2):
                    P.op(POOL, memset(osb[i][:], 0.0), writes=[osb[i]])
                so = l * 16
                cnt = {"pt": 0, "S": 0, "O": 0, "ob": 0, "q": 0}

                qblocks = [(LC + i * 512, 512, False) for i in range(8)]
                if need_ctx:
                    qblocks = [(0, LC, True)] + qblocks

                mixers = []
                mixers.append(dict(name="diff", ktiles=[Q_DK0, Q_DK1], krows=[128, 128], V=VD, vw=260,
                                   qtiles=[Q_DQ0, Q_DQ1], qrows=[128, 128],
                                   heads=[dict(qt=j // 4, kt=j // 4, r0=(j % 4) * 32, r1=(j % 4) * 32 + 32, vh=j // 2,
                                               scale=32 ** -0.5, feat=(j // 2) * 64, diff=j % 2) for j in range(8)]))
                mixers.append(dict(name="swa", ktiles=[Q_SK], krows=[128], V=VS, vw=130,
                                   qtiles=[Q_SQ0, Q_SQ1], qrows=[128, 128],
                                   heads=[dict(qt=h % 2, kt=0, r0=(h // 2) * 64, r1=(h // 2) * 64 + 64, vh=h // 2,
                                               scale=64 ** -0.5, feat=256 + h * 64, sink=h) for h in range(4)]))
                mixers.append(dict(name="mla", ktiles=[Q_MK + h for h in range(4)], krows=[96] * 4, V=VM, vw=260,
                                   qtiles=[Q_MQ + h for h in range(4)], qrows=[96] * 4,
                                   heads=[dict(qt=h, kt=h, r0=0, r1=96, vh=h, scale=96 ** -0.5, feat=512 + h * 64)
                                          for h in range(4)]))
                mixers.append(dict(name="gqa", ktiles=[Q_GK], krows=[128], V=VG, vw=130,
                                   qtiles=[Q_GQ0, Q_GQ1], qrows=[128, 128],
                                   heads=[dict(qt=h % 2, kt=0, r0=(h // 2) * 64, r1=(h // 2) * 64 + 64, vh=h // 2,
                                               scale=64 ** -0.5, feat=768 + h * 64) for h in range(4)]))

                for mx in mixers:
                    for hi_, hd_ in enumerate(mx["heads"]):
                        hd_["hi"] = hi_
                    for i in range(2):
                        for j in range(len(mx["heads"])):
                            P.op(POOL, memset(qts[i][j][:], 0.0), writes=[qts[i][j]])
                    for i, qi in enumerate(mx["ktiles"]):
                        r = mx["krows"][i]
                        P.dma(SP, kres[i][0:r, :], QK[qi][0:r, :], reads=[QK[qi]], writes=[kres[i]])
                    vw = mx["vw"]
                    P.dma(SP, vres[:, :, 0:vw], mx["V"][:, :].rearrange("(s p) c -> p s c", p=128),
                          reads=[mx["V"]], writes=[vres])
                    jobs = []
                    for (t0, qw, isctx) in qblocks:
                        qs = qts[cnt["q"] % 2]
                        cnt["q"] += 1
                        qload = []
                        for hd_ in mx["heads"]:
                            qload.append((qs[hd_["hi"]], hd_["r0"], hd_["r1"], mx["qtiles"][hd_["qt"]]))
                        for hd in mx["heads"]:
                            if isctx:
                                kts = [(0, None), (1, None)]
                            elif mx["name"] == "swa":
                                qb = (t0 - LC) // 128
                                kts = [(0, None), (1, None)]
                                for r in range(-1, 5):
                                    kt_ = qb + r
                                    if 0 <= kt_ < 32:
                                        kts.append((2 + kt_, r + 1))
                            else:
                                kts = [(i, None) for i in range(NKT)]
                            pairs = [kts[i:i + 2] for i in range(0, len(kts), 2)]
                            for pi, pr in enumerate(pairs):
                                jobs.append(dict(t0=t0, qw=qw, hd=hd, pr=pr, first=(pi == 0), last=(pi == len(pairs) - 1),
                                                 qs=qs, qload=qload if (hd is mx["heads"][0] and pi == 0) else None,
                                                 isctx=isctx))

                    def do_S(jb):
                        if jb["qload"] is not None:
                            for (qt_, ra, rb, qi) in jb["qload"]:
                                P.dma(SP, qt_[ra:rb, 0:jb["qw"]], QK[qi][ra:rb, jb["t0"]:jb["t0"] + jb["qw"]],
                                      reads=[QK[qi]], writes=[qt_])
                        hd = jb["hd"]
                        Sp = Sps[cnt["S"] % 2]
                        cnt["S"] += 1
                        jb["Sp"] = Sp
                        qt_ = jb["qs"][hd["hi"]]
                        kr = kres[hd["kt"]]
                        fns = []
                        for i, (kt_, _) in enumerate(jb["pr"]):
                            fns.append(mm(Sp[:, i, 0:jb["qw"]], kr[:, kt_ * 128:(kt_ + 1) * 128],
                                          qt_[:, 0:jb["qw"]]))
                        P.group(PE, fns, reads=[kr, qt_], writes=[Sp])

                    def do_exp(jb):
                        hd = jb["hd"]
                        qw = jb["qw"]
                        npair = len(jb["pr"])
                        pt = pts[cnt["pt"] % 3]
                        cnt["pt"] += 1
                        jb["pt"] = pt
                        Sp = jb["Sp"]
                        if qw == 512:
                            P.op(ACT, act(pt[:, 0:npair, :].rearrange("p a b -> p (a b)"),
                                          Sp[:, 0:npair, :].rearrange("p a b -> p (a b)"), AF.Exp, scale=hd["scale"]),
                                 reads=[Sp], writes=[pt])
                        else:
                            for i in range(npair):
                                P.op(ACT, act(pt[:, i, 0:qw], Sp[:, i, 0:qw], AF.Exp, scale=hd["scale"]),
                                     reads=[Sp], writes=[pt])
                        for i, (kt_, mi) in enumerate(jb["pr"]):
                            if mi is not None:
                                P.op(DVE, tt(pt[:, i, 0:qw], pt[:, i, 0:qw], msk[:, mi, 0:qw], ALU.mult),
                                     reads=[pt, msk], writes=[pt])

                    def do_pv(jb):
                        hd = jb["hd"]
                        qw = jb["qw"]
                        npair = len(jb["pr"])
                        pt = jb["pt"]
                        if jb["first"]:
                            cnt["O"] += 1
                        Op = Ops[cnt["O"] % 2]
                        vh = hd["vh"]
                        fns = []
                        for i, (kt_, _) in enumerate(jb["pr"]):
                            fns.append(mm(Op[0:65, 0:qw], vres[:, kt_, vh * 65:(vh + 1) * 65], pt[:, i, 0:qw],
                                          start=(jb["first"] and i == 0), stop=(jb["last"] and i == npair - 1)))
                        P.group(PE, fns, reads=[vres, pt], writes=[Op])
                        if jb["last"]:
                            finalize(jb, Op)

                    deferred = []

                    def tick():
                        ready = []
                        for it in deferred:
                            it[0] -= 1
                        while deferred and deferred[0][0] <= 0:
                            ready.append(deferred.pop(0)[1])
                        for fn_ in ready:
                            fn_()

                    def finalize(jb, Op):
                        hd = jb["hd"]
                        qw = jb["qw"]
                        t0 = jb["t0"]
                        par = cnt["O"] % 2
                        ob_ = osb[par]
                        rc = rec[par]
                        P.op(DVE, cp(ob_[0:65, 0:qw], Op[0:65, 0:qw]), reads=[Op], writes=[ob_])
                        f0 = hd["feat"]

                        def stage_b():
                            P.op(PE, mm(BCp[0:64, 0:qw], sel[:, 0:64], ob_[:, 0:qw]), reads=[sel, ob_], writes=[BCp])
                            if "sink" in hd:
                                c_ = so + 2 + hd["sink"]
                                P.op(DVE, ts1(rc[0:64, 0:qw], BCp[0:64, 0:qw], small[0:64, c_:c_ + 1], ALU.add),
                                     reads=[BCp, small], writes=[rc])
                                P.op(DVE, lambda e, o=rc[0:64, 0:qw]: e.reciprocal(o, o), reads=[rc], writes=[rc])
                            else:
                                P.op(DVE, lambda e, o=rc[0:64, 0:qw], i=BCp[0:64, 0:qw]: e.reciprocal(o, i),
                                     reads=[BCp], writes=[rc])
                            if "diff" not in hd:
                                ob2 = outb[cnt["ob"] % 3]
                                cnt["ob"] += 1
                                P.op(DVE, tt(ob2[0:64, 0:qw], ob_[0:64, 0:qw], rc[0:64, 0:qw], ALU.mult),
                                     reads=[ob_, rc], writes=[ob2])
                                P.dma(SP, MIX[f0:f0 + 64, t0:t0 + qw], ob2[0:64, 0:qw], reads=[ob2], writes=[MIX])
                            elif hd["diff"] == 0:
                                P.op(DVE, tt(a1[0:64, 0:qw], ob_[0:64, 0:qw], rc[0:64, 0:qw], ALU.mult),
                                     reads=[ob_, rc], writes=[a1])
                            else:
                                P.op(DVE, stt(od[0:64, 0:qw], ob_[0:64, 0:qw], small[0:64, so:so + 1], rc[0:64, 0:qw],
                                              ALU.mult, ALU.mult), reads=[ob_, small, rc], writes=[od])
                                P.op(DVE, tt(od[0:64, 0:qw], od[0:64, 0:qw], a1[0:64, 0:qw], ALU.add),
                                     reads=[od, a1], writes=[od])
                                P.op(POOL, tt(osq[0:64, 0:qw], od[0:64, 0:qw], od[0:64, 0:qw], ALU.mult),
                                     reads=[od], writes=[osq])
                                deferred.append([2, stage_c])

                        def stage_c():
                            P.op(PE, mm(SSp[0:64, 0:qw], ones_b[0:64, 0:64], osq[0:64, 0:qw]),
                                 reads=[ones_b, osq], writes=[SSp])
                            P.op(ACT, act(rs2[0:64, 0:qw], SSp[0:64, 0:qw], AF.Ln, bias=eps_t[0:64, 0:1], scale=1.0 / 64),
                                 reads=[SSp, eps_t], writes=[rs2])
                            P.op(ACT, act(rs2[0:64, 0:qw], rs2[0:64, 0:qw], AF.Exp, scale=-0.5), reads=[rs2], writes=[rs2])
                            ob2 = outb[cnt["ob"] % 3]
                            cnt["ob"] += 1
                            P.op(DVE, stt(ob2[0:64, 0:qw], od[0:64, 0:qw], small[0:64, so + 1:so + 2], rs2[0:64, 0:qw],
                                          ALU.mult, ALU.mult), reads=[od, small, rs2], writes=[ob2])
                            P.dma(SP, MIX[f0:f0 + 64, t0:t0 + qw], ob2[0:64, 0:qw], reads=[ob2], writes=[MIX])

                        deferred.append([2, stage_b])

                    if jobs:
                        do_S(jobs[0])
                        if len(jobs) > 1:
                            do_S(jobs[1])
                        for ji, jb in enumerate(jobs):
                            do_exp(jb)
                            if ji + 2 < len(jobs):
                                do_S(jobs[ji + 2])
                            tick()
                            do_pv(jb)
                        for _ in range(4):
                            tick()
                    flush()
                P.barrier()
                flush()
            if stop_after == ("A", l):
                break

            with ExitStack() as es:
                def sb(name, shape, dt):
                    return T(es.enter_context(sbuf_t(name, shape, dt))[:], name)

                def ps(name, shape, dt=F32):
                    return T(es.enter_context(psum_t(name, shape, dt))[:], name, psum=True)

                wo_h = es.enter_context(sbuf_t("wo", [128, 8, D], BF16))
                wo = [T(wo_h[:, kt, :], "wo%d" % kt) for kt in range(8)]
                for kt in range(8):
                    P.dma(POOL, wo[kt][:], wout_d[l, kt * 128:(kt + 1) * 128, :], writes=[wo[kt]])
                xts = [sb("xo%d" % i, [128, 8, 512], F32) for i in range(2)]
                mxs = [sb("mx%d" % i, [128, 8, 512], BF16) for i in range(2)]
                ysb_h = es.enter_context(sbuf_t("ysb", [128, 8, 512], F32))
                ysb = [T(ysb_h[:, dt, :], "ysb") for dt in range(8)]
                ysq_h = es.enter_context(sbuf_t("ysq", [128, 8, 512], BF16))
                ysq = [T(ysq_h[:, dt, :], "ysq") for dt in range(8)]
                rstd = sb("rstdo", [128, 512], F32)
                tmpf = [sb("tmpo%d" % i, [128, 512], F32) for i in range(3)]
                pp = [ps("po%d" % i, [128, 512]) for i in range(6)]
                blocks = [(LC + i * 512, 512, 0) for i in range(8)]
                if need_ctx:
                    blocks = [(0, LC, 1)] + blocks
                k = 0
                for bi, (t0, tw, who) in enumerate(blocks):
                    xt = xts[bi % 2]
                    mx_ = mxs[bi % 2]
                    P.dma(SP, xt[:, :, 0:tw], X0[:, t0:t0 + tw].rearrange("(k p) t -> p k t", p=128),
                          reads=[X0], writes=[xt])
                    P.dma(SP, mx_[:, :, 0:tw], MIX[:, t0:t0 + tw].rearrange("(k p) t -> p k t", p=128),
                          reads=[MIX], writes=[mx_])
                    for dt in range(8):
                        p_ = pp[k % 5]
                        k += 1
                        P.group(PE, [mm(p_[:, 0:tw], wo[mt][:, dt * 128:(dt + 1) * 128], mx_[:, mt, 0:tw],
                                        start=(mt == 0), stop=(mt == 7)) for mt in range(8)],
                                reads=wo + [mx_], writes=[p_])
                        P.op(ACT, lambda e, o=ysq[dt][:, 0:tw], i=p_[:, 0:tw]: e.square(o, i), reads=[p_], writes=[ysq[dt]])
                        P.op(DVE, cp(ysb[dt][:, 0:tw], p_[:, 0:tw]), reads=[p_], writes=[ysb[dt]])
                    pn = pp[5]
                    P.group(PE, [mm(pn[:, 0:tw], ones_b[:], ysq[dt][:, 0:tw], start=(dt == 0), stop=(dt == 7))
                                 for dt in range(8)], reads=ysq + [ones_b], writes=[pn])
                    P.op(ACT, act(rstd[:, 0:tw], pn[:, 0:tw], AF.Ln, bias=eps_t[:, 0:1], scale=1.0 / D),
                         reads=[pn, eps_t], writes=[rstd])
                    P.op(ACT, act(rstd[:, 0:tw], rstd[:, 0:tw], AF.Exp, scale=-0.5), reads=[rstd], writes=[rstd])
                    G2 = MV(l, who, 1)
                    for dt in range(8):
                        tm = tmpf[dt % 3]
                        P.op(DVE, stt(tm[:, 0:tw], ysb[dt][:, 0:tw], G2[:, dt:dt + 1], rstd[:, 0:tw], ALU.mult, ALU.mult),
                             reads=[ysb[dt], mv, rstd], writes=[tm])
                        P.op(DVE, tt(xt[:, dt, 0:tw], xt[:, dt, 0:tw], tm[:, 0:tw], ALU.add), reads=[tm, xt], writes=[xt])
                    P.dma(SP, X1[:, t0:t0 + tw].rearrange("(k p) t -> p k t", p=128), xt[:, :, 0:tw],
                          reads=[xt], writes=[X1])
                P.barrier()
                flush()
            if stop_after == ("O", l):
                break

            with ExitStack() as es:
                def sb(name, shape, dt):
                    return T(es.enter_context(sbuf_t(name, shape, dt))[:], name)

                def ps(name, shape, dt=F32):
                    return T(es.enter_context(psum_t(name, shape, dt))[:], name, psum=True)

                CW = 1024
                xt = sb("xf", [128, 8, CW + 2], F32)
                fsb_v = [T(xt[:, dt, 0:512], "fsb") for dt in range(8)]
                xre_v = T(xt[:, :, 512:1024], "xre")
                sq_h = es.enter_context(sbuf_t("sqf", [128, 8, CW + 2], BF16))
                sqf = [T(sq_h[:, kt, :], "sqf") for kt in range(8)]
                hT_h = es.enter_context(sbuf_t("hTf", [128, 8, CW + 2], BF16))
                hTf = [T(hT_h[:, kt, :], "hTf") for kt in range(8)]
                g_h = es.enter_context(sbuf_t("gf", [128, NCT, CW], BF16))
                gf = [T(g_h[:, ct, :], "gf") for ct in range(NCT)]
                rstd = sb("rstdf", [128, CW + 2], F32)
                tmpf = [sb("tmpff%d" % i, [128, 512], F32) for i in range(3)]
                U = [[sb("U%d_%d" % (i, j), [128, CW + 2], F32) for j in range(2)] for i in range(2)]
                Y = [[sb("Y%d_%d" % (i, j), [128, CW], F32) for j in range(2)] for i in range(2)]
                wup = [sb("wup%d" % i, [128, 8, 256], BF16) for i in range(3)]
                wdn = [sb("wdn%d" % i, [128, NCT, 128], BF16) for i in range(3)]
                pp = [ps("pf%d" % i, [128, 512]) for i in range(8)]
                ppi = [0]

                def nxt():
                    ppi[0] = (ppi[0] + 1) % 8
                    return pp[ppi[0]]

                chunks = [(LC + i * CW, CW, 0) for i in range(S // CW)]
                if need_ctx:
                    chunks = [(0, LC, 1)] + chunks
                prev_done = []
                nw = 0
                nd = 0
                for (t0, cw, who) in chunks:
                    lo = 0 if who == 1 else LC
                    hi = LC if who == 1 else TT
                    hasl = t0 - 1 >= lo
                    hasr = t0 + cw < hi
                    a0 = t0 - 1 if hasl else t0
                    a1_ = t0 + cw + 1 if hasr else t0 + cw
                    c0 = 0 if hasl else 1
                    c1 = c0 + (a1_ - a0)
                    P.dma(SP, xt[:, :, c0:c1], X1[:, a0:a1_].rearrange("(k p) t -> p k t", p=128),
                          reads=[X1], writes=[xt], extra=prev_done)
                    pieces = [(s_, min(s_ + 512, c1)) for s_ in range(c0, c1, 512)]
                    xre = T(xt[:, :, 512:512 + min(512, cw)], "xre")
                    for kt in range(8):
                        if kt % 2 == 0:
                            P.op(ACT, lambda e, o=sqf[kt][:, c0:c1], i=xt[:, kt, c0:c1]: e.square(o, i),
                                 reads=[xt], writes=[sqf[kt]])
                        else:
                            P.op(POOL, tt(sqf[kt][:, c0:c1], xt[:, kt, c0:c1], xt[:, kt, c0:c1], ALU.mult),
                                 reads=[xt], writes=[sqf[kt]])
                    for (s_, e_) in pieces:
                        pn = nxt()
                        P.group(PE, [mm(pn[:, 0:e_ - s_], ones_b[:], sqf[kt][:, s_:e_], start=(kt == 0), stop=(kt == 7))
                                     for kt in range(8)], reads=sqf + [ones_b], writes=[pn])
                        P.op(ACT, act(rstd[:, s_:e_], pn[:, 0:e_ - s_], AF.Ln, bias=eps_t[:, 0:1], scale=1.0 / D),
                             reads=[pn, eps_t], writes=[rstd])
                    P.op(ACT, act(rstd[:, c0:c1], rstd[:, c0:c1], AF.Exp, scale=-0.5), reads=[rstd], writes=[rstd])
                    A4 = MV(l, who, 2)
                    B4 = MOD(l, 3, who)
                    for kt in range(8):
                        for (s_, e_) in pieces:
                            tm = tmpf[(kt + s_ // 512) % 3]
                            P.op(DVE, stt(tm[:, 0:e_ - s_], xt[:, kt, s_:e_], A4[:, kt:kt + 1], rstd[:, s_:e_],
                                          ALU.mult, ALU.mult), reads=[xt, mv, rstd], writes=[tm])
                            P.op(ACT, act(hTf[kt][:, s_:e_], tm[:, 0:e_ - s_], AF.Identity, bias=B4[:, kt:kt + 1]),
                                 reads=[tm, modsb], writes=[hTf[kt]])
                        if not hasl:
                            P.op(POOL, memset(hTf[kt][:, 0:1], 0.0), writes=[hTf[kt]])
                        if not hasr:
                            P.op(POOL, memset(hTf[kt][:, cw + 1:cw + 2], 0.0), writes=[hTf[kt]])
                    mpieces = [(s_, min(s_ + 512, cw + 2)) for s_ in range(0, cw + 2, 512)]
                    pend_gate = None
                    for ct in range(NCT):
                        w_ = wup[nw % 3]
                        nw += 1
                        P.dma(SP if ct % 2 == 0 else ACT, w_[:], WUPB[l][ct].rearrange("p (k c) -> p k c", c=256),
                              reads=[WUPB[l]], writes=[w_])
                        ys = []
                        for part in range(2):
                            u = U[ct % 2][part]
                            for (s_, e_) in mpieces:
                                p_ = nxt()
                                P.group(PE, [mm(p_[:, 0:e_ - s_], w_[:, kt, part * 128:(part + 1) * 128], hTf[kt][:, s_:e_],
                                                start=(kt == 0), stop=(kt == 7)) for kt in range(8)],
                                        reads=[w_] + hTf, writes=[p_])
                                P.op(ACT, acp(u[:, s_:e_], p_[:, 0:e_ - s_]), reads=[p_], writes=[u])
                            y = Y[ct % 2][part]
                            ti = part * NCT + ct
                            cwv = V("cw", l)
                            cbv = V("cb", l)
                            eng = DVE
                            P.op(ACT, act(y[:, 0:cw], u[:, 0:cw], AF.Identity, bias=cbv[:, ti:ti + 1], scale=cwv[:, ti:ti + 1]),
                                 reads=[u, vecs], writes=[y])
                            P.op(eng, stt(y[:, 0:cw], u[:, 1:cw + 1], cwv[:, 44 + ti:44 + ti + 1], y[:, 0:cw], ALU.mult, ALU.add),
                                 reads=[u, vecs, y], writes=[y])
                            P.op(eng, stt(y[:, 0:cw], u[:, 2:cw + 2], cwv[:, 88 + ti:88 + ti + 1], y[:, 0:cw], ALU.mult, ALU.add),
                                 reads=[u, vecs, y], writes=[y])
                            ys.append(y)
                        if pend_gate is not None:
                            pend_gate()

                        def gate(ct=ct, ys=ys, cw=cw):
                            ya, yv = ys
                            sa = U[ct % 2][0]
                            P.op(ACT, act(sa[:, 0:cw], ya[:, 0:cw], AF.Silu), reads=[ya], writes=[sa])
                            P.op(DVE, tt(gf[ct][:, 0:cw], sa[:, 0:cw], yv[:, 0:cw], ALU.mult), reads=[sa, yv], writes=[gf[ct]])
                        pend_gate = gate
                    pend_gate()
                    G5 = MV(l, who, 3)
                    done = []
                    for s_ in range(0, cw, 512):
                        hw_ = min(512, cw - s_)
                        tx = P.dma(SP, xre[:], X1[:, t0 + s_:t0 + s_ + hw_].rearrange("(k p) t -> p k t", p=128),
                                   reads=[X1] + hTf, writes=[xre], extra=[xt.w] + list(xt.r))
                        for dt in range(8):
                            wd = wdn[nd % 3]
                            nd += 1
                            P.dma(ACT if dt % 2 == 0 else SP, wd[:], WDNB[l][dt].rearrange("p (c k) -> p c k", k=128),
                                  reads=[WDNB[l]], writes=[wd])
                            p_ = nxt()
                            P.group(PE, [mm(p_[:, 0:hw_], wd[:, ct, :], gf[ct][:, s_:s_ + hw_], start=(ct == 0), stop=(ct == NCT - 1))
                                         for ct in range(NCT)], reads=[wd] + gf, writes=[p_])
                            P.op(ACT, lambda e, o=sqf[dt][:, 0:hw_], i=p_[:, 0:hw_]: e.square(o, i), reads=[p_], writes=[sqf[dt]])
                            P.op(DVE, cp(fsb_v[dt][:, 0:hw_], p_[:, 0:hw_]), reads=[p_], writes=[fsb_v[dt]],
                                 extra=[xt.w] + list(xt.r))
                        pn = nxt()
                        P.group(PE, [mm(pn[:, 0:hw_], ones_b[:], sqf[dt][:, 0:hw_], start=(dt == 0), stop=(dt == 7))
                                     for dt in range(8)], reads=sqf + [ones_b], writes=[pn])
                        P.op(ACT, act(rstd[:, 0:hw_], pn[:, 0:hw_], AF.Ln, bias=eps_t[:, 0:1], scale=1.0 / D),
                             reads=[pn, eps_t], writes=[rstd])
                        P.op(ACT, act(rstd[:, 0:hw_], rstd[:, 0:hw_], AF.Exp, scale=-0.5), reads=[rstd], writes=[rstd])
                        for dt in range(8):
                            tm = tmpf[dt % 3]
                            P.op(DVE, stt(tm[:, 0:hw_], fsb_v[dt][:, 0:hw_], G5[:, dt:dt + 1], rstd[:, 0:hw_], ALU.mult, ALU.mult),
                                 reads=[fsb_v[dt], mv, rstd], writes=[tm])
                            P.op(DVE, tt(xre[:, dt, :], xre[:, dt, :], tm[:, 0:hw_], ALU.add), reads=[tm, xre], writes=[xre])
                        if X2 is YT:
                            dst = X2[:, t0 - LC + s_:t0 - LC + s_ + hw_]
                        else:
                            dst = X2[:, t0 + s_:t0 + s_ + hw_]
                        tk = P.dma(SP, dst.rearrange("(k p) t -> p k t", p=128), xre[:], reads=[xre], writes=[X2])
                        done.append(tk)
                        if X2 is YT:
                            out_toks.append(tk)
                        done += [f.w for f in fsb_v if f.w is not None]
                        for f in fsb_v:
                            done += list(f.r)
                    prev_done = [d for d in done if d is not None]
                P.barrier()
                flush()
            if stop_after == ("F", l):
                break

        P.maxops = None
        P.barrier()
        for (name, ap, o, shape) in dbg_list:
            rows = shape[0]
            for r0 in range(0, rows, 128):
                r1 = min(rows, r0 + 128)
                P.dma(POOL, o[r0:r1], ap[r0:r1], max_dma_last_dim=2048)
        P.barrier()
        flush()
    return nc


def make_inputs(inp):
    f32 = np.float32
    g = lambda n: np.asarray(inp[n], f32)
    x, c, ctx, c_ctx = g("x"), g("c"), g("ctx"), g("c_ctx")
    B = x.shape[0]
    w_in = g("w_in")
    w_inx = np.ascontiguousarray(w_in[:, :, EXT_COLS])
    w_uq = g("mla_w_uq")
    cols = []
    for h in range(4):
        a = list(range(h * 96, (h + 1) * 96))
        b = a[:64] + rope_perm_cols(a[64:], 32)
        cols += a + b
    w_uqx = np.ascontiguousarray(w_uq[:, :, cols])
    w_ukv = g("mla_w_ukv")
    kc, vc = [], []
    for h in range(4):
        kc += list(range(h * 128, h * 128 + 64))
        vc += list(range(h * 128 + 64, h * 128 + 128))
    w_ukvx = np.ascontiguousarray(w_ukv[:, :, kc + vc])

    rep = lambda v: np.broadcast_to(v[None, :], (128, v.shape[0]))
    p64 = np.arange(128) % 64
    common = np.zeros((128, NV), f32)
    for l in range(DEPTH):
        def put(name, arr):
            o = VOFF[(name, l)]
            arr = np.asarray(arr, f32)
            if arr.ndim == 1:
                arr = arr[:, None]
            common[:, o:o + arr.shape[1]] = arr
        put("gpre", fm(g("g_mix_pre")[l]))
        put("gpost", fm(g("g_mix_post")[l]))
        put("gfpre", fm(g("g_ffn_pre")[l]))
        put("gfpost", fm(g("g_ffn_post")[l]))
        put("bmod", fm(g("b_mod")[l]))
        put("lq1", rep(g("diff_lam_q1")[l]))
        put("lk1", rep(g("diff_lam_k1")[l]))
        put("lq2", rep(g("diff_lam_q2")[l]))
        put("lk2", rep(g("diff_lam_k2")[l]))
        put("gsub", g("diff_g_sub")[l][p64])
        put("sink", rep(g("swa_sink")[l]))
        mgq = np.zeros((128, 2), f32)
        mgq[:, 0] = g("mla_g_q")[l][:128]
        mgq[:64, 1] = g("mla_g_q")[l][128:192]
        put("mgq", mgq)
        put("mgkv", g("mla_g_kv")[l])
        put("ggq", g("gqa_g_q")[l][p64])
        put("ggqp", g("gqa_g_q")[l][p64 ^ 16])
        put("ggk", g("gqa_g_k")[l][p64])
        put("ggkp", g("gqa_g_k")[l][p64 ^ 16])
        cw = g("ffn_conv_w")[l]
        put("cw", np.concatenate([fm(cw[j]) for j in range(3)], axis=1))
        put("cb", fm(g("ffn_conv_b")[l]))
    rope = rope_tables()
    masks = swa_masks()
    shared = {"rope": rope, "masks": masks, "w_mod": g("w_mod"), "w_inx": w_inx, "w_uqx": w_uqx,
              "w_ukvx": w_ukvx, "w_out": g("w_out"), "w_up": g("ffn_w_up"), "w_dn": g("ffn_w_down")}
    maps = []
    for b in range(B):
        v = common.copy()
        cin = np.zeros((128, 8, 2), f32)
        cin[:, :, 0] = fm(c[b])
        cin[:, :, 1] = fm(c_ctx)
        v[:, VOFF["cin"]:VOFF["cin"] + 16] = cin.reshape(128, 16)
        xin = np.ascontiguousarray(np.concatenate([ctx[b].T, x[b].T], axis=1))
        m = dict(shared)
        m["xin"] = xin
        m["vecs"] = v
        maps.append(m)
    return maps


_NC_CACHE = {}


def kernel(**inputs):
    maps = make_inputs(inputs)
    if "nc" not in _NC_CACHE:
        _NC_CACHE["nc"] = build()
    nc = _NC_CACHE["nc"]
    res = run_bass_kernel_spmd(nc, maps, core_ids=list(range(len(maps))))
    out = np.stack([np.ascontiguousarray(r["yT"].T) for r in res.results], axis=0)
    return out.astype(np.float32)
```
